# Optimizing a Trainium2 kernel written in Bass

```python
import math
import jax, jax.numpy as jnp
from jax import lax
import numpy as np

D_MODEL = 2048
BATCH = 4
SEQ = 2048
DEPTH = 2
DEC_BATCH = 128
DEC_SEQ = 4
PAST_LEN = 16384
PAGE_SIZE = 128

F32 = jnp.float32
N_EVEN = (DEPTH + 1) // 2
N_ODD = DEPTH // 2
N_MEM = 256
EPS = 1e-6

RW_HEADS = 16
RW_HD = 64
RW_DIM = RW_HEADS * RW_HD
RW_DECAY_LORA = 64
RW_A_LORA = 64
RW_GATE_LORA = 128
RW_PROJ = 3 * RW_DIM + RW_DECAY_LORA + RW_A_LORA + RW_GATE_LORA
RW_SPLITS = [RW_DIM, 2 * RW_DIM, 3 * RW_DIM, 3 * RW_DIM + RW_DECAY_LORA, 3 * RW_DIM + RW_DECAY_LORA + RW_A_LORA]
RW_GN_EPS = 6.4e-4

SSD_HEADS = 16
SSD_HD = 64
SSD_DIM = SSD_HEADS * SSD_HD
SSD_GROUPS = 2
SSD_HPG = SSD_HEADS // SSD_GROUPS
SSD_STATE = 128
SSD_CONV = 4
SSD_CONV_DIM = SSD_DIM + 2 * SSD_GROUPS * SSD_STATE
SSD_PROJ = SSD_DIM + SSD_CONV_DIM + SSD_HEADS
SSD_CHUNK = 64

AB_PROJ = RW_PROJ + SSD_PROJ
AB_OUT = RW_DIM + SSD_DIM

GDN_HEADS = 16
GDN_DK = 128
GDN_DV = 128
GDN_QK = GDN_HEADS * GDN_DK
GDN_V = GDN_HEADS * GDN_DV
GDN_CONV = 4
GDN_CONV_DIM = 2 * GDN_QK + GDN_V
GDN_PROJ = GDN_CONV_DIM + GDN_V + 2 * GDN_HEADS
GDN_CHUNK = 64

D_FF = 5632
FFN_CONV = 3

XA_HEADS = 4
XA_HD = 128
XA_DIM = XA_HEADS * XA_HD

kernel_name = 'hybrid_rwkv7_ssd_gdn_memxattn_convffn_step'


def rmsnorm(x, g, eps=EPS):
    xf = x.astype(F32)
    y = xf * lax.rsqrt(jnp.mean(xf * xf, axis=-1, keepdims=True) + eps)
    return (y * g.astype(F32)).astype(x.dtype)


def l2norm(x, eps=1e-6):
    xf = x.astype(F32)
    return xf * lax.rsqrt(jnp.sum(xf * xf, axis=-1, keepdims=True) + eps)


def causal_dwconv(x, buf, w, b):
    width, L = w.shape[0], x.shape[1]
    xp = jnp.concatenate([buf.astype(x.dtype), x], axis=1)
    y = xp[:, 0:L] * w[0]
    for i in range(1, width):
        y = y + xp[:, i:i + L] * w[i]
    if b is not None:
        y = y + b
    return y, xp[:, L:]


def rwkv7_mix(p, shift_buf, S0, mu, w0, w_up, a0, a_up, g_up, k_k, k_a, r_k, gn_w, gn_b):
    Bsz, L, _ = p.shape
    prev = jnp.concatenate([shift_buf[:, None, :].astype(p.dtype), p[:, :-1]], axis=1)
    ps = p + (prev - p) * mu
    r, k, v, wd, ad, gd = jnp.split(ps, RW_SPLITS, axis=-1)
    w = -jax.nn.softplus(-(w0 + jnp.tanh(wd) @ w_up).astype(F32)) - 0.5
    decay = jnp.exp(-jnp.exp(w))
    a = jax.nn.sigmoid((a0 + ad @ a_up).astype(F32))
    g = jax.nn.sigmoid(gd) @ g_up
    shp = (Bsz, L, RW_HEADS, RW_HD)
    r, k, v, decay, a = (t.astype(F32).reshape(shp) for t in (r, k, v, decay, a))
    kk = l2norm(k * k_k.astype(F32).reshape(RW_HEADS, RW_HD))
    k = k * (1.0 + (a - 1.0) * k_a.astype(F32).reshape(RW_HEADS, RW_HD))

    def step(S, inp):
        r_t, w_t, k_t, v_t, a_t, b_t = inp
        sa = jnp.einsum('bhvk,bhk->bhv', S, a_t)
        S = S * w_t[:, :, None, :] + sa[..., None] * b_t[:, :, None, :] + v_t[..., None] * k_t[:, :, None, :]
        return S, jnp.einsum('bhvk,bhk->bhv', S, r_t)

    seq = tuple(jnp.swapaxes(t, 0, 1) for t in (r, decay, k, v, -kk, kk * a))
    S_T, y = lax.scan(step, S0.astype(F32), seq)
    y = jnp.swapaxes(y, 0, 1)
    mean = jnp.mean(y, axis=-1, keepdims=True)
    var = jnp.mean(jnp.square(y - mean), axis=-1, keepdims=True)
    y = (y - mean) * lax.rsqrt(var + RW_GN_EPS) * gn_w.astype(F32) + gn_b.astype(F32)
    y = y + jnp.sum(r * k * r_k.astype(F32), axis=-1, keepdims=True) * v
    out = y.reshape(Bsz, L, RW_DIM).astype(p.dtype) * g
    return out, p[:, -1], S_T


def ssd_chunked(X, dA, Bm, Cm, h0):
    Bsz, L = X.shape[:2]
    Q = min(SSD_CHUNK, L)
    nc = L // Q
    X = X.reshape(Bsz, nc, Q, SSD_GROUPS, SSD_HPG, SSD_HD)
    dA = dA.reshape(Bsz, nc, Q, SSD_GROUPS, SSD_HPG)
    Bm = Bm.reshape(Bsz, nc, Q, SSD_GROUPS, SSD_STATE)
    Cm = Cm.reshape(Bsz, nc, Q, SSD_GROUPS, SSD_STATE)
    Acs = jnp.cumsum(dA, axis=2)
    incl = jnp.tril(jnp.ones((Q, Q), bool))
    Lmat = jnp.exp(jnp.where(incl[:, :, None, None], Acs[:, :, :, None] - Acs[:, :, None, :], -jnp.inf))
    CB = jnp.einsum('bclgn,bcsgn->bclsg', Cm, Bm)
    Y_diag = jnp.einsum('bclsgr,bcsgrp->bclgrp', CB[..., None] * Lmat, X)
    decay_states = jnp.exp(Acs[:, :, -1:] - Acs)
    states = jnp.einsum('bclgn,bclgrp->bcgrpn', Bm, X * decay_states[..., None])
    chunk_decay = jnp.exp(Acs[:, :, -1])

    def step(h, inp):
        st, cd = inp
        return h * cd[..., None, None] + st, h

    hT, h_in = lax.scan(step, h0, (jnp.moveaxis(states, 1, 0), jnp.moveaxis(chunk_decay, 1, 0)))
    h_in = jnp.moveaxis(h_in, 0, 1)
    Y_off = jnp.einsum('bclgn,bcgrpn->bclgrp', Cm, h_in) * jnp.exp(Acs)[..., None]
    return (Y_diag + Y_off).reshape(Bsz, L, SSD_GROUPS, SSD_HPG, SSD_HD), hT


def ssd_mix(p, conv_buf, h0, conv_w, conv_b, dt_bias, A_log, D_skip, norm_w):
    Bsz, L, _ = p.shape
    z, xBC, dt = jnp.split(p, [SSD_DIM, SSD_DIM + SSD_CONV_DIM], axis=-1)
    xBC, new_buf = causal_dwconv(xBC, conv_buf, conv_w, conv_b)
    xBC = jax.nn.silu(xBC).astype(F32)
    xs, Bm, Cm = jnp.split(xBC, [SSD_DIM, SSD_DIM + SSD_GROUPS * SSD_STATE], axis=-1)
    xs = xs.reshape(Bsz, L, SSD_GROUPS, SSD_HPG, SSD_HD)
    Bm = Bm.reshape(Bsz, L, SSD_GROUPS, SSD_STATE)
    Cm = Cm.reshape(Bsz, L, SSD_GROUPS, SSD_STATE)
    dt = jax.nn.softplus(dt.astype(F32) + dt_bias.astype(F32)).reshape(Bsz, L, SSD_GROUPS, SSD_HPG)
    dA = dt * (-jnp.exp(A_log.astype(F32))).reshape(SSD_GROUPS, SSD_HPG)
    y, hT = ssd_chunked(xs * dt[..., None], dA, Bm, Cm, h0.astype(F32))
    y = y + xs * D_skip.astype(F32).reshape(SSD_GROUPS, SSD_HPG, 1)
    yg = (y * jax.nn.silu(z.astype(F32)).reshape(Bsz, L, SSD_GROUPS, SSD_HPG, SSD_HD)).reshape(Bsz, L, SSD_GROUPS, SSD_HPG * SSD_HD)
    yg = yg * lax.rsqrt(jnp.mean(yg * yg, axis=-1, keepdims=True) + EPS)
    y = yg.reshape(Bsz, L, SSD_DIM) * norm_w.astype(F32)
    return y.astype(p.dtype), new_buf, hT


def gdn_chunked(q, k, v, g, beta, S0):
    Bsz, L = q.shape[:2]
    Q = min(GDN_CHUNK, L)
    nc = L // Q

    def blk(t):
        return jnp.moveaxis(t.reshape((Bsz, nc, Q, GDN_HEADS) + t.shape[3:]), 3, 2)

    q, k, v, g, beta = blk(q), blk(k), blk(v), blk(g), blk(beta)
    gc = jnp.cumsum(g, axis=-1)
    incl = jnp.tril(jnp.ones((Q, Q), bool))
    strict = incl & ~jnp.eye(Q, dtype=bool)
    dec = jnp.exp(jnp.where(incl, gc[..., :, None] - gc[..., None, :], -jnp.inf))
    kb = k * beta[..., None]
    A = jnp.where(strict, jnp.einsum('bchld,bchsd->bchls', kb, k) * dec, 0.0) + jnp.eye(Q, dtype=F32)
    rhs = jnp.concatenate([v * beta[..., None], kb * jnp.exp(gc)[..., None]], axis=-1)
    sol = lax.linalg.triangular_solve(A, rhs, left_side=True, lower=True, unit_diagonal=True)
    vw, kcd = jnp.split(sol, [GDN_DV], axis=-1)
    attn = jnp.einsum('bchld,bchsd->bchls', q, k) * dec
    qg = q * jnp.exp(gc)[..., None]
    kg = k * jnp.exp(gc[..., -1:] - gc)[..., None]
    glast = jnp.exp(gc[..., -1])

    def step(S, inp):
        qg_c, kg_c, vw_c, kcd_c, attn_c, gl_c = inp
        v_new = vw_c - jnp.einsum('bhld,bhdv->bhlv', kcd_c, S)
        o = jnp.einsum('bhld,bhdv->bhlv', qg_c, S) + jnp.einsum('bhls,bhsv->bhlv', attn_c, v_new)
        S = S * gl_c[..., None, None] + jnp.einsum('bhld,bhlv->bhdv', kg_c, v_new)
        return S, o

    xs = tuple(jnp.moveaxis(t, 1, 0) for t in (qg, kg, vw, kcd, attn, glast))
    S_T, o = lax.scan(step, S0, xs)
    o = jnp.transpose(o, (1, 0, 3, 2, 4)).reshape(Bsz, L, GDN_HEADS, GDN_DV)
    return o, S_T


def gdn_mix(p, conv_buf, S0, conv_w, A_log, dt_bias, norm_w):
    Bsz, L, _ = p.shape
    qkv, z, b, a = jnp.split(p, [GDN_CONV_DIM, GDN_CONV_DIM + GDN_V, GDN_CONV_DIM + GDN_V + GDN_HEADS], axis=-1)
    qkv, new_buf = causal_dwconv(qkv, conv_buf, conv_w, None)
    qkv = jax.nn.silu(qkv).astype(F32)
    q, k, v = jnp.split(qkv, [GDN_QK, 2 * GDN_QK], axis=-1)
    q = l2norm(q.reshape(Bsz, L, GDN_HEADS, GDN_DK)) * (GDN_DK ** -0.5)
    k = l2norm(k.reshape(Bsz, L, GDN_HEADS, GDN_DK))
    v = v.reshape(Bsz, L, GDN_HEADS, GDN_DV)
    beta = jax.nn.sigmoid(b.astype(F32))
    g = -jnp.exp(A_log.astype(F32)) * jax.nn.softplus(a.astype(F32) + dt_bias.astype(F32))
    o, S_T = gdn_chunked(q, k, v, g, beta, S0.astype(F32))
    o = o * lax.rsqrt(jnp.mean(o * o, axis=-1, keepdims=True) + EPS) * norm_w.astype(F32)
    o = o * jax.nn.silu(z.astype(F32)).reshape(Bsz, L, GDN_HEADS, GDN_DV)
    return o.reshape(Bsz, L, GDN_V).astype(p.dtype), new_buf, S_T


def memory_kv(mem, g, w_k, w_v):
    m = rmsnorm(mem, g)
    Bsz = mem.shape[0]
    return (m @ w_k).reshape(Bsz, N_MEM, XA_HEADS, XA_HD), (m @ w_v).reshape(Bsz, N_MEM, XA_HEADS, XA_HD)


def cross_attn(h, mk, mv, w_q, w_o):
    Bsz, L, _ = h.shape
    q = (h @ w_q).reshape(Bsz, L, XA_HEADS, XA_HD).astype(F32)
    s = jnp.einsum('blhd,bmhd->bhlm', q, mk.astype(F32)) * (XA_HD ** -0.5)
    pr = jax.nn.softmax(s, axis=-1)
    o = jnp.einsum('bhlm,bmhd->blhd', pr, mv.astype(F32))
    return o.reshape(Bsz, L, XA_DIM).astype(h.dtype) @ w_o


def conv_ffn(h, buf, w_in, conv_w, conv_b, w_out):
    gate, up = jnp.split(h @ w_in, [D_FF], axis=-1)
    gate, new_buf = causal_dwconv(gate, buf, conv_w, conv_b)
    return (jax.nn.silu(gate) * up) @ w_out, new_buf


def run_group(x, mem_k, mem_v, st_rw, st_sh, st_ssd, st_ssdc, st_gdn, st_gdnc, st_ffn, P):
    h = x
    n_rw, n_sh, n_ssd, n_ssdc, n_gdn, n_gdnc, n_ffn = [], [], [], [], [], [], []
    for l in range(DEPTH):
        i = l // 2
        hn = rmsnorm(h, P['norm_mix'][l])
        if l % 2 == 0:
            p_rw, p_ssd = jnp.split(hn @ P['w_in_ab'][i], [RW_PROJ], axis=-1)
            y_rw, sh, s_rw = rwkv7_mix(p_rw, st_sh[i], st_rw[i], P['rw_mu'][i], P['rw_w0'][i], P['rw_w_up'][i],
                                       P['rw_a0'][i], P['rw_a_up'][i], P['rw_g_up'][i], P['rw_k_k'][i],
                                       P['rw_k_a'][i], P['rw_r_k'][i], P['rw_gn_w'][i], P['rw_gn_b'][i])
            y_ssd, cb, s_ssd = ssd_mix(p_ssd, st_ssdc[i], st_ssd[i], P['ssd_conv_w'][i], P['ssd_conv_b'][i],
                                       P['ssd_dt_bias'][i], P['ssd_A_log'][i], P['ssd_D'][i], P['ssd_norm_w'][i])
            mix = jnp.concatenate([y_rw, y_ssd], axis=-1) @ P['w_out_ab'][i]
            n_rw.append(s_rw)
            n_sh.append(sh)
            n_ssd.append(s_ssd)
            n_ssdc.append(cb)
        else:
            y_c, cb, s_gdn = gdn_mix(hn @ P['w_in_c'][i], st_gdnc[i], st_gdn[i], P['gdn_conv_w'][i],
                                     P['gdn_A_log'][i], P['gdn_dt_bias'][i], P['gdn_norm_w'][i])
            mix = y_c @ P['w_out_c'][i]
            n_gdn.append(s_gdn)
            n_gdnc.append(cb)
        h = h + mix
        h = h + cross_attn(rmsnorm(h, P['norm_xa'][l]), mem_k[l], mem_v[l], P['w_xq'][l], P['w_xo'][l])
        f, fb = conv_ffn(rmsnorm(h, P['norm_ffn'][l]), st_ffn[l], P['ffn_w_in'][l], P['ffn_conv_w'][l],
                         P['ffn_conv_b'][l], P['ffn_w_out'][l])
        n_ffn.append(fb)
        h = h + f
    y = rmsnorm(h, P['norm_final'])
    return (y, jnp.stack(n_rw), jnp.stack(n_sh), jnp.stack(n_ssd), jnp.stack(n_ssdc),
            jnp.stack(n_gdn), jnp.stack(n_gdnc), jnp.stack(n_ffn))


def setup_inputs(seed: int = 0) -> dict:
    key = jax.random.key(seed)
    ks = iter(jax.random.split(key, 80))

    def nrm(shape, scale):
        return jax.random.normal(next(ks), shape, F32) * scale

    def unif(shape, lo, hi):
        return jax.random.uniform(next(ks), shape, F32, lo, hi)

    def gain(shape):
        return 1.0 + nrm(shape, 0.02)

    def dt_bias(shape):
        dt = jnp.exp(unif(shape, math.log(1e-3), math.log(1e-1)))
        return dt + jnp.log(-jnp.expm1(-dt))

    E, O = N_EVEN, N_ODD
    return {
        'x_prompt': nrm((BATCH, SEQ, D_MODEL), 1.0),
        'x_sample': nrm((DEC_BATCH, DEC_SEQ, D_MODEL), 1.0),
        'mem_prompt': nrm((BATCH, N_MEM, D_MODEL), 1.0),
        'state_rwkv': nrm((E, DEC_BATCH, RW_HEADS, RW_HD, RW_HD), 0.1),
        'state_rwkv_shift': nrm((E, DEC_BATCH, RW_PROJ), 1.0),
        'state_ssd': nrm((E, DEC_BATCH, SSD_GROUPS, SSD_HPG, SSD_HD, SSD_STATE), 0.1),
        'state_ssd_conv': nrm((E, DEC_BATCH, SSD_CONV - 1, SSD_CONV_DIM), 1.0),
        'state_gdn': nrm((O, DEC_BATCH, GDN_HEADS, GDN_DK, GDN_DV), 0.1),
        'state_gdn_conv': nrm((O, DEC_BATCH, GDN_CONV - 1, GDN_CONV_DIM), 1.0),
        'state_ffn_conv': nrm((DEPTH, DEC_BATCH, FFN_CONV - 1, D_FF), 1.0),
        'cache_mem_k': nrm((DEPTH, DEC_BATCH, N_MEM, XA_HEADS, XA_HD), 1.0),
        'cache_mem_v': nrm((DEPTH, DEC_BATCH, N_MEM, XA_HEADS, XA_HD), 1.0),
        'norm_mix': gain((DEPTH, D_MODEL)),
        'norm_xa': gain((DEPTH, D_MODEL)),
        'norm_mem': gain((DEPTH, D_MODEL)),
        'norm_ffn': gain((DEPTH, D_MODEL)),
        'norm_final': gain((D_MODEL,)),
        'w_in_ab': nrm((E, D_MODEL, AB_PROJ), D_MODEL ** -0.5),
        'rw_mu': unif((E, RW_PROJ), 0.0, 1.0),
        'rw_w0': unif((E, RW_DIM), -6.0, -0.5),
        'rw_w_up': nrm((E, RW_DECAY_LORA, RW_DIM), 0.1),
        'rw_a0': nrm((E, RW_DIM), 0.1),
        'rw_a_up': nrm((E, RW_A_LORA, RW_DIM), 0.1),
        'rw_g_up': nrm((E, RW_GATE_LORA, RW_DIM), RW_GATE_LORA ** -0.5),
        'rw_k_k': 0.85 + nrm((E, RW_DIM), 0.02),
        'rw_k_a': 1.0 + nrm((E, RW_DIM), 0.02),
        'rw_r_k': nrm((E, RW_HEADS, RW_HD), 0.1),
        'rw_gn_w': gain((E, RW_HEADS, RW_HD)),
        'rw_gn_b': nrm((E, RW_HEADS, RW_HD), 0.02),
        'ssd_conv_w': nrm((E, SSD_CONV, SSD_CONV_DIM), SSD_CONV ** -0.5),
        'ssd_conv_b': nrm((E, SSD_CONV_DIM), 0.02),
        'ssd_dt_bias': dt_bias((E, SSD_HEADS)),
        'ssd_A_log': jnp.log(unif((E, SSD_HEADS), 1.0, 16.0)),
        'ssd_D': gain((E, SSD_HEADS)),
        'ssd_norm_w': gain((E, SSD_DIM)),
        'w_out_ab': nrm((E, AB_OUT, D_MODEL), 0.5 * AB_OUT ** -0.5),
        'w_in_c': nrm((O, D_MODEL, GDN_PROJ), D_MODEL ** -0.5),
        'gdn_conv_w': nrm((O, GDN_CONV, GDN_CONV_DIM), GDN_CONV ** -0.5),
        'gdn_A_log': jnp.log(unif((O, GDN_HEADS), 1.0, 16.0)),
        'gdn_dt_bias': dt_bias((O, GDN_HEADS)),
        'gdn_norm_w': gain((O, GDN_DV)),
        'w_out_c': nrm((O, GDN_V, D_MODEL), 0.5 * GDN_V ** -0.5),
        'w_xq': nrm((DEPTH, D_MODEL, XA_DIM), D_MODEL ** -0.5),
        'w_xk': nrm((DEPTH, D_MODEL, XA_DIM), D_MODEL ** -0.5),
        'w_xv': nrm((DEPTH, D_MODEL, XA_DIM), D_MODEL ** -0.5),
        'w_xo': nrm((DEPTH, XA_DIM, D_MODEL), 0.5 * XA_DIM ** -0.5),
        'ffn_w_in': nrm((DEPTH, D_MODEL, 2 * D_FF), D_MODEL ** -0.5),
        'ffn_conv_w': nrm((DEPTH, FFN_CONV, D_FF), FFN_CONV ** -0.5),
        'ffn_conv_b': nrm((DEPTH, D_FF), 0.02),
        'ffn_w_out': nrm((DEPTH, D_FF, D_MODEL), 0.5 * D_FF ** -0.5),
    }


def reference(x_prompt, x_sample, mem_prompt, state_rwkv, state_rwkv_shift, state_ssd, state_ssd_conv,
              state_gdn, state_gdn_conv, state_ffn_conv, cache_mem_k, cache_mem_v,
              norm_mix, norm_xa, norm_mem, norm_ffn, norm_final,
              w_in_ab, rw_mu, rw_w0, rw_w_up, rw_a0, rw_a_up, rw_g_up, rw_k_k, rw_k_a, rw_r_k, rw_gn_w, rw_gn_b,
              ssd_conv_w, ssd_conv_b, ssd_dt_bias, ssd_A_log, ssd_D, ssd_norm_w, w_out_ab,
              w_in_c, gdn_conv_w, gdn_A_log, gdn_dt_bias, gdn_norm_w, w_out_c,
              w_xq, w_xk, w_xv, w_xo, ffn_w_in, ffn_conv_w, ffn_conv_b, ffn_w_out):
    P = dict(norm_mix=norm_mix, norm_xa=norm_xa, norm_ffn=norm_ffn, norm_final=norm_final,
             w_in_ab=w_in_ab, rw_mu=rw_mu, rw_w0=rw_w0, rw_w_up=rw_w_up, rw_a0=rw_a0, rw_a_up=rw_a_up,
             rw_g_up=rw_g_up, rw_k_k=rw_k_k, rw_k_a=rw_k_a, rw_r_k=rw_r_k, rw_gn_w=rw_gn_w, rw_gn_b=rw_gn_b,
             ssd_conv_w=ssd_conv_w, ssd_conv_b=ssd_conv_b, ssd_dt_bias=ssd_dt_bias, ssd_A_log=ssd_A_log,
             ssd_D=ssd_D, ssd_norm_w=ssd_norm_w, w_out_ab=w_out_ab,
             w_in_c=w_in_c, gdn_conv_w=gdn_conv_w, gdn_A_log=gdn_A_log, gdn_dt_bias=gdn_dt_bias,
             gdn_norm_w=gdn_norm_w, w_out_c=w_out_c, w_xq=w_xq, w_xo=w_xo,
             ffn_w_in=ffn_w_in, ffn_conv_w=ffn_conv_w, ffn_conv_b=ffn_conv_b, ffn_w_out=ffn_w_out)

    mk_l, mv_l = [], []
    for l in range(DEPTH):
        mk, mv = memory_kv(mem_prompt, norm_mem[l], w_xk[l], w_xv[l])
        mk_l.append(mk)
        mv_l.append(mv)
    mem_k_p = jnp.stack(mk_l)
    mem_v_p = jnp.stack(mv_l)
    dt = x_prompt.dtype
    y_p, rw_p, sh_p, ssd_p, ssdc_p, gdn_p, gdnc_p, ffn_p = run_group(
        x_prompt, mem_k_p, mem_v_p,
        jnp.zeros((N_EVEN, BATCH, RW_HEADS, RW_HD, RW_HD), F32),
        jnp.zeros((N_EVEN, BATCH, RW_PROJ), dt),
        jnp.zeros((N_EVEN, BATCH, SSD_GROUPS, SSD_HPG, SSD_HD, SSD_STATE), F32),
        jnp.zeros((N_EVEN, BATCH, SSD_CONV - 1, SSD_CONV_DIM), dt),
        jnp.zeros((N_ODD, BATCH, GDN_HEADS, GDN_DK, GDN_DV), F32),
        jnp.zeros((N_ODD, BATCH, GDN_CONV - 1, GDN_CONV_DIM), dt),
        jnp.zeros((DEPTH, BATCH, FFN_CONV - 1, D_FF), dt),
        P)

    y_s, rw_s, sh_s, ssd_s, ssdc_s, gdn_s, gdnc_s, ffn_s = run_group(
        x_sample, cache_mem_k, cache_mem_v, state_rwkv, state_rwkv_shift, state_ssd, state_ssd_conv,
        state_gdn, state_gdn_conv, state_ffn_conv, P)

    return (y_p, y_s, rw_p, rw_s, sh_p, sh_s, ssd_p, ssd_s, ssdc_p, ssdc_s, gdn_p, gdn_s, gdnc_p, gdnc_s,
            ffn_p, ffn_s, mem_k_p, mem_v_p)
```

```python
import contextlib
import numpy as np
import concourse.bass as bass
import concourse.mybir as mybir
from concourse.bass_utils import run_bass_kernel_spmd

F32 = mybir.dt.float32
BF16 = mybir.dt.bfloat16
AF = mybir.ActivationFunctionType
ALU = mybir.AluOpType
AX = mybir.AxisListType

NCORES = 8
D = 2048
SEQ = 2048
NS = 16
LS = 4
TS = NS * LS
T = SEQ + TS
NMEM = 256
EPS = 1e-6
RW_PROJ = 3328
SSD_PROJ = 2576
AB_PROJ = RW_PROJ + SSD_PROJ
XA = 512


SAME_ENGINE_NOSYNC = ()


class Buf:
    def __init__(self, t, name):
        self.t = t
        self.name = name
        self.excl = False
        self.w = None
        self.r = {}

    def __getitem__(self, idx):
        return self.t[idx]


class KB:
    NDMA = 6

    def __init__(self):
        self.nc = bass.Bass("TRN2", target_bir_lowering=False)
        self.es = contextlib.ExitStack()
        nc = self.nc
        self.eng = {"pe": nc.tensor, "dve": nc.vector, "act": nc.scalar, "pool": nc.gpsimd, "sp": nc.sync}
        self.sems = {}
        self.cnt = {}
        for e in ("pe", "dve", "act", "pool"):
            self.sems[e] = self.es.enter_context(nc.semaphore("s_" + e))
            self.cnt[e] = 0
        self.dsem = {}
        self.dcnt = {}
        for q in ("sp", "pool", "act"):
            self.dsem[q] = [self.es.enter_context(nc.semaphore(f"d_{q}{i}")) for i in range(self.NDMA)]
            self.dcnt[q] = 0
        self.seen = {}
        self.out_events = []
        self.nid = 0

    def sb(self, name, shape, dt=F32):
        self.nid += 1
        name = f"{name}_{self.nid}"
        return Buf(self.es.enter_context(self.nc.sbuf_tensor(name, list(shape), dt)), name)

    def ps(self, name, shape, dt=F32):
        return Buf(self.es.enter_context(self.nc.psum_tensor(name, list(shape), dt)), name)

    def dram(self, name, shape, kind, dt=F32):
        return Buf(self.nc.dram_tensor(name, list(shape), dt, kind=kind).ap(), name)

    def _sem(self, key):
        if isinstance(key, str):
            return self.sems[key]
        return self.dsem[key[0]][key[1]]

    def _wait(self, engine, ev):
        key, val, src = ev
        if src == engine and engine in SAME_ENGINE_NOSYNC and isinstance(key, str):
            return
        k = (engine, key)
        if self.seen.get(k, 0) >= val:
            return
        self.eng[engine].wait_ge(self._sem(key), val)
        self.seen[k] = val

    def _deps(self, engine, rd, wr):
        for b in rd:
            if b.w is not None:
                self._wait(engine, b.w)
        for b in wr:
            if b.w is not None and not (engine == "pe" and b.w[2] == "pe"):
                self._wait(engine, b.w)
            for key, (val, src) in b.r.items():
                self._wait(engine, (key, val, src))

    def _mark(self, ev, rd, wr):
        key, val, src = ev
        for b in rd:
            b.r[key] = (val, src)
        for b in wr:
            b.w = ev
            b.r = {}

    def op(self, engine, fn, rd=(), wr=()):
        wr = list(wr) + [b for b in rd if b.excl and b not in wr]
        rd = [b for b in rd if not b.excl]
        self._deps(engine, rd, wr)
        ins = fn(self.eng[engine])
        self.cnt[engine] += 1
        ins.then_inc(self.sems[engine], 1)
        self._mark((engine, self.cnt[engine], engine), rd, wr)

    def _pe_mode(self, st):
        pass

    def mm(self, out, lhsT, rhs, start=True, stop=True, rd=(), wr=()):
        self._pe_mode(lhsT)
        self.op("pe", lambda e: e.matmul(out, lhsT, rhs, start=start, stop=stop), rd, wr)

    def tr(self, out, in_, identity, rd=(), wr=()):
        self._pe_mode(in_)
        self.op("pe", lambda e: e.transpose(out=out, in_=in_, identity=identity), rd, wr)

    def dma(self, q, out_buf, out_ap, in_buf, in_ap, is_output=False):
        i = self.dcnt[q]
        s = i % self.NDMA
        key = (q, s)
        if i >= self.NDMA:
            self._wait(q, (key, 16 * (i // self.NDMA), "dma"))
        self._deps(q, [in_buf], [out_buf])
        ins = self.eng[q].dma_start(out=out_ap, in_=in_ap)
        val = 16 * (i // self.NDMA + 1)
        ins.then_inc(self.dsem[q][s], 16)
        self.dcnt[q] += 1
        ev = (key, val, "dma")
        self._mark(ev, [in_buf], [out_buf])
        if is_output:
            self.out_events.append(ev)

    def barrier(self):
        evs = [(e, self.cnt[e], e) for e in ("pe", "dve", "act", "pool") if self.cnt[e]]
        for q in ("sp", "pool", "act"):
            i = self.dcnt[q]
            for s in range(self.NDMA):
                n_on_s = (i - s + self.NDMA - 1) // self.NDMA if i > s else 0
                if n_on_s:
                    evs.append(((q, s), 16 * n_on_s, "dma"))
        for e in ("pe", "dve", "act", "pool", "sp"):
            for ev in evs:
                self._wait(e, ev)

    @contextlib.contextmanager
    def scope(self):
        if getattr(self, "noscope", False):
            yield
            return
        old = self.es
        self.es = contextlib.ExitStack()
        try:
            yield
        finally:
            self.barrier()
            self.es.close()
            self.es = old

    def finish(self):
        for ev in self.out_events:
            self._wait("sp", ev)
        for e in ("pe", "dve", "act", "pool"):
            if self.cnt[e]:
                self._wait("sp", (e, self.cnt[e], e))
        self.es.close()
        return self.nc


DEBUG_UNITS = None
DEBUG_STEPS = 99
DEBUG_OMIT = ()
DEBUG_SKIP = ()
TT = 704
NTT = 3
CB = 47


def consts(kb):
    c = {}
    ident = kb.sb("ident", [128, 128], F32)
    kb.op("pool", lambda e: e.memset(ident[:], 0.0), wr=[ident])
    kb.op("pool", lambda e: e.affine_select(out=ident[:], in_=ident[:], pattern=[[-1, 128]], base=0,
                                            channel_multiplier=1, compare_op=ALU.not_equal, fill=1.0),
          rd=[ident], wr=[ident])
    ones_bf = kb.sb("ones_bf", [128, 128], BF16)
    kb.op("pool", lambda e: e.memset(ones_bf[:], 1.0), wr=[ones_bf])
    c["mkT"] = kb.sb("mkT", [128, 2, 4, NMEM], BF16)
    c["mvB"] = kb.sb("mvB", [128, 2, 2, XA], BF16)
    c["ident"] = ident
    c["ones_bf"] = ones_bf
    PS = kb.ps("PS", [128, 4096], F32)
    c["PS"] = PS
    c["bank"] = [Buf(PS.t[:, 512 * i:512 * (i + 1)], f"bank{i}") for i in range(8)]
    for bb in c["bank"]:
        bb.excl = True
    c["psb"] = c["bank"]
    return c


def stage_memkv(kb, c, io):
    nc = kb.nc
    ident, psb = c["ident"], c["psb"]
    mem_in, norm_mem, w_xk, w_xv, o_memk, o_memv = (io[k] for k in ("mem", "norm_mem", "w_xk", "w_xv", "o_memk", "o_memv"))
    with kb.scope():
        memT = kb.sb("memT", [128, 16, NMEM], BF16)
        mt = [kb.sb(f"mt{i}", [128, D], F32) for i in range(2)]
        junk = kb.sb("junk", [128, D], F32)
        ssq = kb.sb("ssq", [128, 2], F32)
        rstd = kb.sb("rstd", [128, 2], F32)
        for ti in range(2):
            kb.dma("sp", mt[ti], mt[ti][:], mem_in, mem_in[ti * 128:(ti + 1) * 128, :])
            kb.op("act", lambda e: e.activation(out=junk[:], in_=mt[ti][:], func=AF.Square,
                                                accum_out=ssq[:, ti:ti + 1]), rd=[mt[ti]], wr=[junk, ssq])
            kb.op("dve", lambda e: e.tensor_scalar(out=rstd[:, ti:ti + 1], in0=ssq[:, ti:ti + 1], scalar1=1.0 / D,
                                                   scalar2=EPS, op0=ALU.mult, op1=ALU.add), rd=[ssq], wr=[rstd])
            kb.op("act", lambda e: e.activation(out=rstd[:, ti:ti + 1], in_=rstd[:, ti:ti + 1], func=AF.Sqrt),
                  rd=[rstd], wr=[rstd])
            kb.op("dve", lambda e: e.reciprocal(out=rstd[:, ti:ti + 1], in_=rstd[:, ti:ti + 1]), rd=[rstd], wr=[rstd])
            kb.op("dve", lambda e: e.tensor_scalar(out=mt[ti][:], in0=mt[ti][:], scalar1=rstd[:, ti:ti + 1],
                                                   scalar2=None, op0=ALU.mult), rd=[mt[ti], rstd], wr=[mt[ti]])
            for cc in range(16):
                pb = psb[cc % 4]
                kb.tr(out=pb[:, 0:128], in_=mt[ti][:, cc * 128:(cc + 1) * 128],
                                                  identity=ident[:], rd=[mt[ti], ident], wr=[pb])
                kb.op("dve", lambda e: e.tensor_copy(out=memT[:, cc, ti * 128:(ti + 1) * 128], in_=pb[:, 0:128]),
                      rd=[pb], wr=[memT])
        gm = kb.sb("gm", [128, 2, 16], F32)
        with nc.allow_non_contiguous_dma(reason="tiny gain vectors"):
            kb.dma("sp", gm, gm[:], norm_mem, norm_mem.t.rearrange("l (c p) -> p l c", p=128))
        memTg = kb.sb("memTg", [128, 16, NMEM], BF16)
        wkv = [kb.sb(f"wkv{i}", [128, 16, XA], BF16) for i in range(2)]
        okv = [kb.sb(f"okv{i}", [128, XA], F32) for i in range(2)]
        n = 0
        for l in range(2):
            for cc in range(16):
                kb.op("dve", lambda e: e.tensor_scalar(out=memTg[:, cc, :], in0=memT[:, cc, :],
                                                       scalar1=gm[:, l, cc:cc + 1], scalar2=None, op0=ALU.mult),
                      rd=[memT, gm], wr=[memTg])
            for wsrc, odst in ((w_xk, o_memk), (w_xv, o_memv)):
                wb = wkv[n % 2]
                kb.dma("pool", wb, wb[:], wsrc, wsrc.t[l].rearrange("(c p) n -> p c n", p=128))
                for ti in range(2):
                    pb = psb[4 + (n * 2 + ti) % 4]
                    for cc in range(16):
                        kb.mm(pb[:, :], memTg[:, cc, ti * 128:(ti + 1) * 128], wb[:, cc, :],
                                                       start=(cc == 0), stop=(cc == 15), rd=[memTg, wb], wr=[pb])
                    ob = okv[ti]
                    kb.op("act", lambda e: e.copy(out=ob[:], in_=pb[:, :]), rd=[pb], wr=[ob])
                    kb.dma("sp", odst, odst[l, ti * 128:(ti + 1) * 128, :], ob, ob[:], is_output=True)
                    if wsrc is w_xv:
                        kb.op("dve", lambda e: e.tensor_copy(out=c["mvB"][:, l, ti, :], in_=pb[:, :]), rd=[pb], wr=[c["mvB"]])
                if wsrc is w_xk:
                    for hh in range(4):
                        pb = psb[hh % 4]
                        for cc in range(16):
                            kb.mm(pb[:, 0:NMEM], wb[:, cc, hh * 128:(hh + 1) * 128], memTg[:, cc, :],
                                  start=(cc == 0), stop=(cc == 15), rd=[memTg, wb], wr=[pb])
                        kb.op("act", lambda e: e.copy(out=c["mkT"][:, l, hh, :], in_=pb[:, 0:NMEM]), rd=[pb], wr=[c["mkT"]])
                n += 1


def load_gain(kb, name, src, l):
    g = kb.sb(name, [128, 16], F32)
    with kb.nc.allow_non_contiguous_dma(reason="tiny gain vectors"):
        kb.dma("sp", g, g[:], src, src.t[l].rearrange("(c p) -> p c", p=128))
    return g


def rmsnorm_cm(kb, c, hT, gain, hnT, tmp_sq, rstd):
    psb, ones_bf = c["psb"], c["ones_bf"]
    H = TT // 2
    for cc in range(16):
        sq = tmp_sq[cc % 2]
        kb.op("act", lambda e: e.activation(out=sq[:], in_=hT[:, cc, :], func=AF.Square), rd=[hT], wr=[sq])
        for hh in range(2):
            pb = psb[hh]
            kb.mm(pb[:, 0:H], ones_bf[:], sq[:, hh * H:(hh + 1) * H],
                                           start=(cc == 0), stop=(cc == 15), rd=[sq, ones_bf], wr=[pb])
    for hh in range(2):
        pb = psb[hh]
        kb.op("dve", lambda e: e.tensor_scalar(out=rstd[:, hh * H:(hh + 1) * H], in0=pb[:, 0:H], scalar1=1.0 / D,
                                               scalar2=EPS, op0=ALU.mult, op1=ALU.add), rd=[pb], wr=[rstd])
    kb.op("act", lambda e: e.activation(out=rstd[:], in_=rstd[:], func=AF.Sqrt), rd=[rstd], wr=[rstd])
    kb.op("dve", lambda e: e.reciprocal(out=rstd[:], in_=rstd[:]), rd=[rstd], wr=[rstd])
    for cc in range(16):
        dst = hT if hnT is None else hnT
        kb.op("dve", lambda e: e.scalar_tensor_tensor(out=dst[:, cc, :], in0=hT[:, cc, :], scalar=gain[:, cc:cc + 1],
                                                      in1=rstd[:], op0=ALU.mult, op1=ALU.mult), rd=[hT, gain, rstd], wr=[dst])


def stage_A0(kb, c, io):
    ident, psb = c["ident"], c["psb"]
    x_in, w_in, hT_d, pT_d = io["x"], io["w_in_ab"], io["hT"], io["pT0"]
    o_sh, o_ssdc = io["o_sh"], io["o_ssdc"]
    with kb.scope():
        gain = load_gain(kb, "g_mix0", io["norm_mix"], 0)
        hT = kb.sb("A_hT", [128, 16, TT], F32)
        hnT = kb.sb("A_hnT", [128, 16, TT], BF16)
        xr = [kb.sb(f"A_xr{i}", [128, D], F32) for i in range(2)]
        sq = [kb.sb(f"A_sq{i}", [128, TT], BF16) for i in range(2)]
        rstd = kb.sb("A_rstd", [128, TT], F32)
        wt = [kb.sb(f"A_wt{i}", [128, 16, 512], BF16) for i in range(2)]
        ev = [kb.sb(f"A_ev{i}", [128, TT], F32) for i in range(3)]
        ptl = [kb.sb(f"A_ptl{i}", [128, 512], F32) for i in range(2)]
        nev = 0
        for tt in range(NTT):
            c0 = tt * TT
            nrb = (TT + 127) // 128
            for rb in range(nrb):
                r0 = rb * 128
                nr = min(128, TT - r0)
                xb = xr[rb % 2]
                kb.dma("sp", xb, xb[0:nr, :], x_in, x_in[c0 + r0:c0 + r0 + nr, :])
                for g4 in range(4):
                    pb = psb[2 + (rb * 4 + g4) % 4]
                    for k in range(4):
                        cc = g4 * 4 + k
                        kb.tr(out=pb[:, k * 128:k * 128 + nr],
                                                          in_=xb[0:nr, cc * 128:(cc + 1) * 128],
                                                          identity=ident[0:nr, 0:nr], rd=[xb, ident], wr=[pb])
                    kb.op("act" if g4 % 2 else "dve",
                          (lambda e: e.copy(out=hT[:, g4 * 4:g4 * 4 + 4, r0:r0 + nr],
                                            in_=pb[:, :].rearrange("p (k n) -> p k n", n=128)[:, :, 0:nr])) if g4 % 2 else
                          (lambda e: e.tensor_copy(out=hT[:, g4 * 4:g4 * 4 + 4, r0:r0 + nr],
                                                   in_=pb[:, :].rearrange("p (k n) -> p k n", n=128)[:, :, 0:nr])),
                          rd=[pb], wr=[hT])
            kb.dma("sp", hT_d, hT_d.t.rearrange("(c p) t -> p c t", p=128)[:, :, c0:c0 + TT], hT, hT[:])
            rmsnorm_cm(kb, c, hT, gain, hnT, sq, rstd)
            nslab = (AB_PROJ + 511) // 512
            for sl in range(nslab):
                n0 = sl * 512
                ncol = min(512, AB_PROJ - n0)
                wb = wt[sl % 2]
                kb.dma("pool", wb, wb[:, :, 0:ncol], w_in,
                       w_in.t[:, n0:n0 + ncol].rearrange("(c p) n -> p c n", p=128))
                for j in range((ncol + 127) // 128):
                    m = min(128, ncol - j * 128)
                    blk = sl * 4 + j
                    eb = ev[nev % 3]
                    nev += 1
                    for hh in range(2):
                        pb = psb[2 + (blk * 2 + hh) % 4]
                        H = TT // 2
                        for kc in range(16):
                            kb.mm(pb[0:m, 0:H], wb[:, kc, j * 128:j * 128 + m],
                                                           hnT[:, kc, hh * H:(hh + 1) * H],
                                                           start=(kc == 0), stop=(kc == 15), rd=[wb, hnT], wr=[pb])
                        kb.op("act" if hh else "dve",
                              (lambda e: e.copy(out=eb[0:m, hh * H:(hh + 1) * H], in_=pb[0:m, 0:H])) if hh else
                              (lambda e: e.tensor_copy(out=eb[0:m, hh * H:(hh + 1) * H], in_=pb[0:m, 0:H])),
                              rd=[pb], wr=[eb])
                    kb.dma("sp", pT_d, pT_d[blk * 128:blk * 128 + m, c0:c0 + TT], eb, eb[0:m, :])
                if tt == NTT - 1:
                    pb = psb[6 + sl % 2]
                    for kc in range(16):
                        kb.mm(pb[:, 0:ncol], hnT[:, kc, 576:704], wb[:, kc, 0:ncol],
                                                       start=(kc == 0), stop=(kc == 15), rd=[wb, hnT], wr=[pb])
                    pt = ptl[sl % 2]
                    kb.op("act", lambda e: e.copy(out=pt[:, 0:ncol], in_=pb[:, 0:ncol]), rd=[pb], wr=[pt])
                    lo, hi = max(n0, 0), min(n0 + ncol, RW_PROJ)
                    if lo < hi:
                        kb.dma("sp", o_sh, o_sh[0:1, lo:hi], pt, pt[63:64, lo - n0:hi - n0], is_output=True)
                        kb.dma("sp", o_sh, o_sh[1:1 + NS, lo:hi], pt, pt[67:128:4, lo - n0:hi - n0], is_output=True)
                    lo, hi = max(n0, 4352), min(n0 + ncol, 5888)
                    if lo < hi:
                        kb.dma("sp", o_ssdc, o_ssdc[0, :, lo - 4352:hi - 4352], pt, pt[61:64, lo - n0:hi - n0],
                               is_output=True)
                        for t in range(1, 4):
                            kb.dma("sp", o_ssdc, o_ssdc[1:1 + NS, t - 1, lo - 4352:hi - 4352], pt,
                                   pt[64 + t:128:4, lo - n0:hi - n0], is_output=True)


def build():
    kb = KB()
    io = {}
    def din(name, shape):
        io[name] = kb.dram(name, shape, "ExternalInput")
    def dout(name, shape):
        io[name] = kb.dram(name, shape, "ExternalOutput")
    def dscr(name, shape, dt=F32):
        io[name] = kb.dram(name, shape, SCRATCH_KIND, dt)
    din("x", [T, D]); din("mem", [NMEM, D]); din("norm_mem", [2, D]); din("w_xk", [2, D, XA]); din("w_xv", [2, D, XA])
    din("norm_mix", [2, D]); din("w_in_ab", [D, AB_PROJ])
    dout("o_memk", [2, NMEM, XA]); dout("o_memv", [2, NMEM, XA])
    dout("o_sh", [1 + NS, RW_PROJ]); dout("o_ssdc", [1 + NS, 3, 1536])
    dscr("hT", [D, T]); dscr("pT0", [AB_PROJ, T]); dscr("ymix", [2048, T]); din("ssd_D", [16]); din("ssd_norm_w", [1024])
    din("ssd_conv_w", [4, 1536]); din("ssd_conv_b", [1536]); din("ssd_dt_bias", [16]); din("ssd_A_log", [16])
    din("state_ssd_conv", [NS, 3, 1536]); din("state_ssd", [NS, 16, 64, 128])
    dout("o_ssd", [1 + NS, 16, 64, 128])
    for nm, shp in (("rw_mu", [RW_PROJ]), ("rw_w0", [1024]), ("rw_a0", [1024]), ("rw_k_k", [1024]), ("rw_k_a", [1024]),
                    ("rw_r_k", [1024]), ("rw_gn_w", [1024]), ("rw_gn_b", [1024]), ("rw_w_up", [64, 1024]),
                    ("rw_a_up", [64, 1024]), ("rw_g_up", [128, 1024]), ("state_rwkv", [NS, 16, 64, 64]),
                    ("state_rwkv_shift", [NS, RW_PROJ])):
        din(nm, shp)
    dout("o_rw", [1 + NS, 16, 64, 64])
    din("norm_xa", [2, D]); din("norm_ffn", [2, D]); din("w_out_ab", [D, D]); din("w_xq", [2, D, XA]); din("w_xo", [2, XA, D])
    din("ffn_w_in", [2, D, 2 * 5632]); din("ffn_conv_w", [2, 3, 5632]); din("ffn_conv_b", [2, 5632]); din("ffn_w_out", [2, 5632, D])
    din("state_ffn_conv", [2, NS, 2, 5632]); din("cache_mem_k", [2, NS, NMEM, 4, 128]); din("cache_mem_v", [2, NS, NMEM, 4, 128])
    din("w_in_c", [D, 8224])
    dout("o_ffn", [2, 1 + NS, 2, 5632]); dout("o_gdnc", [1 + NS, 3, 6144])
    dscr("pT1", [8224, T])
    din("gdn_conv_w", [4, 6144]); din("gdn_A_log", [16]); din("gdn_dt_bias", [16]); din("gdn_norm_w", [128])
    din("state_gdn_conv", [NS, 3, 6144]); din("state_gdn", [NS, 16, 128, 128]); din("w_out_c", [D, D]); din("norm_final", [D])
    dout("o_gdn", [1 + NS, 16, 128, 128]); dout("o_y", [T, D])
    c = consts(kb)
    if "memkv" not in DEBUG_SKIP:
        stage_memkv(kb, c, io)
    if "A0" not in DEBUG_SKIP:
        stage_A0(kb, c, io)
    def drain(*gens):
        alive = list(gens)
        while alive:
            for g in list(alive):
                try:
                    next(g)
                except StopIteration:
                    alive.remove(g)
    if "Bssd" not in DEBUG_SKIP and "Brw" not in DEBUG_SKIP:
        with kb.scope():
            kb.noscope = True
            drain(stage_B_rwkv(kb, c, io), stage_B_ssd(kb, c, io))
            kb.noscope = False
    else:
        if "Bssd" not in DEBUG_SKIP:
            drain(stage_B_ssd(kb, c, io))
        if "Brw" not in DEBUG_SKIP:
            drain(stage_B_rwkv(kb, c, io))
    if "C0" not in DEBUG_SKIP:
        stage_C(kb, c, io, 0)
    if "Bgdn" not in DEBUG_SKIP:
        drain(stage_B_gdn(kb, c, io))
    if "C1" not in DEBUG_SKIP:
        stage_C(kb, c, io, 1)
    return kb.finish()


def bc(ap, shape):
    return ap.to_broadcast(list(shape))


def stage_B_ssd(kb, c, io):
    nc = kb.nc
    PS, bank, ident = c["PS"], c["bank"], c["ident"]
    pT, ysc = io["pT0"], io["ymix"]
    XB0 = RW_PROJ + 1024
    DT0 = RW_PROJ + 2560
    with kb.scope():
        cw = kb.sb("S_cw", [128, 12, 4])
        cb = kb.sb("S_cb", [128, 12, 1])
        dtb = kb.sb("S_dtb", [64, 16])
        Aneg = kb.sb("S_A", [64, 16])
        with nc.allow_non_contiguous_dma(reason="tiny parameter vectors"):
            for i in range(4):
                kb.dma("sp", cw, cw[:, :, i], io["ssd_conv_w"], io["ssd_conv_w"].t[i].rearrange("(c p) -> p c", p=128))
            kb.dma("sp", cb, cb[:, :, 0], io["ssd_conv_b"], io["ssd_conv_b"].t.rearrange("(c p) -> p c", p=128))
            kb.dma("sp", dtb, dtb[:], io["ssd_dt_bias"], io["ssd_dt_bias"].t.partition_broadcast(64))
            kb.dma("sp", Aneg, Aneg[:], io["ssd_A_log"], io["ssd_A_log"].t.partition_broadcast(64))
        kb.op("act", lambda e: e.activation(out=Aneg[:], in_=Aneg[:], func=AF.Exp), rd=[Aneg], wr=[Aneg])
        kb.op("dve", lambda e: e.tensor_scalar(out=Aneg[:], in0=Aneg[:], scalar1=-1.0, scalar2=None, op0=ALU.mult),
              rd=[Aneg], wr=[Aneg])
        Dsk = kb.sb("S_Dsk", [128, 8, 1])
        nw = kb.sb("S_nw", [128, 8, 1])
        with nc.allow_non_contiguous_dma(reason="tiny parameter vectors"):
            dv = io["ssd_D"].t.rearrange("(j hh) -> hh j", hh=2)
            kb.dma("sp", Dsk, Dsk[0:64, :, 0], io["ssd_D"], dv[0].partition_broadcast(64))
            kb.dma("sp", Dsk, Dsk[64:128, :, 0], io["ssd_D"], dv[1].partition_broadcast(64))
            kb.dma("sp", nw, nw[:, :, 0], io["ssd_norm_w"], io["ssd_norm_w"].t.rearrange("(c p) -> p c", p=128))
        one128 = kb.sb("S_one128", [128, 128], BF16)
        kb.op("pool", lambda e: e.memset(one128[:], 1.0), wr=[one128])
        zt = [kb.sb("S_zt0", [128, 8, 64])] * 2
        yg = None
        ysq = None
        ysqb = kb.sb("S_ysqb", [128, 8, 64], BF16)
        rsd = kb.sb("S_rsd", [128, 2, 64])
        triU = kb.sb("S_triU", [64, 64])
        kb.op("pool", lambda e: e.memset(triU[:], 1.0), wr=[triU])
        kb.op("pool", lambda e: e.affine_select(out=triU[:], in_=triU[:], pattern=[[1, 64]], base=0,
                                                channel_multiplier=-1, compare_op=ALU.is_ge, fill=0.0),
              rd=[triU], wr=[triU])
        ones64 = kb.sb("S_ones", [128, 128])
        kb.op("pool", lambda e: e.memset(ones64[:], 1.0), wr=[ones64])

        raw = [kb.sb("S_raw0", [128, 12, 67])] * 2
        dtr = [kb.sb(f"S_dtr{i}", [16, 64]) for i in range(2)]
        t1 = kb.sb("S_t1", [128, 12, 64])
        t2 = kb.sb("S_t2", [128, 12, 64])
        xbc = kb.sb("S_xbc", [128, 12, 64])
        yg, ysq = t2, t1
        dtT = kb.sb("S_dtT", [64, 16])
        dA = kb.sb("S_dA", [64, 16])
        acs = kb.sb("S_acs", [64, 16])
        G = kb.sb("S_G", [128, 16, 64])
        kb.op("pool", lambda e: e.memset(G[:], 0.0), wr=[G])
        Ebc = kb.sb("S_Ebc", [128, 16, 64])
        Lm = kb.sb("S_Lm", [64, 16, 64])
        X = kb.sb("S_X", [64, 16, 64], BF16)
        Xd = kb.sb("S_Xd", [64, 16, 64], BF16)
        Btm = kb.sb("S_Btm", [64, 2, 128], BF16)
        MT = kb.sb("S_MT", [64, 16, 64], BF16)
        Cdec = kb.sb("S_Cdec", [128, 16, 64], BF16)
        hT = kb.sb("S_hT", [128, 16, 64], BF16)
        bcb = kb.sb("S_bcb", [128, 4, 64], BF16)
        hn = kb.sb("S_hn", [64, 16, 128])
        yT = [kb.sb("S_yT0", [128, 8, 64])] * 2

        units = [(64, 64 * ch, ch, None) for ch in range(SEQ // 64)] + [(LS, SEQ + LS * s, 0, s) for s in range(NS)]
        if DEBUG_UNITS is not None:
            units = [units[i] for i in DEBUG_UNITS]
        for ui, (C, t0, ch, s) in enumerate(units):
            rw, dr = raw[ui % 2], dtr[ui % 2]
            src = pT.t[XB0:XB0 + 1536, :].rearrange("(c p) t -> p c t", p=128)
            if s is None and ch > 0:
                kb.dma("sp", rw, rw[:, :, 0:3 + C], pT, src[:, :, t0 - 3:t0 + C])
            else:
                kb.dma("sp", rw, rw[:, :, 3:3 + C], pT, src[:, :, t0:t0 + C])
                if s is None:
                    kb.op("pool", lambda e: e.memset(rw[:, :, 0:3], 0.0), wr=[rw])
                else:
                    with nc.allow_non_contiguous_dma(reason="conv state transposing load (small)"):
                        for cc in range(12):
                            kb.dma("sp", rw, rw[:, cc, 0:3], io["state_ssd_conv"],
                                   io["state_ssd_conv"].t[s, :, cc * 128:(cc + 1) * 128].rearrange("r p -> p r"))
            kb.dma("sp", dr, dr[:, 0:C], pT, pT[DT0:DT0 + 16, t0:t0 + C])
            zb = zt[ui % 2]
            kb.dma("sp", zb, zb[:, :, 0:C], pT, pT.t[RW_PROJ:RW_PROJ + 1024, :].rearrange("(c p) t -> p c t", p=128)[:, :, t0:t0 + C])
            if s is None and ch == 0:
                kb.op("pool", lambda e: e.memset(hn[:], 0.0), wr=[hn])
            elif s is not None:
                kb.dma("sp", hn, hn[:], io["state_ssd"], io["state_ssd"].t[s].rearrange("h p n -> p h n"))
            yield
            if 1 not in DEBUG_OMIT:
                kb.op("dve", lambda e: e.tensor_tensor(out=t1[:, :, 0:C], in0=rw[:, :, 0:C], in1=bc(cw[:, :, 0:1], [128, 12, C]),
                                                       op=ALU.mult), rd=[rw, cw], wr=[t1])
                for i in range(1, 4):
                    kb.op("pool", lambda e: e.tensor_tensor(out=t2[:, :, 0:C], in0=rw[:, :, i:i + C],
                                                            in1=bc(cw[:, :, i:i + 1], [128, 12, C]), op=ALU.mult),
                          rd=[rw, cw], wr=[t2])
                    kb.op("dve", lambda e: e.tensor_tensor(out=t1[:, :, 0:C], in0=t1[:, :, 0:C], in1=t2[:, :, 0:C], op=ALU.add),
                          rd=[t1, t2], wr=[t1])
                kb.op("dve", lambda e: e.tensor_tensor(out=t1[:, :, 0:C], in0=t1[:, :, 0:C], in1=bc(cb[:, :, 0:1], [128, 12, C]),
                                                       op=ALU.add), rd=[t1, cb], wr=[t1])
                kb.op("act", lambda e: e.activation(out=xbc[:, :, 0:C], in_=t1[:, :, 0:C], func=AF.Silu), rd=[t1], wr=[xbc])
            yield
            if 2 not in DEBUG_OMIT:
                kb.tr(out=bank[0][0:C, 0:16], in_=dr[0:16, 0:C], identity=ident[0:16, 0:16],
                      rd=[dr, ident], wr=[bank[0]])
                kb.op("dve", lambda e: e.tensor_tensor(out=dtT[0:C, :], in0=bank[0][0:C, 0:16], in1=dtb[0:C, :], op=ALU.add),
                      rd=[bank[0], dtb], wr=[dtT])
                kb.op("act", lambda e: e.activation(out=dtT[0:C, :], in_=dtT[0:C, :], func=AF.Exp), rd=[dtT], wr=[dtT])
                kb.op("act", lambda e: e.activation(out=dtT[0:C, :], in_=dtT[0:C, :], func=AF.Ln, bias=1.0), rd=[dtT], wr=[dtT])
                kb.op("dve", lambda e: e.tensor_tensor(out=dA[0:C, :], in0=dtT[0:C, :], in1=Aneg[0:C, :], op=ALU.mult),
                      rd=[dtT, Aneg], wr=[dA])
            else:
                kb.op("dve", lambda e: e.memset(dA[:], -0.1), wr=[dA])
            yield
            kb.op("dve", lambda e: e.tensor_copy(out=G[0:C, :, 0:C], in_=bc(triU[0:C, None, 0:C], [C, 16, C])),
                  rd=[triU], wr=[G])
            kb.op("dve", lambda e: e.tensor_tensor(out=G[0:C, :, 0:C], in0=G[0:C, :, 0:C],
                                                   in1=bc(dA[0:C, :, None], [C, 16, C]), op=ALU.mult),
                  rd=[G, dA], wr=[G])
            kb.mm(bank[0][0:C, 16:32], triU[0:C, 0:C], dA[0:C, :], start=True, stop=True,
                  rd=[triU, dA], wr=[bank[0]])
            kb.op("dve", lambda e: e.tensor_copy(out=acs[0:C, :], in_=bank[0][0:C, 16:32]), rd=[bank[0]], wr=[acs])
            hpb = 512 // C
            nb = (16 + hpb - 1) // hpb
            for b in range(nb):
                h0, h1 = b * hpb, min(16, (b + 1) * hpb)
                hq = max(1, 256 // C)
                for ha in range(h0, h1, hq):
                    hb = min(h1, ha + hq)
                    kb.mm(bank[1 + b][:, (ha - h0) * C:(hb - h0) * C], ones64[:, :],
                                                   G[:, ha:hb, 0:C], start=True, stop=True,
                          rd=[ones64, G], wr=[bank[1 + b]])
            for b in range(nb):
                h0, h1 = b * hpb, min(16, (b + 1) * hpb)
                pv = bank[1 + b][:, 0:(h1 - h0) * C].rearrange("p (h l) -> p h l", l=C)
                if 31 not in DEBUG_OMIT:
                    kb.op("act", lambda e: e.activation(out=Ebc[:, h0:h1, 0:C], in_=pv, func=AF.Exp),
                          rd=[bank[1 + b]], wr=[Ebc])
                if 32 not in DEBUG_OMIT:
                  kb.op("dve", lambda e: e.tensor_tensor(out=Lm[0:C, h0:h1, 0:C], in0=pv[0:C],
                                                       in1=bc(acs[0:C, h0:h1, None], [C, h1 - h0, C]), op=ALU.subtract),
                      rd=[bank[1 + b], acs], wr=[Lm])
            kb.op("dve", lambda e: e.tensor_scalar(out=Lm[0:C, :, 0:C], in0=Lm[0:C, :, 0:C], scalar1=0.0, scalar2=None,
                                                   op0=ALU.min), rd=[Lm], wr=[Lm])
            kb.op("act", lambda e: e.activation(out=Lm[0:C, :, 0:C], in_=Lm[0:C, :, 0:C], func=AF.Exp), rd=[Lm], wr=[Lm])
            kb.op("dve", lambda e: e.tensor_tensor(out=Lm[0:C, :, 0:C], in0=Lm[0:C, :, 0:C],
                                                   in1=bc(triU[0:C, None, 0:C], [C, 16, C]), op=ALU.mult),
                  rd=[Lm, triU], wr=[Lm])
            yield
            for blk in range(8):
                b = 3 + blk // 4
                kb.tr(out=bank[b][0:C, (blk % 4) * 128:(blk % 4 + 1) * 128],
                                                  in_=xbc[:, blk, 0:C], identity=ident[:], rd=[xbc, ident], wr=[bank[b]])
            for b in range(2):
                kb.op("dve", lambda e: e.tensor_tensor(out=X[0:C, 8 * b:8 * b + 8, :],
                                                       in0=bank[3 + b][0:C, :].rearrange("p (h q) -> p h q", q=64),
                                                       in1=bc(dtT[0:C, 8 * b:8 * b + 8, None], [C, 8, 64]), op=ALU.mult),
                      rd=[bank[3 + b], dtT], wr=[X])
            for g in range(2):
                kb.tr(out=bank[5][0:C, g * 128:(g + 1) * 128], in_=xbc[:, 8 + g, 0:C],
                                                  identity=ident[:], rd=[xbc, ident], wr=[bank[5]])
            kb.op("act", lambda e: e.copy(out=Btm[0:C, :, :], in_=bank[5][0:C, 0:256].rearrange("p (g n) -> p g n", n=128)),
                  rd=[bank[5]], wr=[Btm])
            kb.op("dve", lambda e: e.tensor_tensor(out=Xd[0:C], in0=X[0:C], in1=bc(Lm[0:C, :, C - 1:C], [C, 16, 64]),
                                                   op=ALU.mult), rd=[X, Lm], wr=[Xd])
            yield
            kb.op("act", lambda e: e.copy(out=bcb[:, :, 0:C], in_=xbc[:, 8:12, 0:C]), rd=[xbc], wr=[bcb])
            for g in range(2):
                kb.mm(bank[5][0:C, 256 + g * 64:256 + g * 64 + C], bcb[:, g, 0:C],
                                               bcb[:, 2 + g, 0:C], start=True, stop=True, rd=[bcb], wr=[bank[5]])
            for g in range(2):
                kb.op("dve", lambda e: e.tensor_tensor(
                    out=MT[0:C, 8 * g:8 * g + 8, 0:C], in0=Lm[0:C, 8 * g:8 * g + 8, 0:C],
                    in1=bc(bank[5][0:C, None, 256 + g * 64:256 + g * 64 + C], [C, 8, C]), op=ALU.mult),
                    rd=[Lm, bank[5]], wr=[MT])
                kb.op("pool", lambda e: e.tensor_tensor(
                    out=Cdec[:, 8 * g:8 * g + 8, 0:C], in0=Ebc[:, 8 * g:8 * g + 8, 0:C],
                    in1=bc(xbc[:, 10 + g, None, 0:C], [128, 8, C]), op=ALU.mult), rd=[Ebc, xbc], wr=[Cdec])
            yield
            for h in range(16):
                b = 6 + h // 8
                kb.tr(out=bank[b][:, (h % 8) * 64:(h % 8 + 1) * 64], in_=hn[:, h, :],
                                                  identity=ident[0:64, 0:64], rd=[hn, ident], wr=[bank[b]])
            for b in range(2):
                kb.op("act" if b else "dve",
                      (lambda e: e.copy(out=hT[:, 8 * b:8 * b + 8, :], in_=bank[6 + b][:, :].rearrange("p (h q) -> p h q", q=64)))
                      if b else
                      (lambda e: e.tensor_copy(out=hT[:, 8 * b:8 * b + 8, :], in_=bank[6 + b][:, :].rearrange("p (h q) -> p h q", q=64))),
                      rd=[bank[6 + b]], wr=[hT])
            yield
            for h in range(16):
                po = (h % 2) * 64
                o = bank[0][po:po + 64, (h // 2) * 64:(h // 2) * 64 + C]
                kb.mm(o, X[0:C, h, :], MT[0:C, h, 0:C], start=True, stop=False,
                      rd=[X, MT], wr=[bank[0]])
                kb.mm(o, hT[:, h, :], Cdec[:, h, 0:C], start=False, stop=True,
                      rd=[hT, Cdec], wr=[bank[0]])
            yb = yT[ui % 2]
            kb.op("dve", lambda e: e.tensor_tensor(out=yg[:, 0:8, 0:C], in0=xbc[:, 0:8, 0:C], in1=bc(Dsk[:], [128, 8, C]), op=ALU.mult),
                  rd=[xbc, Dsk], wr=[yg])
            kb.op("dve", lambda e: e.tensor_tensor(out=yg[:, 0:8, 0:C], in0=yg[:, 0:8, 0:C],
                                                   in1=bank[0][:, :].rearrange("p (k l) -> p k l", l=64)[:, :, 0:C], op=ALU.add),
                  rd=[yg, bank[0]], wr=[yg])
            kb.op("act", lambda e: e.activation(out=ysq[:, 0:8, 0:C], in_=zb[:, :, 0:C], func=AF.Silu), rd=[zb], wr=[ysq])
            kb.op("dve", lambda e: e.tensor_tensor(out=yg[:, 0:8, 0:C], in0=yg[:, 0:8, 0:C], in1=ysq[:, 0:8, 0:C], op=ALU.mult), rd=[yg, ysq], wr=[yg])
            kb.op("act", lambda e: e.activation(out=ysqb[:, :, 0:C], in_=yg[:, 0:8, 0:C], func=AF.Square), rd=[yg], wr=[ysqb])
            for g in range(2):
                for q in range(4):
                    kb.mm(bank[5][:, g * 64:g * 64 + C], one128[:], ysqb[:, 4 * g + q, 0:C], start=(q == 0), stop=(q == 3),
                          rd=[one128, ysqb], wr=[bank[5]])
            kb.op("dve", lambda e: e.tensor_scalar(out=rsd[:, :, 0:C], in0=bank[5][:, 0:128].rearrange("p (g l) -> p g l", l=64)[:, :, 0:C],
                                                   scalar1=1.0 / 512, scalar2=EPS, op0=ALU.mult, op1=ALU.add), rd=[bank[5]], wr=[rsd])
            kb.op("act", lambda e: e.activation(out=rsd[:, :, 0:C], in_=rsd[:, :, 0:C], func=AF.Sqrt), rd=[rsd], wr=[rsd])
            kb.op("dve", lambda e: e.reciprocal(out=rsd[:, :, 0:C], in_=rsd[:, :, 0:C]), rd=[rsd], wr=[rsd])
            for g in range(2):
                kb.op("dve", lambda e: e.tensor_tensor(out=yg[:, 4 * g:4 * g + 4, 0:C], in0=yg[:, 4 * g:4 * g + 4, 0:C],
                                                       in1=bc(rsd[:, g:g + 1, 0:C], [128, 4, C]), op=ALU.mult), rd=[yg, rsd], wr=[yg])
            kb.op("dve", lambda e: e.tensor_tensor(out=yb[:, :, 0:C], in0=yg[:, 0:8, 0:C], in1=bc(nw[:], [128, 8, C]), op=ALU.mult),
                  rd=[yg, nw], wr=[yb])
            kb.dma("pool", ysc, ysc.t[1024:2048, :].rearrange("(k p) t -> p k t", p=128)[:, :, t0:t0 + C], yb, yb[:, :, 0:C])
            yield
            for h in range(16):
                b = 1 + h // 4
                kb.mm(bank[b][0:64, (h % 4) * 128:(h % 4 + 1) * 128], Xd[0:C, h, :],
                                               Btm[0:C, h // 8, :], start=True, stop=True, rd=[Xd, Btm], wr=[bank[b]])
            kb.op("dve", lambda e: e.tensor_tensor(out=hn[:], in0=hn[:], in1=bc(Ebc[0:64, :, C - 1:C], [64, 16, 128]),
                                                   op=ALU.mult), rd=[hn, Ebc], wr=[hn])
            for b in range(4):
                kb.op("dve", lambda e: e.tensor_tensor(out=hn[:, 4 * b:4 * b + 4, :], in0=hn[:, 4 * b:4 * b + 4, :],
                                                       in1=bank[1 + b][0:64, :].rearrange("p (h n) -> p h n", n=128),
                                                       op=ALU.add), rd=[hn, bank[1 + b]], wr=[hn])
            if s is not None or ch == SEQ // 64 - 1 or DEBUG_UNITS is not None:
                row = 0 if s is None else 1 + s
                kb.dma("pool", io["o_ssd"], io["o_ssd"].t[row].rearrange("h p n -> p h n"), hn, hn[:], is_output=True)
            yield


def hd_param(kb, name, src, n=16):
    t = kb.sb(name, [64, n, 1])
    with kb.nc.allow_non_contiguous_dma(reason="tiny parameter vectors"):
        kb.dma("sp", t, t[:, :, 0], src, src.t.rearrange("(h k) -> k h", k=64))
    return t


def stage_B_rwkv(kb, c, io):
    nc = kb.nc
    PS, bank, ident = c["PS"], c["bank"], c["ident"]
    pT, ymix = io["pT0"], io["ymix"]
    with kb.scope():
        muH = kb.sb("R_muH", [64, 48, 1])
        muW = kb.sb("R_muW", [64, 2, 1])
        muG = kb.sb("R_muG", [128, 1])
        with nc.allow_non_contiguous_dma(reason="tiny parameter vectors"):
            kb.dma("sp", muH, muH[:, :, 0], io["rw_mu"], io["rw_mu"].t[0:3072].rearrange("(a k) -> k a", k=64))
            kb.dma("sp", muW, muW[:, :, 0], io["rw_mu"], io["rw_mu"].t[3072:3200].rearrange("(a k) -> k a", k=64))
            kb.dma("sp", muG, muG[:, :], io["rw_mu"], io["rw_mu"].t[3200:3328].rearrange("(k o) -> k o", o=1))
        w0, a0, kkp, kap, rkp, gnw, gnb = (hd_param(kb, "R_" + n, io[n]) for n in
                                           ("rw_w0", "rw_a0", "rw_k_k", "rw_k_a", "rw_r_k", "rw_gn_w", "rw_gn_b"))
        Wup = kb.sb("R_Wup", [64, 1024], BF16)
        Aup = kb.sb("R_Aup", [64, 1024], BF16)
        Gup = kb.sb("R_Gup", [128, 1024], BF16)
        kb.dma("pool", Wup, Wup[:], io["rw_w_up"], io["rw_w_up"][:, :])
        kb.dma("pool", Aup, Aup[:], io["rw_a_up"], io["rw_a_up"][:, :])
        kb.dma("pool", Gup, Gup[:], io["rw_g_up"], io["rw_g_up"][:, :])
        inclU = kb.sb("R_inclU", [64, 64])
        strU = kb.sb("R_strU", [64, 64])
        strL = kb.sb("R_strL", [64, 64])
        for m, pat, cmul, cmp_ in ((inclU, 1, -1, ALU.is_ge), (strU, 1, -1, ALU.is_gt), (strL, -1, 1, ALU.is_gt)):
            kb.op("pool", lambda e: e.memset(m[:], 1.0), wr=[m])
            kb.op("pool", lambda e: e.affine_select(out=m[:], in_=m[:], pattern=[[pat, 64]], base=0,
                                                    channel_multiplier=cmul, compare_op=cmp_, fill=0.0), rd=[m], wr=[m])
        one64 = kb.sb("R_one64", [64, 64], BF16)
        kb.op("pool", lambda e: e.memset(one64[:], 1.0), wr=[one64])

        rawH = [kb.sb("R_rawH0", [64, 48, 65])] * 2
        rawW = [kb.sb(f"R_rawW{i}", [64, 2, 65]) for i in range(2)]
        rawG = [kb.sb(f"R_rawG{i}", [128, 65]) for i in range(2)]
        PH = kb.sb("R_PH", [64, 48, 64])
        PW = kb.sb("R_PW", [64, 2, 64])
        PG = kb.sb("R_PG", [128, 64])
        T16 = lambda nm: kb.sb("R_" + nm, [64, 16, 64])
        B16 = lambda nm: kb.sb("R_" + nm, [64, 16, 64], BF16)
        th = kb.sb("R_th", [64, 64], BF16)
        adb = kb.sb("R_adb", [64, 64], BF16)
        sgd = kb.sb("R_sgd", [128, 64], BF16)
        t1, lw, asig, gsb, kkr, rn, kk, kfin, bvec = (T16(n) for n in ("t1", "lw", "asig", "gsb", "kkr", "rn", "kk", "kfin", "bvec"))
        cs = [T16("cs0"), T16("cs1")]
        Gm, Gi = (T16(n) for n in ("Gm", "Gi"))
        Bh, Kh = lw, asig
        sq, rt, at, bt, kt = (B16(n) for n in ("sq", "rt", "at", "bt", "kt"))
        MT, Mq, LakT, RBT, RKT = (B16(n) for n in ("MT", "Mq", "LakT", "RBT", "RKT"))
        Pp, Qq, Rr = [B16("P0"), B16("P1")], [B16("Q0"), B16("Q1")], [B16("R0"), B16("R1")]
        Vtm, Bhtm, Khtm, Wsb, Usb, Zb, ysbb = (B16(n) for n in ("Vtm", "Bhtm", "Khtm", "Wsb", "Usb", "Zb", "ysbb"))
        Z = T16("Z")
        Sv = gsb
        Gp = kkr
        rk = sq
        ysb, yc = cs[0], cs[1]
        yn = Gi
        yout = [t1, t1]

        def wide(b0, rows, n, C):
            return PS.t[0:rows, b0 * 512:b0 * 512 + n * C].rearrange("p (j l) -> p j l", l=C)

        def wb(b0, ncols):
            return [bank[b0 + i] for i in range((ncols + 511) // 512)]

        def mm16(b0, rows, C_out, lfn, rfn, rd):
            for h in range(16):
                kb.mm(PS.t[0:rows, b0 * 512 + h * C_out:b0 * 512 + (h + 1) * C_out], lfn(h), rfn(h), rd=rd, wr=wb(b0, 16 * C_out))

        def mmones(b0, src, C):
            for q in range((16 * C + 511) // 512):
                hq = 512 // C if C * 16 > 512 else 16
                kb.mm(PS.t[0:64, b0 * 512 + q * 512:b0 * 512 + q * 512 + hq * C], one64[:], src[:, q * hq:(q + 1) * hq, 0:C],
                      rd=[one64, src], wr=wb(b0, 16 * C))

        units = [(64, 64 * ch, ch, None) for ch in range(SEQ // 64)] + [(LS, SEQ + LS * s, 0, s) for s in range(NS)]
        if DEBUG_UNITS is not None:
            units = [units[i] for i in DEBUG_UNITS]
        for ui, (C, t0, ch, s) in enumerate(units):
            rH, rW, rG = rawH[ui % 2], rawW[ui % 2], rawG[ui % 2]
            srcH = pT.t[0:3072, :].rearrange("(a k) t -> k a t", k=64)
            srcW = pT.t[3072:3200, :].rearrange("(a k) t -> k a t", k=64)
            srcG = pT.t[3200:3328, :]
            lo = 0 if (s is None and ch > 0) else 1
            kb.dma("sp", rH, rH[:, :, lo:1 + C], pT, srcH[:, :, t0 - 1 + lo:t0 + C])
            kb.dma("sp", rW, rW[:, :, lo:1 + C], pT, srcW[:, :, t0 - 1 + lo:t0 + C])
            kb.dma("sp", rG, rG[:, lo:1 + C], pT, srcG[:, t0 - 1 + lo:t0 + C])
            if lo == 1:
                if s is None:
                    kb.op("pool", lambda e: e.memset(rH[:, :, 0:1], 0.0), wr=[rH])
                    kb.op("pool", lambda e: e.memset(rW[:, :, 0:1], 0.0), wr=[rW])
                    kb.op("pool", lambda e: e.memset(rG[:, 0:1], 0.0), wr=[rG])
                else:
                    st = io["state_rwkv_shift"]
                    with nc.allow_non_contiguous_dma(reason="shift state transposing load (small)"):
                        kb.dma("sp", rH, rH[:, :, 0], st, st.t[s, 0:3072].rearrange("(a k) -> k a", k=64))
                        kb.dma("sp", rW, rW[:, :, 0], st, st.t[s, 3072:3200].rearrange("(a k) -> k a", k=64))
                        kb.dma("sp", rG, rG[:, 0:1], st, st.t[s, 3200:3328].rearrange("(k o) -> k o", o=1))
            yield
            if s is None and ch == 0:
                kb.op("pool", lambda e: e.memset(Z[:], 0.0), wr=[Z])
            elif s is not None:
                kb.dma("sp", Sv, Sv[:], io["state_rwkv"], io["state_rwkv"].t[s].rearrange("h v k -> v h k"))
                for h in range(16):
                    kb.tr(out=PS.t[0:64, 3072 + h * 64:3072 + (h + 1) * 64], in_=Sv[:, h, :], identity=ident[0:64, 0:64],
                          rd=[Sv, ident], wr=[bank[6], bank[7]])
                kb.op("act", lambda e: e.copy(out=Z[:], in_=wide(6, 64, 16, 64)), rd=[bank[6], bank[7]], wr=[Z])
            yield
            for raw_, P_, mu_, n in ((rH, PH, muH, 48), (rW, PW, muW, 2)):
                kb.op("dve", lambda e: e.tensor_tensor(out=P_[:, 0:n, 0:C], in0=raw_[:, 0:n, 0:C], in1=raw_[:, 0:n, 1:1 + C], op=ALU.subtract),
                      rd=[raw_], wr=[P_])
                kb.op("dve", lambda e: e.tensor_tensor(out=P_[:, 0:n, 0:C], in0=P_[:, 0:n, 0:C], in1=bc(mu_[:], [64, n, C]), op=ALU.mult),
                      rd=[P_, mu_], wr=[P_])
                kb.op("pool", lambda e: e.tensor_tensor(out=P_[:, 0:n, 0:C], in0=P_[:, 0:n, 0:C], in1=raw_[:, 0:n, 1:1 + C], op=ALU.add),
                      rd=[P_, raw_], wr=[P_])
            kb.op("dve", lambda e: e.tensor_tensor(out=PG[:, 0:C], in0=rG[:, 0:C], in1=rG[:, 1:1 + C], op=ALU.subtract), rd=[rG], wr=[PG])
            kb.op("dve", lambda e: e.tensor_scalar(out=PG[:, 0:C], in0=PG[:, 0:C], scalar1=muG[:, 0:1], scalar2=None, op0=ALU.mult),
                  rd=[PG, muG], wr=[PG])
            kb.op("dve", lambda e: e.tensor_tensor(out=PG[:, 0:C], in0=PG[:, 0:C], in1=rG[:, 1:1 + C], op=ALU.add), rd=[PG, rG], wr=[PG])
            Pr, Pk, Pvv = PH[:, 0:16, 0:C], PH[:, 16:32, 0:C], PH[:, 32:48, 0:C]
            kb.op("act", lambda e: e.copy(out=Zb[:], in_=Z[:]), rd=[Z], wr=[Zb])
            yield
            kb.op("act", lambda e: e.activation(out=th[:, 0:C], in_=PW[:, 0, 0:C], func=AF.Tanh), rd=[PW], wr=[th])
            kb.op("act", lambda e: e.activation(out=sgd[:, 0:C], in_=PG[:, 0:C], func=AF.Sigmoid), rd=[PG], wr=[sgd])
            mm16(0, 64, C, lambda h: Wup[:, h * 64:(h + 1) * 64], lambda h: th[:, 0:C], [Wup, th])
            kb.op("act", lambda e: e.copy(out=adb[:, 0:C], in_=PW[:, 1, 0:C]), rd=[PW], wr=[adb])
            mm16(2, 64, C, lambda h: Aup[:, h * 64:(h + 1) * 64], lambda h: adb[:, 0:C], [Aup, adb])
            mm16(4, 64, C, lambda h: Gup[:, h * 64:(h + 1) * 64], lambda h: sgd[:, 0:C], [Gup, sgd])
            V8 = lambda t: t[:, :, 0:C]
            kb.op("dve", lambda e: e.tensor_tensor(out=V8(t1), in0=wide(0, 64, 16, C), in1=bc(w0[:], [64, 16, C]), op=ALU.add),
                  rd=wb(0, 16 * C) + [w0], wr=[t1])
            kb.op("act", lambda e: e.activation(out=V8(lw), in_=V8(t1), func=AF.Sigmoid), rd=[t1], wr=[lw])
            kb.op("dve", lambda e: e.tensor_scalar(out=V8(lw), in0=V8(lw), scalar1=-0.6065306597126334, scalar2=None, op0=ALU.mult),
                  rd=[lw], wr=[lw])
            kb.op("dve", lambda e: e.tensor_tensor(out=V8(t1), in0=wide(2, 64, 16, C), in1=bc(a0[:], [64, 16, C]), op=ALU.add),
                  rd=wb(2, 16 * C) + [a0], wr=[t1])
            kb.op("act", lambda e: e.activation(out=V8(asig), in_=V8(t1), func=AF.Sigmoid), rd=[t1], wr=[asig])
            kb.op("act", lambda e: e.copy(out=V8(gsb), in_=wide(4, 64, 16, C)), rd=wb(4, 16 * C), wr=[gsb])
            yield
            kb.op("dve", lambda e: e.tensor_tensor(out=V8(kkr), in0=Pk, in1=bc(kkp[:], [64, 16, C]), op=ALU.mult), rd=[PH, kkp], wr=[kkr])
            kb.op("act", lambda e: e.activation(out=V8(sq), in_=V8(kkr), func=AF.Square), rd=[kkr], wr=[sq])
            mmones(6, sq, C)
            kb.op("dve", lambda e: e.tensor_scalar(out=V8(rn), in0=wide(6, 64, 16, C), scalar1=1e-6, scalar2=None, op0=ALU.add),
                  rd=wb(6, 16 * C), wr=[rn])
            kb.op("act", lambda e: e.activation(out=V8(rn), in_=V8(rn), func=AF.Sqrt), rd=[rn], wr=[rn])
            kb.op("dve", lambda e: e.reciprocal(out=V8(rn), in_=V8(rn)), rd=[rn], wr=[rn])
            kb.op("dve", lambda e: e.tensor_tensor(out=V8(kk), in0=V8(kkr), in1=V8(rn), op=ALU.mult), rd=[kkr, rn], wr=[kk])
            yield
            kb.op("dve", lambda e: e.tensor_scalar(out=V8(t1), in0=V8(asig), scalar1=-1.0, scalar2=None, op0=ALU.add), rd=[asig], wr=[t1])
            kb.op("dve", lambda e: e.tensor_tensor(out=V8(t1), in0=V8(t1), in1=bc(kap[:], [64, 16, C]), op=ALU.mult), rd=[t1, kap], wr=[t1])
            kb.op("dve", lambda e: e.tensor_scalar(out=V8(t1), in0=V8(t1), scalar1=1.0, scalar2=None, op0=ALU.add), rd=[t1], wr=[t1])
            kb.op("dve", lambda e: e.tensor_tensor(out=V8(kfin), in0=Pk, in1=V8(t1), op=ALU.mult), rd=[PH, t1], wr=[kfin])
            kb.op("pool", lambda e: e.tensor_tensor(out=V8(bvec), in0=V8(kk), in1=V8(asig), op=ALU.mult), rd=[kk, asig], wr=[bvec])
            yield
            cur = lw
            sft, k2 = 1, 0
            while sft < C:
                nx = cs[k2 % 2]
                kb.op("pool", lambda e: e.tensor_copy(out=nx[:, :, 0:sft], in_=cur[:, :, 0:sft]), rd=[cur], wr=[nx])
                kb.op("dve", lambda e: e.tensor_tensor(out=nx[:, :, sft:C], in0=cur[:, :, sft:C], in1=cur[:, :, 0:C - sft], op=ALU.add),
                      rd=[cur], wr=[nx])
                cur = nx
                sft *= 2
                k2 += 1
            csum = cur
            kb.op("act", lambda e: e.activation(out=V8(Gm), in_=V8(csum), func=AF.Exp), rd=[csum], wr=[Gm])
            kb.op("act", lambda e: e.activation(out=V8(Gi), in_=V8(csum), func=AF.Exp, scale=-1.0), rd=[csum], wr=[Gi])
            kb.op("dve", lambda e: e.tensor_tensor(out=V8(t1), in0=V8(csum), in1=V8(lw), op=ALU.subtract), rd=[csum, lw], wr=[t1])
            kb.op("act", lambda e: e.activation(out=V8(Gp), in_=V8(t1), func=AF.Exp), rd=[t1], wr=[Gp])
            kb.op("dve", lambda e: e.tensor_tensor(out=V8(rt), in0=Pr, in1=V8(Gm), op=ALU.mult), rd=[PH, Gm], wr=[rt])
            kb.op("dve", lambda e: e.scalar_tensor_tensor(out=V8(at), in0=V8(kk), scalar=-1.0, in1=V8(Gp), op0=ALU.mult, op1=ALU.mult),
                  rd=[kk, Gp], wr=[at])
            kb.op("pool", lambda e: e.tensor_tensor(out=V8(bt), in0=V8(bvec), in1=V8(Gi), op=ALU.mult), rd=[bvec, Gi], wr=[bt])
            kb.op("pool", lambda e: e.tensor_tensor(out=V8(kt), in0=V8(kfin), in1=V8(Gi), op=ALU.mult), rd=[kfin, Gi], wr=[kt])
            GCb = lambda n: bc(Gm[:, :, C - 1:C], [64, 16, n])
            kb.op("dve", lambda e: e.tensor_tensor(out=V8(Bh), in0=V8(bt), in1=GCb(C), op=ALU.mult), rd=[bt, Gm], wr=[Bh])
            kb.op("dve", lambda e: e.tensor_tensor(out=V8(Kh), in0=V8(kt), in1=GCb(C), op=ALU.mult), rd=[kt, Gm], wr=[Kh])
            yield
            for dst, sfn, srcb, b0 in ((Vtm, lambda h: PH[:, 32 + h, 0:C], PH, 0), (Bhtm, lambda h: Bh[:, h, 0:C], Bh, 2),
                                       (Khtm, lambda h: Kh[:, h, 0:C], Kh, 4)):
                for h in range(16):
                    kb.tr(out=PS.t[0:C, b0 * 512 + h * 64:b0 * 512 + (h + 1) * 64], in_=sfn(h), identity=ident[0:64, 0:64],
                          rd=[srcb, ident], wr=wb(b0, 1024))
                kb.op("act" if b0 == 2 else "dve",
                      (lambda e: e.copy(out=dst[0:C], in_=wide(b0, C, 16, 64))) if b0 == 2 else
                      (lambda e: e.tensor_copy(out=dst[0:C], in_=wide(b0, C, 16, 64))), rd=wb(b0, 1024), wr=[dst])
            yield
            kinds = ((MT, bt, at, strU), (Mq, at, bt, strL), (LakT, kt, at, strU), (RBT, bt, rt, inclU), (RKT, kt, rt, inclU))
            for ki, (dst, lt, rt_, msk) in enumerate(kinds):
                b0 = (ki % 4) * 2
                mm16(b0, C, C, lambda h: lt[:, h, 0:C], lambda h: rt_[:, h, 0:C], [lt, rt_])
                kb.op("dve", lambda e: e.tensor_tensor(out=dst[0:C, :, 0:C], in0=wide(b0, C, 16, C),
                                                       in1=bc(msk[0:C, None, 0:C], [C, 16, C]), op=ALU.mult),
                      rd=wb(b0, 16 * C) + [msk], wr=[dst])
            yield
            Rc, Pc, Qc = Rr[0], MT, Mq
            kb.op("dve", lambda e: e.tensor_tensor(out=Rc[0:C, :, 0:C], in0=MT[0:C, :, 0:C], in1=bc(ident[0:C, None, 0:C], [C, 16, C]),
                                                   op=ALU.add), rd=[MT, ident], wr=[Rc])
            levels = {64: 5, 4: 1}[C]
            for lvl in range(levels):
                last = lvl == levels - 1
                Qn, Pn, Rn = Qq[lvl % 2], Pp[lvl % 2], Rr[(lvl + 1) % 2]
                mm16(0, C, C, lambda h: Pc[0:C, h, 0:C], lambda h: Qc[0:C, h, 0:C], [Pc, Qc])
                if not last:
                    mm16(2, C, C, lambda h: Qc[0:C, h, 0:C], lambda h: Pc[0:C, h, 0:C], [Pc, Qc])
                kb.op("act", lambda e: e.copy(out=Qn[0:C, :, 0:C], in_=wide(0, C, 16, C)), rd=wb(0, 16 * C), wr=[Qn])
                if not last:
                    kb.op("dve", lambda e: e.tensor_copy(out=Pn[0:C, :, 0:C], in_=wide(2, C, 16, C)), rd=wb(2, 16 * C), wr=[Pn])
                mm16(4, C, C, lambda h: Qn[0:C, h, 0:C], lambda h: Rc[0:C, h, 0:C], [Qn, Rc])
                kb.op("dve", lambda e: e.tensor_tensor(out=Rn[0:C, :, 0:C], in0=Rc[0:C, :, 0:C], in1=wide(4, C, 16, C), op=ALU.add),
                      rd=[Rc] + wb(4, 16 * C), wr=[Rn])
                Rc, Pc, Qc = Rn, Pn, Qn
            yield
            for h in range(16):
                o = PS.t[0:C, 3072 + h * 64:3072 + (h + 1) * 64]
                kb.mm(o, at[:, h, 0:C], Zb[:, h, :], start=True, stop=False, rd=[at, Zb], wr=[bank[6], bank[7]])
                kb.mm(o, LakT[0:C, h, 0:C], Vtm[0:C, h, :], start=False, stop=True, rd=[LakT, Vtm], wr=[bank[6], bank[7]])
            kb.op("act", lambda e: e.copy(out=Wsb[0:C], in_=wide(6, C, 16, 64)), rd=[bank[6], bank[7]], wr=[Wsb])
            mm16(0, C, 64, lambda h: Rc[0:C, h, 0:C], lambda h: Wsb[0:C, h, :], [Rc, Wsb])
            kb.op("dve", lambda e: e.tensor_copy(out=Usb[0:C], in_=wide(0, C, 16, 64)), rd=[bank[0], bank[1]], wr=[Usb])
            yield
            for h in range(16):
                o = PS.t[0:64, 1024 + h * C:1024 + (h + 1) * C]
                kb.mm(o, Zb[:, h, :], rt[:, h, 0:C], start=True, stop=False, rd=[Zb, rt], wr=wb(2, 16 * C))
                kb.mm(o, Usb[0:C, h, :], RBT[0:C, h, 0:C], start=False, stop=False, rd=[Usb, RBT], wr=wb(2, 16 * C))
                kb.mm(o, Vtm[0:C, h, :], RKT[0:C, h, 0:C], start=False, stop=True, rd=[Vtm, RKT], wr=wb(2, 16 * C))
            for h in range(16):
                o = PS.t[0:64, 2048 + h * 64:2048 + (h + 1) * 64]
                kb.mm(o, Bhtm[0:C, h, :], Usb[0:C, h, :], start=True, stop=False, rd=[Bhtm, Usb], wr=[bank[4], bank[5]])
                kb.mm(o, Khtm[0:C, h, :], Vtm[0:C, h, :], start=False, stop=True, rd=[Khtm, Vtm], wr=[bank[4], bank[5]])
            kb.op("act", lambda e: e.copy(out=V8(ysb), in_=wide(2, 64, 16, C)), rd=wb(2, 16 * C), wr=[ysb])
            kb.op("dve", lambda e: e.tensor_tensor(out=Z[:], in0=Z[:], in1=GCb(64), op=ALU.mult), rd=[Z, Gm], wr=[Z])
            kb.op("dve", lambda e: e.tensor_tensor(out=Z[:], in0=Z[:], in1=wide(4, 64, 16, 64), op=ALU.add), rd=[Z, bank[4], bank[5]], wr=[Z])
            yield
            kb.op("dve", lambda e: e.tensor_copy(out=V8(ysbb), in_=V8(ysb)), rd=[ysb], wr=[ysbb])
            mmones(6, ysbb, C)
            kb.op("dve", lambda e: e.scalar_tensor_tensor(out=V8(yc), in0=wide(6, 64, 16, C), scalar=-1.0 / 64, in1=V8(ysb), op0=ALU.mult, op1=ALU.add),
                  rd=wb(6, 16 * C) + [ysb], wr=[yc])
            kb.op("act", lambda e: e.activation(out=V8(sq), in_=V8(yc), func=AF.Square), rd=[yc], wr=[sq])
            mmones(0, sq, C)
            kb.op("dve", lambda e: e.tensor_scalar(out=V8(rn), in0=wide(0, 64, 16, C), scalar1=1.0 / 64, scalar2=6.4e-4, op0=ALU.mult, op1=ALU.add),
                  rd=wb(0, 16 * C), wr=[rn])
            kb.op("act", lambda e: e.activation(out=V8(rn), in_=V8(rn), func=AF.Sqrt), rd=[rn], wr=[rn])
            kb.op("dve", lambda e: e.reciprocal(out=V8(rn), in_=V8(rn)), rd=[rn], wr=[rn])
            kb.op("dve", lambda e: e.tensor_tensor(out=V8(yn), in0=V8(yc), in1=V8(rn), op=ALU.mult), rd=[yc, rn], wr=[yn])
            kb.op("dve", lambda e: e.tensor_tensor(out=V8(yn), in0=V8(yn), in1=bc(gnw[:], [64, 16, C]), op=ALU.mult), rd=[yn, gnw], wr=[yn])
            kb.op("dve", lambda e: e.tensor_tensor(out=V8(yn), in0=V8(yn), in1=bc(gnb[:], [64, 16, C]), op=ALU.add), rd=[yn, gnb], wr=[yn])
            kb.op("pool", lambda e: e.tensor_tensor(out=V8(rk), in0=Pr, in1=V8(kfin), op=ALU.mult), rd=[PH, kfin], wr=[rk])
            kb.op("pool", lambda e: e.tensor_tensor(out=V8(rk), in0=V8(rk), in1=bc(rkp[:], [64, 16, C]), op=ALU.mult), rd=[rk, rkp], wr=[rk])
            mmones(2, rk, C)
            kb.op("dve", lambda e: e.tensor_tensor(out=V8(yc), in0=wide(2, 64, 16, C), in1=Pvv, op=ALU.mult), rd=wb(2, 16 * C) + [PH], wr=[yc])
            kb.op("dve", lambda e: e.tensor_tensor(out=V8(yn), in0=V8(yn), in1=V8(yc), op=ALU.add), rd=[yn, yc], wr=[yn])
            yo = yout[ui % 2]
            kb.op("dve", lambda e: e.tensor_tensor(out=V8(yo), in0=V8(yn), in1=V8(gsb), op=ALU.mult), rd=[yn, gsb], wr=[yo])
            kb.dma("pool", ymix, ymix.t[0:1024, :].rearrange("(h v) t -> v h t", v=64)[:, :, t0:t0 + C], yo, V8(yo))
            yield
            if s is not None or ch == SEQ // 64 - 1 or DEBUG_UNITS is not None:
                for h in range(16):
                    kb.tr(out=PS.t[0:64, 2048 + h * 64:2048 + (h + 1) * 64], in_=Z[:, h, :], identity=ident[0:64, 0:64],
                          rd=[Z, ident], wr=[bank[4], bank[5]])
                kb.op("act", lambda e: e.copy(out=Sv[:], in_=wide(4, 64, 16, 64)), rd=[bank[4], bank[5]], wr=[Sv])
                row = 0 if s is None else 1 + s
                kb.dma("pool", io["o_rw"], io["o_rw"].t[row].rearrange("h v k -> v h k"), Sv, Sv[:], is_output=True)
DFF = 5632
GDN_PROJ = 8224
H2 = TT // 2


def proj_cm(kb, c, W, KC, col0, ncols, rhsT, wt, epi, state, slabw=256, tail=None):
    psb = c["psb"]
    Wv = W.t if isinstance(W, Buf) else W
    for s0 in range(0, ncols, slabw):
        nsl = min(slabw, ncols - s0)
        wb = wt[state["n"] % len(wt)]
        state["n"] += 1
        kb.dma("pool", wb, wb[:, 0:KC, 0:nsl], state["src"],
               Wv[:, col0 + s0:col0 + s0 + nsl].rearrange("(c p) n -> p c n", p=128))
        for j in range((nsl + 127) // 128):
            m = min(128, nsl - j * 128)
            blk = (s0 + j * 128) // 128
            for hh in range(2):
                pb = psb[2 + state["pb"] % 4]
                state["pb"] += 1
                for kc in range(KC):
                    kb.mm(pb[0:m, 0:H2], wb[:, kc, j * 128:j * 128 + m], rhsT[:, kc, hh * H2:(hh + 1) * H2],
                          start=(kc == 0), stop=(kc == KC - 1), rd=[wb, rhsT], wr=[pb])
                epi(blk, m, hh, pb)
        if tail is not None:
            tail(s0, nsl, wb)


def stage_C(kb, c, io, l):
    nc = kb.nc
    PS, bank, ident, psb = c["PS"], c["bank"], c["ident"], c["psb"]
    hT_d = io["hT"]
    w_out = io["w_out_ab"] if l == 0 else io["w_out_c"]
    SC = 1.0 / (128.0 ** 0.5)
    with kb.scope():
        g_xa = load_gain(kb, "C_gxa", io["norm_xa"], l)
        g_ffn = load_gain(kb, "C_gffn", io["norm_ffn"], l)
        if l == 0:
            g_nxt = load_gain(kb, "C_gnxt", io["norm_mix"], 1)
        else:
            g_nxt = kb.sb("C_gfin", [128, 16], F32)
            with nc.allow_non_contiguous_dma(reason="tiny gain vectors"):
                kb.dma("sp", g_nxt, g_nxt[:], io["norm_final"], io["norm_final"].t.rearrange("(c p) -> p c", p=128))
        cw = kb.sb("C_cw", [128, 44, 3])
        cbi = kb.sb("C_cb", [128, 44])
        with nc.allow_non_contiguous_dma(reason="tiny parameter vectors"):
            for i in range(3):
                kb.dma("sp", cw, cw[:, :, i], io["ffn_conv_w"], io["ffn_conv_w"].t[l, i].rearrange("(c p) -> p c", p=128))
            kb.dma("sp", cbi, cbi[:], io["ffn_conv_b"], io["ffn_conv_b"].t[l].rearrange("(c p) -> p c", p=128))
        shalo = kb.sb("C_shalo", [128, 44, 2 * NS], F32)
        with kb.scope():
            stt = kb.sb("C_stt", [2 * NS, DFF], F32)
            kb.dma("sp", stt, stt[:], io["state_ffn_conv"], io["state_ffn_conv"].t[l].rearrange("s r c -> (s r) c"))
            for j in range(44):
                pb = psb[j % 4]
                kb.tr(out=pb[:, 0:2 * NS], in_=stt[:, j * 128:(j + 1) * 128], identity=ident[0:2 * NS, 0:2 * NS], rd=[stt, ident], wr=[pb])
                kb.op("act" if j % 2 else "dve",
                      (lambda e: e.copy(out=shalo[:, j, :], in_=pb[:, 0:2 * NS])) if j % 2 else
                      (lambda e: e.tensor_copy(out=shalo[:, j, :], in_=pb[:, 0:2 * NS])), rd=[pb], wr=[shalo])

        hT = kb.sb("C_hT", [128, 16, TT], F32)
        aT = kb.sb("C_aT", [128, 16, TT], BF16)
        hid = kb.sb("C_hid", [128, 22, TT], BF16)
        wt = [kb.sb(f"C_wt{i}", [128, 16, 256], BF16) for i in range(3)]
        wo = [kb.sb(f"C_wo{i}", [128, 22, 128], BF16) for i in range(3 if l == 0 else 2)]
        sq = [kb.sb(f"C_sq{i}", [128, TT], BF16) for i in range(2)]
        rstd = kb.sb("C_rstd", [128, TT], F32)
        qT = kb.sb("C_qT", [128, 4, TT], BF16)
        qS = kb.sb("C_qS", [128, 4, TS], F32)
        oT = kb.sb("C_oT", [128, 4, TT], BF16)
        G4 = kb.sb("C_G4", [128, 2, TT], F32)
        gext = kb.sb("C_gext", [128, 2 + TT], F32)
        gacc = kb.sb("C_gacc", [128, TT], F32)
        halo = kb.sb("C_halo", [128, 44, 2], F32)
        sext = kb.sb("C_sext", [128, NS, 6], F32)
        sacc = kb.sb("C_sacc", [128, NS, 4], F32)
        ptl = [kb.sb(f"C_ptl{i}", [128, 256], F32) for i in range(2)]
        ytok = [kb.sb("C_ytok0", [128, D], F32)] * 2 if l == 1 else None
        E4 = kb.sb("C_E4", [128, 4, NMEM], F32)
        PT4 = kb.sb("C_PT4", [128, 8, 128], BF16)
        st4 = kb.sb("C_st4", [128, 16], F32)
        Kc = kb.sb("C_Kc", [128, 2, XA], F32)
        Vc = kb.sb("C_Vc", [128, 2, XA], F32)
        KT = kb.sb("C_KT", [128, 4, NMEM], F32)
        ms = kb.sb("C_ms", [4, 8], F32)
        PTs = kb.sb("C_PTs", [128, 8, 4], F32)
        oS = kb.sb("C_oS", [128, 4, TS], F32)
        kb.op("pool", lambda e: e.memset(halo[:], 0.0), wr=[halo])
        pst = {"n": 0, "pb": 0, "src": None}

        def add_into_h(blk, m, hh, pb):
            kb.op("dve" if hh else "pool" if False else "dve",
                  lambda e: e.tensor_tensor(out=hT[0:m, blk, hh * H2:(hh + 1) * H2], in0=hT[0:m, blk, hh * H2:(hh + 1) * H2],
                                            in1=pb[0:m, 0:H2], op=ALU.add), rd=[hT, pb], wr=[hT])

        for tt in range(NTT if DEBUG_UNITS is None else 1):
            if DEBUG_UNITS is not None:
                tt = NTT - 1
            c0 = tt * TT
            last = tt == NTT - 1
            kb.dma("sp", hT, hT[:], hT_d, hT_d.t.rearrange("(c p) t -> p c t", p=128)[:, :, c0:c0 + TT])
            kb.dma("pool", aT, aT[:], io["ymix"], io["ymix"].t.rearrange("(c p) t -> p c t", p=128)[:, :, c0:c0 + TT])
            pst["src"] = w_out
            proj_cm(kb, c, w_out, 16, 0, D, aT, wt, add_into_h, pst)
            rmsnorm_cm(kb, c, hT, g_xa, aT, sq, rstd)
            pst["src"] = io["w_xq"]

            def q_epi(blk, m, hh, pb):
                kb.op("act", lambda e: e.copy(out=qT[:, blk, hh * H2:(hh + 1) * H2], in_=pb[:, 0:H2]), rd=[pb], wr=[qT])
                if last and hh == 1:
                    kb.op("dve", lambda e: e.tensor_copy(out=qS[:, blk, :], in_=pb[:, H2 - TS:H2]), rd=[pb], wr=[qS])
            proj_cm(kb, c, io["w_xq"].t[l], 16, 0, XA, aT, wt, q_epi, pst)
            npt = TT - TS if last else TT
            for b0 in range(0, npt, 128):
                nb = min(128, npt - b0)
                for h in range(4):
                    kb.mm(bank[6 + h // 2][0:nb, (h % 2) * NMEM:(h % 2 + 1) * NMEM], qT[:, h, b0:b0 + nb], c["mkT"][:, l, h, :],
                          rd=[qT, c["mkT"]], wr=[bank[6 + h // 2]])
                for hp in range(2):
                    kb.op("dve", lambda e: e.tensor_reduce(out=st4[0:nb, 2 * hp:2 * hp + 2],
                                                           in_=bank[6 + hp][0:nb, :].rearrange("p (h m) -> p h m", m=NMEM), axis=AX.X, op=ALU.max),
                          rd=[bank[6 + hp]], wr=[st4])
                kb.op("dve", lambda e: e.tensor_scalar(out=st4[0:nb, 4:8], in0=st4[0:nb, 0:4], scalar1=-SC, scalar2=None, op0=ALU.mult),
                      rd=[st4], wr=[st4])
                for h in range(4):
                    kb.op("act", lambda e: e.activation(out=E4[0:nb, h, :], in_=bank[6 + h // 2][0:nb, (h % 2) * NMEM:(h % 2 + 1) * NMEM],
                                                        func=AF.Exp, bias=st4[0:nb, 4 + h:5 + h], scale=SC, accum_out=st4[0:nb, 8 + h:9 + h]),
                          rd=[bank[6 + h // 2], st4], wr=[E4, st4])
                kb.op("dve", lambda e: e.reciprocal(out=st4[0:nb, 12:16], in_=st4[0:nb, 8:12]), rd=[st4], wr=[st4])
                kb.op("dve", lambda e: e.tensor_tensor(out=E4[0:nb], in0=E4[0:nb], in1=bc(st4[0:nb, 12:16, None], [nb, 4, NMEM]), op=ALU.mult),
                      rd=[E4, st4], wr=[E4])
                for h in range(4):
                    for mb in range(2):
                        kb.tr(out=bank[2 + h // 2][:, ((h % 2) * 2 + mb) * 128:((h % 2) * 2 + mb) * 128 + nb],
                              in_=E4[0:nb, h, mb * 128:(mb + 1) * 128], identity=ident[0:nb, 0:nb], rd=[E4, ident], wr=[bank[2 + h // 2]])
                for hp in range(2):
                    kb.op("act" if hp else "dve",
                          (lambda e: e.copy(out=PT4[:, 4 * hp:4 * hp + 4, 0:nb], in_=bank[2 + hp][:, :].rearrange("p (a b) -> p a b", b=128)[:, :, 0:nb]))
                          if hp else
                          (lambda e: e.tensor_copy(out=PT4[:, 4 * hp:4 * hp + 4, 0:nb], in_=bank[2 + hp][:, :].rearrange("p (a b) -> p a b", b=128)[:, :, 0:nb])),
                          rd=[bank[2 + hp]], wr=[PT4])
                for h in range(4):
                    for mb in range(2):
                        kb.mm(bank[4][:, h * 128:h * 128 + nb], c["mvB"][:, l, mb, h * 128:(h + 1) * 128], PT4[:, h * 2 + mb, 0:nb],
                              start=(mb == 0), stop=(mb == 1), rd=[c["mvB"], PT4], wr=[bank[4]])
                kb.op("act", lambda e: e.copy(out=oT[:, :, b0:b0 + nb], in_=bank[4][:, :].rearrange("p (h t) -> p h t", t=128)[:, :, 0:nb]),
                      rd=[bank[4]], wr=[oT])
            if last:
                Es = E4.t[0:LS]
                for s in range(NS):
                    kb.dma("sp", Kc, Kc[:], io["cache_mem_k"], io["cache_mem_k"].t[l, s].rearrange("(mb m) h d -> m mb (h d)", m=128))
                    kb.dma("sp", Vc, Vc[:], io["cache_mem_v"], io["cache_mem_v"].t[l, s].rearrange("(mb m) h d -> m mb (h d)", m=128))
                    for h in range(4):
                        pb = bank[h % 2]
                        for mb in range(2):
                            kb.tr(out=pb[:, mb * 128:(mb + 1) * 128], in_=Kc[:, mb, h * 128:(h + 1) * 128], identity=ident[:],
                                  rd=[Kc, ident], wr=[pb])
                        kb.op("act" if h % 2 else "dve",
                              (lambda e: e.copy(out=KT[:, h, :], in_=pb[:, 0:NMEM])) if h % 2 else
                              (lambda e: e.tensor_copy(out=KT[:, h, :], in_=pb[:, 0:NMEM])), rd=[pb], wr=[KT])
                    for h in range(4):
                        kb.mm(PS.t[0:LS, 3072 + h * NMEM:3072 + (h + 1) * NMEM], qS[:, h, s * LS:(s + 1) * LS], KT[:, h, :],
                              rd=[qS, KT], wr=[bank[6], bank[7]])
                    Sv4 = PS.t[0:LS, 3072:4096].rearrange("p (h m) -> p h m", m=NMEM)
                    kb.op("dve", lambda e: e.tensor_reduce(out=ms[:, 0:4], in_=Sv4, axis=AX.X, op=ALU.max), rd=[bank[6], bank[7]], wr=[ms])
                    kb.op("dve", lambda e: e.tensor_tensor(out=Es[:], in0=Sv4, in1=bc(ms[:, 0:4, None], [LS, 4, NMEM]), op=ALU.subtract),
                          rd=[bank[6], bank[7], ms], wr=[E4])
                    kb.op("act", lambda e: e.activation(out=Es[:], in_=Es[:], func=AF.Exp, scale=SC), rd=[E4], wr=[E4])
                    kb.op("dve", lambda e: e.tensor_reduce(out=ms[:, 4:8], in_=Es[:], axis=AX.X, op=ALU.add), rd=[E4], wr=[ms])
                    kb.op("dve", lambda e: e.reciprocal(out=ms[:, 4:8], in_=ms[:, 4:8]), rd=[ms], wr=[ms])
                    kb.op("dve", lambda e: e.tensor_tensor(out=Es[:], in0=Es[:], in1=bc(ms[:, 4:8, None], [LS, 4, NMEM]), op=ALU.mult),
                          rd=[E4, ms], wr=[E4])
                    for h in range(4):
                        for mb in range(2):
                            kb.tr(out=bank[2][:, (h * 2 + mb) * LS:(h * 2 + mb + 1) * LS], in_=Es[:, h, mb * 128:(mb + 1) * 128],
                                  identity=ident[0:LS, 0:LS], rd=[E4, ident], wr=[bank[2]])
                    kb.op("act", lambda e: e.copy(out=PTs[:], in_=bank[2][:, 0:8 * LS].rearrange("p (a b) -> p a b", b=LS)), rd=[bank[2]], wr=[PTs])
                    for h in range(4):
                        for mb in range(2):
                            kb.mm(bank[3][:, h * LS:(h + 1) * LS], Vc[:, mb, h * 128:(h + 1) * 128], PTs[:, h * 2 + mb, :],
                                  start=(mb == 0), stop=(mb == 1), rd=[Vc, PTs], wr=[bank[3]])
                    kb.op("dve", lambda e: e.tensor_copy(out=oT[:, :, npt + s * LS:npt + (s + 1) * LS],
                                                         in_=bank[3][:, 0:4 * LS].rearrange("p (h t) -> p h t", t=LS)), rd=[bank[3]], wr=[oT])
            pst["src"] = io["w_xo"]
            proj_cm(kb, c, io["w_xo"].t[l], 4, 0, D, oT, wt, add_into_h, pst)
            rmsnorm_cm(kb, c, hT, g_ffn, aT, sq, rstd)
            pst["src"] = io["ffn_w_in"]
            Win = io["ffn_w_in"].t[l]
            for half in range(2):
                for jp in range(0, 22, 2):
                    j0 = half * 22 + jp

                    def gate_epi(blk, m, hh, pb, j0=j0):
                        j = j0 + blk
                        kb.op("act", lambda e: e.copy(out=gext[:, 2 + hh * H2:2 + (hh + 1) * H2], in_=pb[:, 0:H2]), rd=[pb], wr=[gext])
                        if hh == 1:
                            kb.op("act", lambda e: e.copy(out=gext[:, 0:2], in_=halo[:, j, :]), rd=[halo], wr=[gext])
                            kb.op("dve", lambda e: e.tensor_scalar(out=gacc[:], in0=gext[:, 0:TT], scalar1=cw[:, j, 0:1], scalar2=None, op0=ALU.mult),
                                  rd=[gext, cw], wr=[gacc])
                            for i in (1, 2):
                                kb.op("dve", lambda e: e.scalar_tensor_tensor(out=gacc[:], in0=gext[:, i:i + TT], scalar=cw[:, j, i:i + 1],
                                                                              in1=gacc[:], op0=ALU.mult, op1=ALU.add), rd=[gext, cw, gacc], wr=[gacc])
                            if last:
                                kb.op("act", lambda e: e.copy(out=sext[:, :, 0:2], in_=shalo[:, j, :].rearrange("p (s r) -> p s r", r=2)),
                                      rd=[shalo], wr=[sext])
                                kb.op("act", lambda e: e.copy(out=sext[:, :, 2:6],
                                                                in_=gext[:, 2 + npt:2 + TT].rearrange("p (s t) -> p s t", t=LS)),
                                      rd=[gext], wr=[sext])
                                kb.op("dve", lambda e: e.tensor_scalar(out=sacc[:], in0=sext[:, :, 0:4], scalar1=cw[:, j, 0:1], scalar2=None, op0=ALU.mult),
                                      rd=[sext, cw], wr=[sacc])
                                for i in (1, 2):
                                    kb.op("dve", lambda e: e.scalar_tensor_tensor(out=sacc[:], in0=sext[:, :, i:i + 4], scalar=cw[:, j, i:i + 1],
                                                                                  in1=sacc[:], op0=ALU.mult, op1=ALU.add), rd=[sext, cw, sacc], wr=[sacc])
                                kb.op("act", lambda e: e.copy(out=gacc[:, npt:TT].rearrange("p (s t) -> p s t", t=LS), in_=sacc[:]),
                                      rd=[sacc], wr=[gacc])
                            else:
                                kb.op("act", lambda e: e.copy(out=halo[:, j, :], in_=gext[:, TT:TT + 2]), rd=[gext], wr=[halo])
                            kb.op("act", lambda e: e.activation(out=G4[:, blk, :], in_=gacc[:], func=AF.Silu, bias=cbi[:, j:j + 1]),
                                  rd=[gacc, cbi], wr=[G4])

                    def gate_tail(s0, nsl, wb, j0=j0):
                        if not last:
                            return
                        pb = bank[6 + (j0 // 2) % 2]
                        for kc in range(16):
                            kb.mm(pb[:, 0:nsl], aT[:, kc, TT - 128:TT], wb[:, kc, 0:nsl], start=(kc == 0), stop=(kc == 15), rd=[wb, aT], wr=[pb])
                        pt = ptl[(j0 // 2) % 2]
                        kb.op("act", lambda e: e.copy(out=pt[:, 0:nsl], in_=pb[:, 0:nsl]), rd=[pb], wr=[pt])
                        cl = j0 * 128
                        o = io["o_ffn"]
                        kb.dma("sp", o, o.t[l, 0, :, cl:cl + nsl], pt, pt[62:64, 0:nsl], is_output=True)
                        for r in range(2):
                            kb.dma("sp", o, o.t[l, 1:1 + NS, r, cl:cl + nsl], pt, pt[64 + 2 + r:128:4, 0:nsl], is_output=True)
                    proj_cm(kb, c, Win, 16, j0 * 128, 256, aT, wt, gate_epi, pst, tail=gate_tail)

                    def up_epi(blk, m, hh, pb, jp=jp):
                        kb.op("dve", lambda e: e.tensor_tensor(out=hid[:, jp + blk, hh * H2:(hh + 1) * H2], in0=G4[:, blk, hh * H2:(hh + 1) * H2],
                                                               in1=pb[:, 0:H2], op=ALU.mult), rd=[G4, pb], wr=[hid])
                    proj_cm(kb, c, Win, 16, DFF + j0 * 128, 256, aT, wt, up_epi, pst)
                pst["src"] = io["ffn_w_out"]
                proj_cm(kb, c, io["ffn_w_out"].t[l, half * 22 * 128:(half + 1) * 22 * 128, :], 22, 0, D, hid, wo, add_into_h, pst, slabw=128)
                pst["src"] = io["ffn_w_in"]
            if l == 1:
                rmsnorm_cm(kb, c, hT, g_nxt, None, sq, rstd)
                for b0 in range(0, TT, 128):
                    nb = min(128, TT - b0)
                    yt = ytok[(b0 // 128) % 2]
                    for g4 in range(4):
                        pb = psb[2 + g4]
                        for k in range(4):
                            cc = g4 * 4 + k
                            kb.tr(out=pb[0:nb, k * 128:(k + 1) * 128], in_=hT[:, cc, b0:b0 + nb], identity=ident[:], rd=[hT, ident], wr=[pb])
                        kb.op("act" if g4 % 2 else "dve",
                              (lambda e: e.copy(out=yt[0:nb, g4 * 512:(g4 + 1) * 512], in_=pb[0:nb, :])) if g4 % 2 else
                              (lambda e: e.tensor_copy(out=yt[0:nb, g4 * 512:(g4 + 1) * 512], in_=pb[0:nb, :])), rd=[pb], wr=[yt])
                    kb.dma("sp", io["o_y"], io["o_y"][c0 + b0:c0 + b0 + nb, :], yt, yt[0:nb, :], is_output=True)
            else:
                kb.dma("sp", hT_d, hT_d.t.rearrange("(c p) t -> p c t", p=128)[:, :, c0:c0 + TT], hT, hT[:])
            if l == 0:
                rmsnorm_cm(kb, c, hT, g_nxt, aT, sq, rstd)
                pst["src"] = io["w_in_c"]
                ev = [gext, gacc]

                def p1_epi(blk, m, hh, pb):
                    eb = ev[blk % 2]
                    kb.op("act" if hh else "dve",
                          (lambda e: e.copy(out=eb[0:m, hh * H2:(hh + 1) * H2], in_=pb[0:m, 0:H2])) if hh else
                          (lambda e: e.tensor_copy(out=eb[0:m, hh * H2:(hh + 1) * H2], in_=pb[0:m, 0:H2])), rd=[pb], wr=[eb])
                    if hh == 1:
                        kb.dma("sp", io["pT1"], io["pT1"][blk * 128:blk * 128 + m, c0:c0 + TT], eb, eb[0:m, 0:TT])

                def p1_tail(s0, nsl, wb):
                    if not last or s0 >= 6144:
                        return
                    pb = bank[6 + (s0 // 256) % 2]
                    for kc in range(16):
                        kb.mm(pb[:, 0:nsl], aT[:, kc, TT - 128:TT], wb[:, kc, 0:nsl], start=(kc == 0), stop=(kc == 15), rd=[wb, aT], wr=[pb])
                    pt = ptl[(s0 // 256) % 2]
                    kb.op("act", lambda e: e.copy(out=pt[:, 0:nsl], in_=pb[:, 0:nsl]), rd=[pb], wr=[pt])
                    o = io["o_gdnc"]
                    kb.dma("sp", o, o.t[0, :, s0:s0 + nsl], pt, pt[61:64, 0:nsl], is_output=True)
                    for t in range(1, 4):
                        kb.dma("sp", o, o.t[1:1 + NS, t - 1, s0:s0 + nsl], pt, pt[64 + t:128:4, 0:nsl], is_output=True)
                proj_cm(kb, c, io["w_in_c"], 16, 0, GDN_PROJ, aT, wt, p1_epi, pst, tail=p1_tail)
            if DEBUG_UNITS is not None:
                break


def stage_B_gdn(kb, c, io):
    nc = kb.nc
    PS, bank, ident = c["PS"], c["bank"], c["ident"]
    pT, ymix = io["pT1"], io["ymix"]
    with kb.scope():
        cw = kb.sb("G_cw", [128, 48, 4])
        nwv = kb.sb("G_nw", [128, 1])
        dtb = kb.sb("G_dtb", [64, 16])
        Aneg = kb.sb("G_A", [64, 16])
        with nc.allow_non_contiguous_dma(reason="tiny parameter vectors"):
            for i in range(4):
                kb.dma("sp", cw, cw[:, :, i], io["gdn_conv_w"], io["gdn_conv_w"].t[i].rearrange("(c p) -> p c", p=128))
            kb.dma("sp", nwv, nwv[:, :], io["gdn_norm_w"], io["gdn_norm_w"].t.rearrange("(p o) -> p o", o=1))
            kb.dma("sp", dtb, dtb[:], io["gdn_dt_bias"], io["gdn_dt_bias"].t.partition_broadcast(64))
            kb.dma("sp", Aneg, Aneg[:], io["gdn_A_log"], io["gdn_A_log"].t.partition_broadcast(64))
        kb.op("act", lambda e: e.activation(out=Aneg[:], in_=Aneg[:], func=AF.Exp), rd=[Aneg], wr=[Aneg])
        kb.op("dve", lambda e: e.tensor_scalar(out=Aneg[:], in0=Aneg[:], scalar1=-1.0, scalar2=None, op0=ALU.mult), rd=[Aneg], wr=[Aneg])
        inclU = kb.sb("G_inclU", [64, 64])
        strU = kb.sb("G_strU", [64, 64])
        strL = kb.sb("G_strL", [64, 64])
        for m, pat, cmul, cmp_ in ((inclU, 1, -1, ALU.is_ge), (strU, 1, -1, ALU.is_gt), (strL, -1, 1, ALU.is_gt)):
            kb.op("pool", lambda e: e.memset(m[:], 1.0), wr=[m])
            kb.op("pool", lambda e: e.affine_select(out=m[:], in_=m[:], pattern=[[pat, 64]], base=0,
                                                    channel_multiplier=cmul, compare_op=cmp_, fill=0.0), rd=[m], wr=[m])
        one128 = kb.sb("G_one", [128, 128])
        kb.op("pool", lambda e: e.memset(one128[:], 1.0), wr=[one128])

        raw = kb.sb("G_raw", [128, 48, 67])
        t1 = kb.sb("G_t1", [128, 48, 64])
        qkv = kb.sb("G_qkv", [128, 48, 64])
        zt = kb.sb("G_zt", [128, 16, 64])
        bar = kb.sb("G_bar", [16, 2, 64])
        T64 = lambda nm: kb.sb("G_" + nm, [64, 16, 64])
        B64 = lambda nm: kb.sb("G_" + nm, [64, 16, 64], BF16)
        T128 = lambda nm: kb.sb("G_" + nm, [128, 16, 64])
        B128 = lambda nm: kb.sb("G_" + nm, [128, 16, 64], BF16)
        TK = lambda nm: kb.sb("G_" + nm, [64, 16, 128])
        BK = lambda nm: kb.sb("G_" + nm, [64, 16, 128], BF16)
        Gt, dU, dL = (T64(n) for n in ("Gt", "dU", "dL"))
        Mq, MT, atT = (B64(n) for n in ("Mq", "MT", "atT"))
        Pp, Qq, Rr = [B64("P0"), B64("P1")], [B64("Q0"), B64("Q1")], [B64("R0"), B64("R1")]
        Ebc, osb = T128("Ebc"), T128("osb")
        bbc, qgT, nkc = (B128(n) for n in ("bbc", "qgT", "nkc"))
        ktm, vtm, kbt = (TK(n) for n in ("ktm", "vtm", "kbt"))
        kgt, kbtb, vtb, vn = (BK(n) for n in ("kgt", "kbtb", "vtb", "vn"))
        sqb = kb.sb("G_sqb", [128, 32, 64], BF16)
        qkb = kb.sb("G_qkb", [128, 32, 64], BF16)
        S = kb.sb("G_S", [128, 16, 128])
        Sb = kb.sb("G_Sb", [128, 16, 128], BF16)
        one128b = kb.sb("G_oneb", [128, 128], BF16)
        kb.op("pool", lambda e: e.memset(one128b[:], 1.0), wr=[one128b])
        bet = kb.sb("G_bet", [64, 16])
        gg = kb.sb("G_gg", [64, 16])
        gcs = kb.sb("G_gcs", [64, 16])
        egc = kb.sb("G_egc", [64, 16])
        rn = t1

        def wide(b0, rows, n, C):
            return PS.t[0:rows, b0 * 512:b0 * 512 + n * C].rearrange("p (j l) -> p j l", l=C)

        def wb(b0, ncols):
            return [bank[b0 + i] for i in range((ncols + 511) // 512)]

        def mm16(b0, rows, C_out, lfn, rfn, rd):
            for h in range(16):
                kb.mm(PS.t[0:rows, b0 * 512 + h * C_out:b0 * 512 + (h + 1) * C_out], lfn(h), rfn(h), rd=rd, wr=wb(b0, 16 * C_out))

        def mmbc(b0, lhsT, src, nblk, C, rd):
            per = max(1, 512 // C)
            for q in range(0, nblk, per):
                n = min(per, nblk - q)
                kb.mm(PS.t[:, b0 * 512 + q * C:b0 * 512 + (q + n) * C], lhsT, src(q, q + n), rd=rd, wr=wb(b0, nblk * C))

        units = [(64, 64 * ch, ch, None) for ch in range(SEQ // 64)] + [(LS, SEQ + LS * s, 0, s) for s in range(NS)]
        if DEBUG_UNITS is not None:
            units = [units[i] for i in DEBUG_UNITS]
        for ui, (C, t0, ch, s) in enumerate(units):
            src = pT.t[0:6144, :].rearrange("(c p) t -> p c t", p=128)
            lo = 0 if (s is None and ch > 0) else 3
            kb.dma("sp", raw, raw[:, :, lo:3 + C], pT, src[:, :, t0 - 3 + lo:t0 + C])
            if lo:
                if s is None:
                    kb.op("pool", lambda e: e.memset(raw[:, :, 0:3], 0.0), wr=[raw])
                else:
                    with nc.allow_non_contiguous_dma(reason="conv state transposing load (small)"):
                        for r in range(3):
                            kb.dma("sp", raw, raw[:, :, r], io["state_gdn_conv"], io["state_gdn_conv"].t[s, r].rearrange("(c p) -> p c", p=128))
            kb.dma("sp", zt, zt[:, :, 0:C], pT, pT.t[6144:8192, :].rearrange("(c p) t -> p c t", p=128)[:, :, t0:t0 + C])
            kb.dma("sp", bar, bar[:, :, 0:C], pT, pT.t[8192:8224, :].rearrange("(a h) t -> h a t", h=16)[:, :, t0:t0 + C])
            if s is None and ch == 0:
                kb.op("pool", lambda e: e.memset(S[:], 0.0), wr=[S])
            elif s is not None:
                kb.dma("sp", S, S[:], io["state_gdn"], io["state_gdn"].t[s].rearrange("h k v -> k h v"))
            V = lambda t, n=48: t[:, 0:n, 0:C]
            kb.op("dve", lambda e: e.tensor_tensor(out=V(qkv), in0=raw[:, :, 0:C], in1=bc(cw[:, :, 0:1], [128, 48, C]), op=ALU.mult),
                  rd=[raw, cw], wr=[qkv])
            for i in range(1, 4):
                kb.op("pool", lambda e: e.tensor_tensor(out=V(t1), in0=raw[:, :, i:i + C], in1=bc(cw[:, :, i:i + 1], [128, 48, C]), op=ALU.mult),
                      rd=[raw, cw], wr=[t1])
                kb.op("dve", lambda e: e.tensor_tensor(out=V(qkv), in0=V(qkv), in1=V(t1), op=ALU.add), rd=[qkv, t1], wr=[qkv])
            kb.op("act", lambda e: e.activation(out=V(qkv), in_=V(qkv), func=AF.Silu), rd=[qkv], wr=[qkv])
            kb.op("act", lambda e: e.activation(out=V(sqb, 32), in_=V(qkv, 32), func=AF.Square), rd=[qkv], wr=[sqb])
            mmbc(0, one128b[:], lambda a, b: sqb[:, a:b, 0:C], 32, C, [one128b, sqb])
            kb.op("dve", lambda e: e.tensor_scalar(out=V(rn, 32), in0=wide(0, 128, 32, C), scalar1=1e-6, scalar2=None, op0=ALU.add),
                  rd=wb(0, 32 * C), wr=[rn])
            kb.op("act", lambda e: e.activation(out=V(rn, 32), in_=V(rn, 32), func=AF.Sqrt), rd=[rn], wr=[rn])
            kb.op("dve", lambda e: e.reciprocal(out=V(rn, 32), in_=V(rn, 32)), rd=[rn], wr=[rn])
            kb.op("dve", lambda e: e.scalar_tensor_tensor(out=qkv[:, 0:16, 0:C], in0=qkv[:, 0:16, 0:C], scalar=128.0 ** -0.5, in1=rn[:, 0:16, 0:C],
                                                          op0=ALU.mult, op1=ALU.mult), rd=[qkv, rn], wr=[qkv])
            kb.op("dve", lambda e: e.tensor_tensor(out=qkv[:, 16:32, 0:C], in0=qkv[:, 16:32, 0:C], in1=rn[:, 16:32, 0:C], op=ALU.mult),
                  rd=[qkv, rn], wr=[qkv])
            kb.op("dve", lambda e: e.tensor_copy(out=V(qkb, 32), in_=V(qkv, 32)), rd=[qkv], wr=[qkb])
            kb.op("act", lambda e: e.copy(out=Sb[:], in_=S[:]), rd=[S], wr=[Sb])
            qT, kT, vT = (lambda h: qkb[:, h, 0:C]), (lambda h: qkb[:, 16 + h, 0:C]), (lambda h: qkv[:, 32 + h, 0:C])
            kTf = lambda h: qkv[:, 16 + h, 0:C]
            for a in range(2):
                kb.tr(out=bank[4][0:C, a * 16:(a + 1) * 16], in_=bar[:, a, 0:C], identity=ident[0:16, 0:16], rd=[bar, ident], wr=[bank[4]])
            kb.op("act", lambda e: e.activation(out=bet[0:C, :], in_=bank[4][0:C, 0:16], func=AF.Sigmoid), rd=[bank[4]], wr=[bet])
            kb.op("dve", lambda e: e.tensor_tensor(out=gg[0:C, :], in0=bank[4][0:C, 16:32], in1=dtb[0:C, :], op=ALU.add), rd=[bank[4], dtb], wr=[gg])
            kb.op("act", lambda e: e.activation(out=gg[0:C, :], in_=gg[0:C, :], func=AF.Exp), rd=[gg], wr=[gg])
            kb.op("act", lambda e: e.activation(out=gg[0:C, :], in_=gg[0:C, :], func=AF.Ln, bias=1.0), rd=[gg], wr=[gg])
            kb.op("dve", lambda e: e.tensor_tensor(out=gg[0:C, :], in0=gg[0:C, :], in1=Aneg[0:C, :], op=ALU.mult), rd=[gg, Aneg], wr=[gg])
            kb.mm(bank[4][0:C, 32:48], inclU[0:C, 0:C], gg[0:C, :], rd=[inclU, gg], wr=[bank[4]])
            kb.op("dve", lambda e: e.tensor_copy(out=gcs[0:C, :], in_=bank[4][0:C, 32:48]), rd=[bank[4]], wr=[gcs])
            kb.op("act", lambda e: e.activation(out=egc[0:C, :], in_=bank[4][0:C, 32:48], func=AF.Exp), rd=[bank[4]], wr=[egc])
            kb.op("dve", lambda e: e.tensor_copy(out=Gt[0:C, :, 0:C], in_=bc(inclU[0:C, None, 0:C], [C, 16, C])), rd=[inclU], wr=[Gt])
            kb.op("dve", lambda e: e.tensor_tensor(out=Gt[0:C, :, 0:C], in0=Gt[0:C, :, 0:C], in1=bc(gg[0:C, :, None], [C, 16, C]), op=ALU.mult),
                  rd=[Gt, gg], wr=[Gt])
            mmbc(5, one128[0:C, :], lambda a, b: Gt[0:C, a:b, 0:C], 16, C, [one128, Gt])
            kb.op("act", lambda e: e.activation(out=Ebc[:, :, 0:C], in_=wide(5, 128, 16, C), func=AF.Exp), rd=wb(5, 16 * C), wr=[Ebc])
            kb.op("dve", lambda e: e.tensor_tensor(out=dU[0:C, :, 0:C], in0=wide(5, C, 16, C), in1=bc(gcs[0:C, :, None], [C, 16, C]), op=ALU.subtract),
                  rd=wb(5, 16 * C) + [gcs], wr=[dU])
            kb.op("dve", lambda e: e.tensor_scalar(out=dL[0:C, :, 0:C], in0=dU[0:C, :, 0:C], scalar1=-1.0, scalar2=0.0, op0=ALU.mult, op1=ALU.min),
                  rd=[dU], wr=[dL])
            kb.op("dve", lambda e: e.tensor_scalar(out=dU[0:C, :, 0:C], in0=dU[0:C, :, 0:C], scalar1=0.0, scalar2=None, op0=ALU.min), rd=[dU], wr=[dU])
            kb.op("act", lambda e: e.activation(out=dU[0:C, :, 0:C], in_=dU[0:C, :, 0:C], func=AF.Exp), rd=[dU], wr=[dU])
            kb.op("act", lambda e: e.activation(out=dL[0:C, :, 0:C], in_=dL[0:C, :, 0:C], func=AF.Exp), rd=[dL], wr=[dL])
            kb.op("dve", lambda e: e.tensor_tensor(out=dU[0:C, :, 0:C], in0=dU[0:C, :, 0:C], in1=bc(inclU[0:C, None, 0:C], [C, 16, C]), op=ALU.mult),
                  rd=[dU, inclU], wr=[dU])
            kb.op("dve", lambda e: e.tensor_copy(out=Gt[0:C, :, 0:C], in_=bc(ident[0:C, None, 0:C], [C, 16, C])), rd=[ident], wr=[Gt])
            kb.op("dve", lambda e: e.tensor_tensor(out=Gt[0:C, :, 0:C], in0=Gt[0:C, :, 0:C], in1=bc(bet[0:C, :, None], [C, 16, C]), op=ALU.mult),
                  rd=[Gt, bet], wr=[Gt])
            mmbc(0, one128[0:C, :], lambda a, b: Gt[0:C, a:b, 0:C], 16, C, [one128, Gt])
            kb.op("dve", lambda e: e.tensor_tensor(out=bbc[:, :, 0:C], in0=qkv[:, 16:32, 0:C], in1=wide(0, 128, 16, C), op=ALU.mult),
                  rd=[qkv] + wb(0, 16 * C), wr=[bbc])
            kb.op("pool", lambda e: e.tensor_tensor(out=qgT[:, :, 0:C], in0=qkv[:, 0:16, 0:C], in1=Ebc[:, :, 0:C], op=ALU.mult), rd=[qkv, Ebc], wr=[qgT])
            for dst, fn, b0 in ((ktm, kTf, 0), (vtm, vT, 4)):
                for h in range(16):
                    kb.tr(out=PS.t[0:C, b0 * 512 + h * 128:b0 * 512 + (h + 1) * 128], in_=fn(h), identity=ident[:], rd=[qkv, ident], wr=wb(b0, 2048))
                kb.op("act" if b0 else "dve",
                      (lambda e: e.copy(out=dst[0:C], in_=wide(b0, C, 16, 128))) if b0 else
                      (lambda e: e.tensor_copy(out=dst[0:C], in_=wide(b0, C, 16, 128))), rd=wb(b0, 2048), wr=[dst])
            kb.op("dve", lambda e: e.tensor_tensor(out=kgt[0:C], in0=ktm[0:C], in1=bc(dU[0:C, :, C - 1:C], [C, 16, 128]), op=ALU.mult), rd=[ktm, dU], wr=[kgt])
            kb.op("dve", lambda e: e.tensor_tensor(out=kbt[0:C], in0=ktm[0:C], in1=bc(bet[0:C, :, None], [C, 16, 128]), op=ALU.mult), rd=[ktm, bet], wr=[kbt])
            kb.op("dve", lambda e: e.tensor_tensor(out=kbtb[0:C], in0=kbt[0:C], in1=bc(egc[0:C, :, None], [C, 16, 128]), op=ALU.mult), rd=[kbt, egc], wr=[kbtb])
            kb.op("dve", lambda e: e.tensor_tensor(out=vtb[0:C], in0=vtm[0:C], in1=bc(bet[0:C, :, None], [C, 16, 128]), op=ALU.mult), rd=[vtm, bet], wr=[vtb])
            mm16(0, C, C, lambda h: bbc[:, h, 0:C], kT, [bbc, qkb])
            kb.op("dve", lambda e: e.scalar_tensor_tensor(out=Mq[0:C, :, 0:C], in0=wide(0, C, 16, C), scalar=-1.0, in1=dL[0:C, :, 0:C], op0=ALU.mult, op1=ALU.mult),
                  rd=wb(0, 16 * C) + [dL], wr=[Mq])
            kb.op("dve", lambda e: e.tensor_tensor(out=Mq[0:C, :, 0:C], in0=Mq[0:C, :, 0:C], in1=bc(strL[0:C, None, 0:C], [C, 16, C]), op=ALU.mult),
                  rd=[Mq, strL], wr=[Mq])
            mm16(2, C, C, kT, lambda h: bbc[:, h, 0:C], [bbc, qkb])
            kb.op("dve", lambda e: e.scalar_tensor_tensor(out=MT[0:C, :, 0:C], in0=wide(2, C, 16, C), scalar=-1.0, in1=dU[0:C, :, 0:C], op0=ALU.mult, op1=ALU.mult),
                  rd=wb(2, 16 * C) + [dU], wr=[MT])
            kb.op("dve", lambda e: e.tensor_tensor(out=MT[0:C, :, 0:C], in0=MT[0:C, :, 0:C], in1=bc(strU[0:C, None, 0:C], [C, 16, C]), op=ALU.mult),
                  rd=[MT, strU], wr=[MT])
            mm16(4, C, C, kT, qT, [qkb])
            kb.op("dve", lambda e: e.tensor_tensor(out=atT[0:C, :, 0:C], in0=wide(4, C, 16, C), in1=dU[0:C, :, 0:C], op=ALU.mult),
                  rd=wb(4, 16 * C) + [dU], wr=[atT])
            Rc, Pc, Qc = Rr[0], MT, Mq
            kb.op("dve", lambda e: e.tensor_tensor(out=Rc[0:C, :, 0:C], in0=MT[0:C, :, 0:C], in1=bc(ident[0:C, None, 0:C], [C, 16, C]), op=ALU.add),
                  rd=[MT, ident], wr=[Rc])
            levels = {64: 5, 4: 1}[C]
            for lvl in range(levels):
                lastl = lvl == levels - 1
                Qn, Pn, Rn = Qq[lvl % 2], Pp[lvl % 2], Rr[(lvl + 1) % 2]
                mm16(0, C, C, lambda h: Pc[0:C, h, 0:C], lambda h: Qc[0:C, h, 0:C], [Pc, Qc])
                if not lastl:
                    mm16(2, C, C, lambda h: Qc[0:C, h, 0:C], lambda h: Pc[0:C, h, 0:C], [Pc, Qc])
                kb.op("act", lambda e: e.copy(out=Qn[0:C, :, 0:C], in_=wide(0, C, 16, C)), rd=wb(0, 16 * C), wr=[Qn])
                if not lastl:
                    kb.op("dve", lambda e: e.tensor_copy(out=Pn[0:C, :, 0:C], in_=wide(2, C, 16, C)), rd=wb(2, 16 * C), wr=[Pn])
                mm16(4, C, C, lambda h: Qn[0:C, h, 0:C], lambda h: Rc[0:C, h, 0:C], [Qn, Rc])
                kb.op("dve", lambda e: e.tensor_tensor(out=Rn[0:C, :, 0:C], in0=Rc[0:C, :, 0:C], in1=wide(4, C, 16, C), op=ALU.add),
                      rd=[Rc] + wb(4, 16 * C), wr=[Rn])
                Rc, Pc, Qc = Rn, Pn, Qn
            mm16(6, 128, C, lambda h: kbtb[0:C, h, :], lambda h: Rc[0:C, h, 0:C], [kbtb, Rc])
            kb.op("dve", lambda e: e.tensor_scalar(out=nkc[:, :, 0:C], in0=wide(6, 128, 16, C), scalar1=-1.0, scalar2=None, op0=ALU.mult),
                  rd=wb(6, 16 * C), wr=[nkc])
            for h in range(16):
                o = PS.t[0:C, h * 128:(h + 1) * 128]
                kb.mm(o, Rc[0:C, h, 0:C], vtb[0:C, h, :], start=True, stop=False, rd=[Rc, vtb], wr=wb(0, 2048))
                kb.mm(o, nkc[:, h, 0:C], Sb[:, h, :], start=False, stop=True, rd=[nkc, Sb], wr=wb(0, 2048))
            kb.op("act", lambda e: e.copy(out=vn[0:C], in_=wide(0, C, 16, 128)), rd=wb(0, 2048), wr=[vn])
            for h in range(16):
                o = PS.t[:, 2048 + h * C:2048 + (h + 1) * C]
                kb.mm(o, Sb[:, h, :], qgT[:, h, 0:C], start=True, stop=False, rd=[Sb, qgT], wr=wb(4, 16 * C))
                kb.mm(o, vn[0:C, h, :], atT[0:C, h, 0:C], start=False, stop=True, rd=[vn, atT], wr=wb(4, 16 * C))
            kb.op("act", lambda e: e.copy(out=osb[:, :, 0:C], in_=wide(4, 128, 16, C)), rd=wb(4, 16 * C), wr=[osb])
            mm16(0, 128, 128, lambda h: kgt[0:C, h, :], lambda h: vn[0:C, h, :], [kgt, vn])
            kb.op("dve", lambda e: e.tensor_tensor(out=S[:], in0=S[:], in1=bc(Ebc[:, :, C - 1:C], [128, 16, 128]), op=ALU.mult), rd=[S, Ebc], wr=[S])
            kb.op("dve", lambda e: e.tensor_tensor(out=S[:], in0=S[:], in1=wide(0, 128, 16, 128), op=ALU.add), rd=[S] + wb(0, 2048), wr=[S])
            kb.op("act", lambda e: e.activation(out=qgT[:, :, 0:C], in_=osb[:, :, 0:C], func=AF.Square), rd=[osb], wr=[qgT])
            mmbc(6, one128b[:], lambda a, b: qgT[:, a:b, 0:C], 16, C, [one128b, qgT])
            kb.op("dve", lambda e: e.tensor_scalar(out=Ebc[:, :, 0:C], in0=wide(6, 128, 16, C), scalar1=1.0 / 128, scalar2=EPS, op0=ALU.mult, op1=ALU.add),
                  rd=wb(6, 16 * C), wr=[Ebc])
            kb.op("act", lambda e: e.activation(out=Ebc[:, :, 0:C], in_=Ebc[:, :, 0:C], func=AF.Sqrt), rd=[Ebc], wr=[Ebc])
            kb.op("dve", lambda e: e.reciprocal(out=Ebc[:, :, 0:C], in_=Ebc[:, :, 0:C]), rd=[Ebc], wr=[Ebc])
            kb.op("dve", lambda e: e.scalar_tensor_tensor(out=osb[:, :, 0:C], in0=osb[:, :, 0:C], scalar=nwv[:, 0:1], in1=Ebc[:, :, 0:C], op0=ALU.mult, op1=ALU.mult),
                  rd=[osb, nwv, Ebc], wr=[osb])
            kb.op("act", lambda e: e.activation(out=zt[:, :, 0:C], in_=zt[:, :, 0:C], func=AF.Silu), rd=[zt], wr=[zt])
            kb.op("dve", lambda e: e.tensor_tensor(out=osb[:, :, 0:C], in0=osb[:, :, 0:C], in1=zt[:, :, 0:C], op=ALU.mult), rd=[osb, zt], wr=[osb])
            kb.dma("pool", ymix, ymix.t.rearrange("(h p) t -> p h t", p=128)[:, :, t0:t0 + C], osb, osb[:, :, 0:C])
            if s is not None or ch == SEQ // 64 - 1 or DEBUG_UNITS is not None:
                row = 0 if s is None else 1 + s
                kb.dma("pool", io["o_gdn"], io["o_gdn"].t[row].rearrange("h k v -> k h v"), S, S[:], is_output=True)
            yield
SCRATCH_KIND = "Internal"
_NC_CACHE = {}


def make_in_maps(inp):
    f32 = np.float32
    B = 4
    in_maps = []
    ca = lambda a: np.ascontiguousarray(a, dtype=f32)
    shared = {
        "norm_mem": ca(inp["norm_mem"]), "w_xk": ca(inp["w_xk"]), "w_xv": ca(inp["w_xv"]),
        "norm_mix": ca(inp["norm_mix"]), "w_in_ab": ca(inp["w_in_ab"][0]),
        "rw_mu": ca(inp["rw_mu"][0]), "rw_w0": ca(inp["rw_w0"][0]), "rw_a0": ca(inp["rw_a0"][0]),
        "rw_k_k": ca(inp["rw_k_k"][0]), "rw_k_a": ca(inp["rw_k_a"][0]), "rw_r_k": ca(inp["rw_r_k"][0]).reshape(1024),
        "rw_gn_w": ca(inp["rw_gn_w"][0]).reshape(1024), "rw_gn_b": ca(inp["rw_gn_b"][0]).reshape(1024),
        "rw_w_up": ca(inp["rw_w_up"][0]), "rw_a_up": ca(inp["rw_a_up"][0]), "rw_g_up": ca(inp["rw_g_up"][0]),
        "norm_xa": ca(inp["norm_xa"]), "norm_ffn": ca(inp["norm_ffn"]), "w_out_ab": ca(inp["w_out_ab"][0]),
        "w_xq": ca(inp["w_xq"]), "w_xo": ca(inp["w_xo"]), "ffn_w_in": ca(inp["ffn_w_in"]), "ffn_conv_w": ca(inp["ffn_conv_w"]),
        "ffn_conv_b": ca(inp["ffn_conv_b"]), "ffn_w_out": ca(inp["ffn_w_out"]), "w_in_c": ca(inp["w_in_c"][0]),
        "gdn_conv_w": ca(inp["gdn_conv_w"][0]), "gdn_A_log": ca(inp["gdn_A_log"][0]), "gdn_dt_bias": ca(inp["gdn_dt_bias"][0]),
        "gdn_norm_w": ca(inp["gdn_norm_w"][0]), "w_out_c": ca(inp["w_out_c"][0]), "norm_final": ca(inp["norm_final"]),
        "ssd_D": ca(inp["ssd_D"][0]), "ssd_norm_w": ca(inp["ssd_norm_w"][0]),
        "ssd_conv_w": ca(inp["ssd_conv_w"][0]), "ssd_conv_b": ca(inp["ssd_conv_b"][0]),
        "ssd_dt_bias": ca(inp["ssd_dt_bias"][0]), "ssd_A_log": ca(inp["ssd_A_log"][0]),
    }
    for c in range(NCORES):
        b = c % B
        xs = inp["x_sample"][c * NS:(c + 1) * NS].reshape(TS, D)
        m = dict(shared)
        m["x"] = ca(np.concatenate([inp["x_prompt"][b], xs], axis=0))
        m["mem"] = ca(inp["mem_prompt"][b])
        ss = slice(c * NS, (c + 1) * NS)
        m["state_ssd_conv"] = ca(inp["state_ssd_conv"][0, ss])
        m["state_ffn_conv"] = ca(inp["state_ffn_conv"][:, ss])
        m["cache_mem_k"] = ca(inp["cache_mem_k"][:, ss])
        m["cache_mem_v"] = ca(inp["cache_mem_v"][:, ss])
        m["state_gdn_conv"] = ca(inp["state_gdn_conv"][0, ss])
        m["state_gdn"] = ca(inp["state_gdn"][0, ss])
        m["state_rwkv"] = ca(inp["state_rwkv"][0, ss])
        m["state_rwkv_shift"] = ca(inp["state_rwkv_shift"][0, ss])
        m["state_ssd"] = ca(inp["state_ssd"][0, ss]).reshape(NS, 16, 64, 128)
        in_maps.append(m)
    return in_maps


def kernel(**inp):
    f32 = np.float32
    B = 4
    if "nc" not in _NC_CACHE:
        _NC_CACHE["nc"] = build()
    nc = _NC_CACHE["nc"]
    in_maps = make_in_maps(inp)
    res = run_bass_kernel_spmd(nc, in_maps, core_ids=list(range(NCORES)))
    R = res.results
    _NC_CACHE["R"] = R
    E, O, DEPTH, DB = 1, 1, 2, 128
    y_p = np.zeros((B, SEQ, D), f32)
    y_s = np.zeros((DB, LS, D), f32)
    rw_p = np.zeros((E, B, 16, 64, 64), f32)
    rw_s = np.zeros((E, DB, 16, 64, 64), f32)
    sh_p = np.zeros((E, B, RW_PROJ), f32)
    sh_s = np.zeros((E, DB, RW_PROJ), f32)
    ssd_p = np.zeros((E, B, 2, 8, 64, 128), f32)
    ssd_s = np.zeros((E, DB, 2, 8, 64, 128), f32)
    ssdc_p = np.zeros((E, B, 3, 1536), f32)
    ssdc_s = np.zeros((E, DB, 3, 1536), f32)
    gdn_p = np.zeros((O, B, 16, 128, 128), f32)
    gdn_s = np.zeros((O, DB, 16, 128, 128), f32)
    gdnc_p = np.zeros((O, B, 3, 6144), f32)
    gdnc_s = np.zeros((O, DB, 3, 6144), f32)
    ffn_p = np.zeros((DEPTH, B, 2, 5632), f32)
    ffn_s = np.zeros((DEPTH, DB, 2, 5632), f32)
    mk_p = np.zeros((DEPTH, B, NMEM, 4, 128), f32)
    mv_p = np.zeros((DEPTH, B, NMEM, 4, 128), f32)
    for c in range(NCORES):
        r = R[c]
        sl = slice(c * NS, (c + 1) * NS)
        sh_s[0, sl] = r["o_sh"][1:]
        ssdc_s[0, sl] = r["o_ssdc"][1:]
        ssd_s[0, sl] = r["o_ssd"][1:].reshape(NS, 2, 8, 64, 128)
        rw_s[0, sl] = r["o_rw"][1:]
        y_s[sl] = r["o_y"][SEQ:].reshape(NS, LS, D)
        gdn_s[0, sl] = r["o_gdn"][1:]
        ffn_s[:, sl] = r["o_ffn"][:, 1:]
        gdnc_s[0, sl] = r["o_gdnc"][1:]
        if c < B:
            mk_p[:, c] = r["o_memk"].reshape(DEPTH, NMEM, 4, 128)
            mv_p[:, c] = r["o_memv"].reshape(DEPTH, NMEM, 4, 128)
            sh_p[0, c] = r["o_sh"][0]
            ssdc_p[0, c] = r["o_ssdc"][0]
            ssd_p[0, c] = r["o_ssd"][0].reshape(2, 8, 64, 128)
            rw_p[0, c] = r["o_rw"][0]
            y_p[c] = r["o_y"][:SEQ]
            gdn_p[0, c] = r["o_gdn"][0]
            ffn_p[:, c] = r["o_ffn"][:, 0]
            gdnc_p[0, c] = r["o_gdnc"][0]
    return (y_p, y_s, rw_p, rw_s, sh_p, sh_s, ssd_p, ssd_s, ssdc_p, ssdc_s, gdn_p, gdn_s, gdnc_p, gdnc_s,
            ffn_p, ffn_s, mk_p, mv_p)
```

```python
import contextlib
import numpy as np
import concourse.bass as bass
import concourse.mybir as mybir
from concourse.bass_utils import run_bass_kernel_spmd

F32 = mybir.dt.float32
BF16 = mybir.dt.bfloat16
AF = mybir.ActivationFunctionType
ALU = mybir.AluOpType
AX = mybir.AxisListType

NCORES = 8
D = 2048
SEQ = 2048
NS = 16
LS = 4
TS = NS * LS
T = SEQ + TS
NMEM = 256
EPS = 1e-6
RW_PROJ = 3328
SSD_PROJ = 2576
AB_PROJ = RW_PROJ + SSD_PROJ
XA = 512


SAME_ENGINE_NOSYNC = ()


class Buf:
    def __init__(self, t, name):
        self.t = t
        self.name = name
        self.excl = False
        self.w = None
        self.r = {}

    def __getitem__(self, idx):
        return self.t[idx]


class KB:
    NDMA = 6

    def __init__(self):
        self.nc = bass.Bass("TRN2", target_bir_lowering=False)
        self.es = contextlib.ExitStack()
        nc = self.nc
        self.eng = {"pe": nc.tensor, "dve": nc.vector, "act": nc.scalar, "pool": nc.gpsimd, "sp": nc.sync}
        self.sems = {}
        self.cnt = {}
        for e in ("pe", "dve", "act", "pool"):
            self.sems[e] = self.es.enter_context(nc.semaphore("s_" + e))
            self.cnt[e] = 0
        self.dsem = {}
        self.dcnt = {}
        for q in ("sp", "pool", "act"):
            self.dsem[q] = [self.es.enter_context(nc.semaphore(f"d_{q}{i}")) for i in range(self.NDMA)]
            self.dcnt[q] = 0
        self.seen = {}
        self.out_events = []
        self.nid = 0

    def sb(self, name, shape, dt=F32):
        self.nid += 1
        name = f"{name}_{self.nid}"
        return Buf(self.es.enter_context(self.nc.sbuf_tensor(name, list(shape), dt)), name)

    def ps(self, name, shape, dt=F32):
        return Buf(self.es.enter_context(self.nc.psum_tensor(name, list(shape), dt)), name)

    def dram(self, name, shape, kind, dt=F32):
        return Buf(self.nc.dram_tensor(name, list(shape), dt, kind=kind).ap(), name)

    def _sem(self, key):
        if isinstance(key, str):
            return self.sems[key]
        return self.dsem[key[0]][key[1]]

    def _wait(self, engine, ev):
        key, val, src = ev
        if src == engine and engine in SAME_ENGINE_NOSYNC and isinstance(key, str):
            return
        k = (engine, key)
        if self.seen.get(k, 0) >= val:
            return
        self.eng[engine].wait_ge(self._sem(key), val)
        self.seen[k] = val

    def _deps(self, engine, rd, wr):
        for b in rd:
            if b.w is not None:
                self._wait(engine, b.w)
        for b in wr:
            if b.w is not None and not (engine == "pe" and b.w[2] == "pe"):
                self._wait(engine, b.w)
            for key, (val, src) in b.r.items():
                self._wait(engine, (key, val, src))

    def _mark(self, ev, rd, wr):
        key, val, src = ev
        for b in rd:
            b.r[key] = (val, src)
        for b in wr:
            b.w = ev
            b.r = {}

    def op(self, engine, fn, rd=(), wr=()):
        wr = list(wr) + [b for b in rd if b.excl and b not in wr]
        rd = [b for b in rd if not b.excl]
        self._deps(engine, rd, wr)
        ins = fn(self.eng[engine])
        self.cnt[engine] += 1
        ins.then_inc(self.sems[engine], 1)
        self._mark((engine, self.cnt[engine], engine), rd, wr)

    def _pe_mode(self, st):
        pass

    def mm(self, out, lhsT, rhs, start=True, stop=True, rd=(), wr=()):
        self._pe_mode(lhsT)
        self.op("pe", lambda e: e.matmul(out, lhsT, rhs, start=start, stop=stop), rd, wr)

    def tr(self, out, in_, identity, rd=(), wr=()):
        self._pe_mode(in_)
        self.op("pe", lambda e: e.transpose(out=out, in_=in_, identity=identity), rd, wr)

    def dma(self, q, out_buf, out_ap, in_buf, in_ap, is_output=False):
        i = self.dcnt[q]
        s = i % self.NDMA
        key = (q, s)
        if i >= self.NDMA:
            self._wait(q, (key, 16 * (i // self.NDMA), "dma"))
        self._deps(q, [in_buf], [out_buf])
        ins = self.eng[q].dma_start(out=out_ap, in_=in_ap)
        val = 16 * (i // self.NDMA + 1)
        ins.then_inc(self.dsem[q][s], 16)
        self.dcnt[q] += 1
        ev = (key, val, "dma")
        self._mark(ev, [in_buf], [out_buf])
        if is_output:
            self.out_events.append(ev)

    def barrier(self):
        evs = [(e, self.cnt[e], e) for e in ("pe", "dve", "act", "pool") if self.cnt[e]]
        for q in ("sp", "pool", "act"):
            i = self.dcnt[q]
            for s in range(self.NDMA):
                n_on_s = (i - s + self.NDMA - 1) // self.NDMA if i > s else 0
                if n_on_s:
                    evs.append(((q, s), 16 * n_on_s, "dma"))
        for e in ("pe", "dve", "act", "pool", "sp"):
            for ev in evs:
                self._wait(e, ev)

    @contextlib.contextmanager
    def scope(self):
        if getattr(self, "noscope", False):
            yield
            return
        old = self.es
        self.es = contextlib.ExitStack()
        try:
            yield
        finally:
            self.barrier()
            self.es.close()
            self.es = old

    def finish(self):
        for ev in self.out_events:
            self._wait("sp", ev)
        for e in ("pe", "dve", "act", "pool"):
            if self.cnt[e]:
                self._wait("sp", (e, self.cnt[e], e))
        self.es.close()
        return self.nc


DEBUG_UNITS = None
DEBUG_STEPS = 99
DEBUG_OMIT = ()
DEBUG_SKIP = ()
TT = 704
NTT = 3
CB = 47


def consts(kb):
    c = {}
    ident = kb.sb("ident", [128, 128], F32)
    kb.op("pool", lambda e: e.memset(ident[:], 0.0), wr=[ident])
    kb.op("pool", lambda e: e.affine_select(out=ident[:], in_=ident[:], pattern=[[-1, 128]], base=0,
                                            channel_multiplier=1, compare_op=ALU.not_equal, fill=1.0),
          rd=[ident], wr=[ident])
    ones_bf = kb.sb("ones_bf", [128, 128], BF16)
    kb.op("pool", lambda e: e.memset(ones_bf[:], 1.0), wr=[ones_bf])
    c["mkT"] = kb.sb("mkT", [128, 2, 4, NMEM], BF16)
    c["mvB"] = kb.sb("mvB", [128, 2, 2, XA], BF16)
    c["ident"] = ident
    c["ones_bf"] = ones_bf
    PS = kb.ps("PS", [128, 4096], F32)
    c["PS"] = PS
    c["bank"] = [Buf(PS.t[:, 512 * i:512 * (i + 1)], f"bank{i}") for i in range(8)]
    for bb in c["bank"]:
        bb.excl = True
    c["psb"] = c["bank"]
    return c


def stage_memkv(kb, c, io):
    nc = kb.nc
    ident, psb = c["ident"], c["psb"]
    mem_in, norm_mem, w_xk, w_xv, o_memk, o_memv = (io[k] for k in ("mem", "norm_mem", "w_xk", "w_xv", "o_memk", "o_memv"))
    with kb.scope():
        memT = kb.sb("memT", [128, 16, NMEM], BF16)
        mt = [kb.sb(f"mt{i}", [128, D], F32) for i in range(2)]
        junk = kb.sb("junk", [128, D], F32)
        ssq = kb.sb("ssq", [128, 2], F32)
        rstd = kb.sb("rstd", [128, 2], F32)
        for ti in range(2):
            kb.dma("sp", mt[ti], mt[ti][:], mem_in, mem_in[ti * 128:(ti + 1) * 128, :])
            kb.op("act", lambda e: e.activation(out=junk[:], in_=mt[ti][:], func=AF.Square,
                                                accum_out=ssq[:, ti:ti + 1]), rd=[mt[ti]], wr=[junk, ssq])
            kb.op("dve", lambda e: e.tensor_scalar(out=rstd[:, ti:ti + 1], in0=ssq[:, ti:ti + 1], scalar1=1.0 / D,
                                                   scalar2=EPS, op0=ALU.mult, op1=ALU.add), rd=[ssq], wr=[rstd])
            kb.op("act", lambda e: e.activation(out=rstd[:, ti:ti + 1], in_=rstd[:, ti:ti + 1], func=AF.Sqrt),
                  rd=[rstd], wr=[rstd])
            kb.op("dve", lambda e: e.reciprocal(out=rstd[:, ti:ti + 1], in_=rstd[:, ti:ti + 1]), rd=[rstd], wr=[rstd])
            kb.op("dve", lambda e: e.tensor_scalar(out=mt[ti][:], in0=mt[ti][:], scalar1=rstd[:, ti:ti + 1],
                                                   scalar2=None, op0=ALU.mult), rd=[mt[ti], rstd], wr=[mt[ti]])
            for cc in range(16):
                pb = psb[cc % 4]
                kb.tr(out=pb[:, 0:128], in_=mt[ti][:, cc * 128:(cc + 1) * 128],
                                                  identity=ident[:], rd=[mt[ti], ident], wr=[pb])
                kb.op("dve", lambda e: e.tensor_copy(out=memT[:, cc, ti * 128:(ti + 1) * 128], in_=pb[:, 0:128]),
                      rd=[pb], wr=[memT])
        gm = kb.sb("gm", [128, 2, 16], F32)
        with nc.allow_non_contiguous_dma(reason="tiny gain vectors"):
            kb.dma("sp", gm, gm[:], norm_mem, norm_mem.t.rearrange("l (c p) -> p l c", p=128))
        memTg = kb.sb("memTg", [128, 16, NMEM], BF16)
        wkv = [kb.sb(f"wkv{i}", [128, 16, XA], BF16) for i in range(2)]
        okv = [kb.sb(f"okv{i}", [128, XA], F32) for i in range(2)]
        n = 0
        for l in range(2):
            for cc in range(16):
                kb.op("dve", lambda e: e.tensor_scalar(out=memTg[:, cc, :], in0=memT[:, cc, :],
                                                       scalar1=gm[:, l, cc:cc + 1], scalar2=None, op0=ALU.mult),
                      rd=[memT, gm], wr=[memTg])
            for wsrc, odst in ((w_xk, o_memk), (w_xv, o_memv)):
                wb = wkv[n % 2]
                kb.dma("pool", wb, wb[:], wsrc, wsrc.t[l].rearrange("(c p) n -> p c n", p=128))
                for ti in range(2):
                    pb = psb[4 + (n * 2 + ti) % 4]
                    for cc in range(16):
                        kb.mm(pb[:, :], memTg[:, cc, ti * 128:(ti + 1) * 128], wb[:, cc, :],
                                                       start=(cc == 0), stop=(cc == 15), rd=[memTg, wb], wr=[pb])
                    ob = okv[ti]
                    kb.op("act", lambda e: e.copy(out=ob[:], in_=pb[:, :]), rd=[pb], wr=[ob])
                    kb.dma("sp", odst, odst[l, ti * 128:(ti + 1) * 128, :], ob, ob[:], is_output=True)
                    if wsrc is w_xv:
                        kb.op("dve", lambda e: e.tensor_copy(out=c["mvB"][:, l, ti, :], in_=pb[:, :]), rd=[pb], wr=[c["mvB"]])
                if wsrc is w_xk:
                    for hh in range(4):
                        pb = psb[hh % 4]
                        for cc in range(16):
                            kb.mm(pb[:, 0:NMEM], wb[:, cc, hh * 128:(hh + 1) * 128], memTg[:, cc, :],
                                  start=(cc == 0), stop=(cc == 15), rd=[memTg, wb], wr=[pb])
                        kb.op("act", lambda e: e.copy(out=c["mkT"][:, l, hh, :], in_=pb[:, 0:NMEM]), rd=[pb], wr=[c["mkT"]])
                n += 1


def load_gain(kb, name, src, l):
    g = kb.sb(name, [128, 16], F32)
    with kb.nc.allow_non_contiguous_dma(reason="tiny gain vectors"):
        kb.dma("sp", g, g[:], src, src.t[l].rearrange("(c p) -> p c", p=128))
    return g


def rmsnorm_cm(kb, c, hT, gain, hnT, tmp_sq, rstd):
    psb, ones_bf = c["psb"], c["ones_bf"]
    H = TT // 2
    for cc in range(16):
        sq = tmp_sq[cc % 2]
        kb.op("act", lambda e: e.activation(out=sq[:], in_=hT[:, cc, :], func=AF.Square), rd=[hT], wr=[sq])
        for hh in range(2):
            pb = psb[hh]
            kb.mm(pb[:, 0:H], ones_bf[:], sq[:, hh * H:(hh + 1) * H],
                                           start=(cc == 0), stop=(cc == 15), rd=[sq, ones_bf], wr=[pb])
    for hh in range(2):
        pb = psb[hh]
        kb.op("dve", lambda e: e.tensor_scalar(out=rstd[:, hh * H:(hh + 1) * H], in0=pb[:, 0:H], scalar1=1.0 / D,
                                               scalar2=EPS, op0=ALU.mult, op1=ALU.add), rd=[pb], wr=[rstd])
    kb.op("act", lambda e: e.activation(out=rstd[:], in_=rstd[:], func=AF.Sqrt), rd=[rstd], wr=[rstd])
    kb.op("dve", lambda e: e.reciprocal(out=rstd[:], in_=rstd[:]), rd=[rstd], wr=[rstd])
    for cc in range(16):
        dst = hT if hnT is None else hnT
        kb.op("dve", lambda e: e.scalar_tensor_tensor(out=dst[:, cc, :], in0=hT[:, cc, :], scalar=gain[:, cc:cc + 1],
                                                      in1=rstd[:], op0=ALU.mult, op1=ALU.mult), rd=[hT, gain, rstd], wr=[dst])


def stage_A0(kb, c, io):
    ident, psb = c["ident"], c["psb"]
    x_in, w_in, hT_d, pT_d = io["x"], io["w_in_ab"], io["hT"], io["pT0"]
    o_sh, o_ssdc = io["o_sh"], io["o_ssdc"]
    with kb.scope():
        gain = load_gain(kb, "g_mix0", io["norm_mix"], 0)
        hT = kb.sb("A_hT", [128, 16, TT], F32)
        hnT = kb.sb("A_hnT", [128, 16, TT], BF16)
        xr = [kb.sb(f"A_xr{i}", [128, D], F32) for i in range(2)]
        sq = [kb.sb(f"A_sq{i}", [128, TT], BF16) for i in range(2)]
        rstd = kb.sb("A_rstd", [128, TT], F32)
        wt = [kb.sb(f"A_wt{i}", [128, 16, 512], BF16) for i in range(2)]
        ev = [kb.sb(f"A_ev{i}", [128, TT], F32) for i in range(3)]
        ptl = [kb.sb(f"A_ptl{i}", [128, 512], F32) for i in range(2)]
        nev = 0
        for tt in range(NTT):
            c0 = tt * TT
            nrb = (TT + 127) // 128
            for rb in range(nrb):
                r0 = rb * 128
                nr = min(128, TT - r0)
                xb = xr[rb % 2]
                kb.dma("sp", xb, xb[0:nr, :], x_in, x_in[c0 + r0:c0 + r0 + nr, :])
                for g4 in range(4):
                    pb = psb[2 + (rb * 4 + g4) % 4]
                    for k in range(4):
                        cc = g4 * 4 + k
                        kb.tr(out=pb[:, k * 128:k * 128 + nr],
                                                          in_=xb[0:nr, cc * 128:(cc + 1) * 128],
                                                          identity=ident[0:nr, 0:nr], rd=[xb, ident], wr=[pb])
                    kb.op("act" if g4 % 2 else "dve",
                          (lambda e: e.copy(out=hT[:, g4 * 4:g4 * 4 + 4, r0:r0 + nr],
                                            in_=pb[:, :].rearrange("p (k n) -> p k n", n=128)[:, :, 0:nr])) if g4 % 2 else
                          (lambda e: e.tensor_copy(out=hT[:, g4 * 4:g4 * 4 + 4, r0:r0 + nr],
                                                   in_=pb[:, :].rearrange("p (k n) -> p k n", n=128)[:, :, 0:nr])),
                          rd=[pb], wr=[hT])
            kb.dma("sp", hT_d, hT_d.t.rearrange("(c p) t -> p c t", p=128)[:, :, c0:c0 + TT], hT, hT[:])
            rmsnorm_cm(kb, c, hT, gain, hnT, sq, rstd)
            nslab = (AB_PROJ + 511) // 512
            for sl in range(nslab):
                n0 = sl * 512
                ncol = min(512, AB_PROJ - n0)
                wb = wt[sl % 2]
                kb.dma("pool", wb, wb[:, :, 0:ncol], w_in,
                       w_in.t[:, n0:n0 + ncol].rearrange("(c p) n -> p c n", p=128))
                for j in range((ncol + 127) // 128):
                    m = min(128, ncol - j * 128)
                    blk = sl * 4 + j
                    eb = ev[nev % 3]
                    nev += 1
                    for hh in range(2):
                        pb = psb[2 + (blk * 2 + hh) % 4]
                        H = TT // 2
                        for kc in range(16):
                            kb.mm(pb[0:m, 0:H], wb[:, kc, j * 128:j * 128 + m],
                                                           hnT[:, kc, hh * H:(hh + 1) * H],
                                                           start=(kc == 0), stop=(kc == 15), rd=[wb, hnT], wr=[pb])
                        kb.op("act" if hh else "dve",
                              (lambda e: e.copy(out=eb[0:m, hh * H:(hh + 1) * H], in_=pb[0:m, 0:H])) if hh else
                              (lambda e: e.tensor_copy(out=eb[0:m, hh * H:(hh + 1) * H], in_=pb[0:m, 0:H])),
                              rd=[pb], wr=[eb])
                    kb.dma("sp", pT_d, pT_d[blk * 128:blk * 128 + m, c0:c0 + TT], eb, eb[0:m, :])
                if tt == NTT - 1:
                    pb = psb[6 + sl % 2]
                    for kc in range(16):
                        kb.mm(pb[:, 0:ncol], hnT[:, kc, 576:704], wb[:, kc, 0:ncol],
                                                       start=(kc == 0), stop=(kc == 15), rd=[wb, hnT], wr=[pb])
                    pt = ptl[sl % 2]
                    kb.op("act", lambda e: e.copy(out=pt[:, 0:ncol], in_=pb[:, 0:ncol]), rd=[pb], wr=[pt])
                    lo, hi = max(n0, 0), min(n0 + ncol, RW_PROJ)
                    if lo < hi:
                        kb.dma("sp", o_sh, o_sh[0:1, lo:hi], pt, pt[63:64, lo - n0:hi - n0], is_output=True)
                        kb.dma("sp", o_sh, o_sh[1:1 + NS, lo:hi], pt, pt[67:128:4, lo - n0:hi - n0], is_output=True)
                    lo, hi = max(n0, 4352), min(n0 + ncol, 5888)
                    if lo < hi:
                        kb.dma("sp", o_ssdc, o_ssdc[0, :, lo - 4352:hi - 4352], pt, pt[61:64, lo - n0:hi - n0],
                               is_output=True)
                        for t in range(1, 4):
                            kb.dma("sp", o_ssdc, o_ssdc[1:1 + NS, t - 1, lo - 4352:hi - 4352], pt,
                                   pt[64 + t:128:4, lo - n0:hi - n0], is_output=True)


def build():
    kb = KB()
    io = {}
    def din(name, shape):
        io[name] = kb.dram(name, shape, "ExternalInput")
    def dout(name, shape):
        io[name] = kb.dram(name, shape, "ExternalOutput")
    def dscr(name, shape, dt=F32):
        io[name] = kb.dram(name, shape, SCRATCH_KIND, dt)
    din("x", [T, D]); din("mem", [NMEM, D]); din("norm_mem", [2, D]); din("w_xk", [2, D, XA]); din("w_xv", [2, D, XA])
    din("norm_mix", [2, D]); din("w_in_ab", [D, AB_PROJ])
    dout("o_memk", [2, NMEM, XA]); dout("o_memv", [2, NMEM, XA])
    dout("o_sh", [1 + NS, RW_PROJ]); dout("o_ssdc", [1 + NS, 3, 1536])
    dscr("hT", [D, T]); dscr("pT0", [AB_PROJ, T]); dscr("ymix", [2048, T]); din("ssd_D", [16]); din("ssd_norm_w", [1024])
    din("ssd_conv_w", [4, 1536]); din("ssd_conv_b", [1536]); din("ssd_dt_bias", [16]); din("ssd_A_log", [16])
    din("state_ssd_conv", [NS, 3, 1536]); din("state_ssd", [NS, 16, 64, 128])
    dout("o_ssd", [1 + NS, 16, 64, 128])
    for nm, shp in (("rw_mu", [RW_PROJ]), ("rw_w0", [1024]), ("rw_a0", [1024]), ("rw_k_k", [1024]), ("rw_k_a", [1024]),
                    ("rw_r_k", [1024]), ("rw_gn_w", [1024]), ("rw_gn_b", [1024]), ("rw_w_up", [64, 1024]),
                    ("rw_a_up", [64, 1024]), ("rw_g_up", [128, 1024]), ("state_rwkv", [NS, 16, 64, 64]),
                    ("state_rwkv_shift", [NS, RW_PROJ])):
        din(nm, shp)
    dout("o_rw", [1 + NS, 16, 64, 64])
    din("norm_xa", [2, D]); din("norm_ffn", [2, D]); din("w_out_ab", [D, D]); din("w_xq", [2, D, XA]); din("w_xo", [2, XA, D])
    din("ffn_w_in", [2, D, 2 * 5632]); din("ffn_conv_w", [2, 3, 5632]); din("ffn_conv_b", [2, 5632]); din("ffn_w_out", [2, 5632, D])
    din("state_ffn_conv", [2, NS, 2, 5632]); din("cache_mem_k", [2, NS, NMEM, 4, 128]); din("cache_mem_v", [2, NS, NMEM, 4, 128])
    din("w_in_c", [D, 8224])
    dout("o_ffn", [2, 1 + NS, 2, 5632]); dout("o_gdnc", [1 + NS, 3, 6144])
    dscr("pT1", [8224, T])
    din("gdn_conv_w", [4, 6144]); din("gdn_A_log", [16]); din("gdn_dt_bias", [16]); din("gdn_norm_w", [128])
    din("state_gdn_conv", [NS, 3, 6144]); din("state_gdn", [NS, 16, 128, 128]); din("w_out_c", [D, D]); din("norm_final", [D])
    dout("o_gdn", [1 + NS, 16, 128, 128]); dout("o_y", [T, D])
    c = consts(kb)
    if "memkv" not in DEBUG_SKIP:
        stage_memkv(kb, c, io)
    if "A0" not in DEBUG_SKIP:
        stage_A0(kb, c, io)
    def drain(*gens):
        alive = list(gens)
        while alive:
            for g in list(alive):
                try:
                    next(g)
                except StopIteration:
                    alive.remove(g)
    if "Bssd" not in DEBUG_SKIP and "Brw" not in DEBUG_SKIP:
        with kb.scope():
            kb.noscope = True
            drain(stage_B_rwkv(kb, c, io), stage_B_ssd(kb, c, io))
            kb.noscope = False
    else:
        if "Bssd" not in DEBUG_SKIP:
            drain(stage_B_ssd(kb, c, io))
        if "Brw" not in DEBUG_SKIP:
            drain(stage_B_rwkv(kb, c, io))
    if "C0" not in DEBUG_SKIP:
        stage_C(kb, c, io, 0)
    if "Bgdn" not in DEBUG_SKIP:
        drain(stage_B_gdn(kb, c, io))
    if "C1" not in DEBUG_SKIP:
        stage_C(kb, c, io, 1)
    return kb.finish()


def bc(ap, shape):
    return ap.to_broadcast(list(shape))


def stage_B_ssd(kb, c, io):
    nc = kb.nc
    PS, bank, ident = c["PS"], c["bank"], c["ident"]
    pT, ysc = io["pT0"], io["ymix"]
    XB0 = RW_PROJ + 1024
    DT0 = RW_PROJ + 2560
    with kb.scope():
        cw = kb.sb("S_cw", [128, 12, 4])
        cb = kb.sb("S_cb", [128, 12, 1])
        dtb = kb.sb("S_dtb", [64, 16])
        Aneg = kb.sb("S_A", [64, 16])
        with nc.allow_non_contiguous_dma(reason="tiny parameter vectors"):
            for i in range(4):
                kb.dma("sp", cw, cw[:, :, i], io["ssd_conv_w"], io["ssd_conv_w"].t[i].rearrange("(c p) -> p c", p=128))
            kb.dma("sp", cb, cb[:, :, 0], io["ssd_conv_b"], io["ssd_conv_b"].t.rearrange("(c p) -> p c", p=128))
            kb.dma("sp", dtb, dtb[:], io["ssd_dt_bias"], io["ssd_dt_bias"].t.partition_broadcast(64))
            kb.dma("sp", Aneg, Aneg[:], io["ssd_A_log"], io["ssd_A_log"].t.partition_broadcast(64))
        kb.op("act", lambda e: e.activation(out=Aneg[:], in_=Aneg[:], func=AF.Exp), rd=[Aneg], wr=[Aneg])
        kb.op("dve", lambda e: e.tensor_scalar(out=Aneg[:], in0=Aneg[:], scalar1=-1.0, scalar2=None, op0=ALU.mult),
              rd=[Aneg], wr=[Aneg])
        Dsk = kb.sb("S_Dsk", [128, 8, 1])
        nw = kb.sb("S_nw", [128, 8, 1])
        with nc.allow_non_contiguous_dma(reason="tiny parameter vectors"):
            dv = io["ssd_D"].t.rearrange("(j hh) -> hh j", hh=2)
            kb.dma("sp", Dsk, Dsk[0:64, :, 0], io["ssd_D"], dv[0].partition_broadcast(64))
            kb.dma("sp", Dsk, Dsk[64:128, :, 0], io["ssd_D"], dv[1].partition_broadcast(64))
            kb.dma("sp", nw, nw[:, :, 0], io["ssd_norm_w"], io["ssd_norm_w"].t.rearrange("(c p) -> p c", p=128))
        one128 = kb.sb("S_one128", [128, 128], BF16)
        kb.op("pool", lambda e: e.memset(one128[:], 1.0), wr=[one128])
        zt = [kb.sb("S_zt0", [128, 8, 64])] * 2
        yg = None
        ysq = None
        ysqb = kb.sb("S_ysqb", [128, 8, 64], BF16)
        rsd = kb.sb("S_rsd", [128, 2, 64])
        triU = kb.sb("S_triU", [64, 64])
        kb.op("pool", lambda e: e.memset(triU[:], 1.0), wr=[triU])
        kb.op("pool", lambda e: e.affine_select(out=triU[:], in_=triU[:], pattern=[[1, 64]], base=0,
                                                channel_multiplier=-1, compare_op=ALU.is_ge, fill=0.0),
              rd=[triU], wr=[triU])
        ones64 = kb.sb("S_ones", [128, 128])
        kb.op("pool", lambda e: e.memset(ones64[:], 1.0), wr=[ones64])

        raw = [kb.sb("S_raw0", [128, 12, 67])] * 2
        dtr = [kb.sb(f"S_dtr{i}", [16, 64]) for i in range(2)]
        t1 = kb.sb("S_t1", [128, 12, 64])
        t2 = kb.sb("S_t2", [128, 12, 64])
        xbc = kb.sb("S_xbc", [128, 12, 64])
        yg, ysq = t2, t1
        dtT = kb.sb("S_dtT", [64, 16])
        dA = kb.sb("S_dA", [64, 16])
        acs = kb.sb("S_acs", [64, 16])
        G = kb.sb("S_G", [128, 16, 64])
        kb.op("pool", lambda e: e.memset(G[:], 0.0), wr=[G])
        Ebc = kb.sb("S_Ebc", [128, 16, 64])
        Lm = kb.sb("S_Lm", [64, 16, 64])
        X = kb.sb("S_X", [64, 16, 64], BF16)
        Xd = kb.sb("S_Xd", [64, 16, 64], BF16)
        Btm = kb.sb("S_Btm", [64, 2, 128], BF16)
        MT = kb.sb("S_MT", [64, 16, 64], BF16)
        Cdec = kb.sb("S_Cdec", [128, 16, 64], BF16)
        hT = kb.sb("S_hT", [128, 16, 64], BF16)
        bcb = kb.sb("S_bcb", [128, 4, 64], BF16)
        hn = kb.sb("S_hn", [64, 16, 128])
        yT = [kb.sb("S_yT0", [128, 8, 64])] * 2

        units = [(64, 64 * ch, ch, None) for ch in range(SEQ // 64)] + [(LS, SEQ + LS * s, 0, s) for s in range(NS)]
        if DEBUG_UNITS is not None:
            units = [units[i] for i in DEBUG_UNITS]
        for ui, (C, t0, ch, s) in enumerate(units):
            rw, dr = raw[ui % 2], dtr[ui % 2]
            src = pT.t[XB0:XB0 + 1536, :].rearrange("(c p) t -> p c t", p=128)
            if s is None and ch > 0:
                kb.dma("sp", rw, rw[:, :, 0:3 + C], pT, src[:, :, t0 - 3:t0 + C])
            else:
                kb.dma("sp", rw, rw[:, :, 3:3 + C], pT, src[:, :, t0:t0 + C])
                if s is None:
                    kb.op("pool", lambda e: e.memset(rw[:, :, 0:3], 0.0), wr=[rw])
                else:
                    with nc.allow_non_contiguous_dma(reason="conv state transposing load (small)"):
                        for cc in range(12):
                            kb.dma("sp", rw, rw[:, cc, 0:3], io["state_ssd_conv"],
                                   io["state_ssd_conv"].t[s, :, cc * 128:(cc + 1) * 128].rearrange("r p -> p r"))
            kb.dma("sp", dr, dr[:, 0:C], pT, pT[DT0:DT0 + 16, t0:t0 + C])
            zb = zt[ui % 2]
            kb.dma("sp", zb, zb[:, :, 0:C], pT, pT.t[RW_PROJ:RW_PROJ + 1024, :].rearrange("(c p) t -> p c t", p=128)[:, :, t0:t0 + C])
            if s is None and ch == 0:
                kb.op("pool", lambda e: e.memset(hn[:], 0.0), wr=[hn])
            elif s is not None:
                kb.dma("sp", hn, hn[:], io["state_ssd"], io["state_ssd"].t[s].rearrange("h p n -> p h n"))
            yield
            if 1 not in DEBUG_OMIT:
                kb.op("dve", lambda e: e.tensor_tensor(out=t1[:, :, 0:C], in0=rw[:, :, 0:C], in1=bc(cw[:, :, 0:1], [128, 12, C]),
                                                       op=ALU.mult), rd=[rw, cw], wr=[t1])
                for i in range(1, 4):
                    kb.op("pool", lambda e: e.tensor_tensor(out=t2[:, :, 0:C], in0=rw[:, :, i:i + C],
                                                            in1=bc(cw[:, :, i:i + 1], [128, 12, C]), op=ALU.mult),
                          rd=[rw, cw], wr=[t2])
                    kb.op("dve", lambda e: e.tensor_tensor(out=t1[:, :, 0:C], in0=t1[:, :, 0:C], in1=t2[:, :, 0:C], op=ALU.add),
                          rd=[t1, t2], wr=[t1])
                kb.op("dve", lambda e: e.tensor_tensor(out=t1[:, :, 0:C], in0=t1[:, :, 0:C], in1=bc(cb[:, :, 0:1], [128, 12, C]),
                                                       op=ALU.add), rd=[t1, cb], wr=[t1])
                kb.op("act", lambda e: e.activation(out=xbc[:, :, 0:C], in_=t1[:, :, 0:C], func=AF.Silu), rd=[t1], wr=[xbc])
            yield
            if 2 not in DEBUG_OMIT:
                kb.tr(out=bank[0][0:C, 0:16], in_=dr[0:16, 0:C], identity=ident[0:16, 0:16],
                      rd=[dr, ident], wr=[bank[0]])
                kb.op("dve", lambda e: e.tensor_tensor(out=dtT[0:C, :], in0=bank[0][0:C, 0:16], in1=dtb[0:C, :], op=ALU.add),
                      rd=[bank[0], dtb], wr=[dtT])
                kb.op("act", lambda e: e.activation(out=dtT[0:C, :], in_=dtT[0:C, :], func=AF.Exp), rd=[dtT], wr=[dtT])
                kb.op("act", lambda e: e.activation(out=dtT[0:C, :], in_=dtT[0:C, :], func=AF.Ln, bias=1.0), rd=[dtT], wr=[dtT])
                kb.op("dve", lambda e: e.tensor_tensor(out=dA[0:C, :], in0=dtT[0:C, :], in1=Aneg[0:C, :], op=ALU.mult),
                      rd=[dtT, Aneg], wr=[dA])
            else:
                kb.op("dve", lambda e: e.memset(dA[:], -0.1), wr=[dA])
            yield
            kb.op("dve", lambda e: e.tensor_copy(out=G[0:C, :, 0:C], in_=bc(triU[0:C, None, 0:C], [C, 16, C])),
                  rd=[triU], wr=[G])
            kb.op("dve", lambda e: e.tensor_tensor(out=G[0:C, :, 0:C], in0=G[0:C, :, 0:C],
                                                   in1=bc(dA[0:C, :, None], [C, 16, C]), op=ALU.mult),
                  rd=[G, dA], wr=[G])
            kb.mm(bank[0][0:C, 16:32], triU[0:C, 0:C], dA[0:C, :], start=True, stop=True,
                  rd=[triU, dA], wr=[bank[0]])
            kb.op("dve", lambda e: e.tensor_copy(out=acs[0:C, :], in_=bank[0][0:C, 16:32]), rd=[bank[0]], wr=[acs])
            hpb = 512 // C
            nb = (16 + hpb - 1) // hpb
            for b in range(nb):
                h0, h1 = b * hpb, min(16, (b + 1) * hpb)
                hq = max(1, 256 // C)
                for ha in range(h0, h1, hq):
                    hb = min(h1, ha + hq)
                    kb.mm(bank[1 + b][:, (ha - h0) * C:(hb - h0) * C], ones64[:, :],
                                                   G[:, ha:hb, 0:C], start=True, stop=True,
                          rd=[ones64, G], wr=[bank[1 + b]])
            for b in range(nb):
                h0, h1 = b * hpb, min(16, (b + 1) * hpb)
                pv = bank[1 + b][:, 0:(h1 - h0) * C].rearrange("p (h l) -> p h l", l=C)
                if 31 not in DEBUG_OMIT:
                    kb.op("act", lambda e: e.activation(out=Ebc[:, h0:h1, 0:C], in_=pv, func=AF.Exp),
                          rd=[bank[1 + b]], wr=[Ebc])
                if 32 not in DEBUG_OMIT:
                  kb.op("dve", lambda e: e.tensor_tensor(out=Lm[0:C, h0:h1, 0:C], in0=pv[0:C],
                                                       in1=bc(acs[0:C, h0:h1, None], [C, h1 - h0, C]), op=ALU.subtract),
                      rd=[bank[1 + b], acs], wr=[Lm])
            kb.op("dve", lambda e: e.tensor_scalar(out=Lm[0:C, :, 0:C], in0=Lm[0:C, :, 0:C], scalar1=0.0, scalar2=None,
                                                   op0=ALU.min), rd=[Lm], wr=[Lm])
            kb.op("act", lambda e: e.activation(out=Lm[0:C, :, 0:C], in_=Lm[0:C, :, 0:C], func=AF.Exp), rd=[Lm], wr=[Lm])
            kb.op("dve", lambda e: e.tensor_tensor(out=Lm[0:C, :, 0:C], in0=Lm[0:C, :, 0:C],
                                                   in1=bc(triU[0:C, None, 0:C], [C, 16, C]), op=ALU.mult),
                  rd=[Lm, triU], wr=[Lm])
            yield
            for blk in range(8):
                b = 3 + blk // 4
                kb.tr(out=bank[b][0:C, (blk % 4) * 128:(blk % 4 + 1) * 128],
                                                  in_=xbc[:, blk, 0:C], identity=ident[:], rd=[xbc, ident], wr=[bank[b]])
            for b in range(2):
                kb.op("dve", lambda e: e.tensor_tensor(out=X[0:C, 8 * b:8 * b + 8, :],
                                                       in0=bank[3 + b][0:C, :].rearrange("p (h q) -> p h q", q=64),
                                                       in1=bc(dtT[0:C, 8 * b:8 * b + 8, None], [C, 8, 64]), op=ALU.mult),
                      rd=[bank[3 + b], dtT], wr=[X])
            for g in range(2):
                kb.tr(out=bank[5][0:C, g * 128:(g + 1) * 128], in_=xbc[:, 8 + g, 0:C],
                                                  identity=ident[:], rd=[xbc, ident], wr=[bank[5]])
            kb.op("act", lambda e: e.copy(out=Btm[0:C, :, :], in_=bank[5][0:C, 0:256].rearrange("p (g n) -> p g n", n=128)),
                  rd=[bank[5]], wr=[Btm])
            kb.op("dve", lambda e: e.tensor_tensor(out=Xd[0:C], in0=X[0:C], in1=bc(Lm[0:C, :, C - 1:C], [C, 16, 64]),
                                                   op=ALU.mult), rd=[X, Lm], wr=[Xd])
            yield
            kb.op("act", lambda e: e.copy(out=bcb[:, :, 0:C], in_=xbc[:, 8:12, 0:C]), rd=[xbc], wr=[bcb])
            for g in range(2):
                kb.mm(bank[5][0:C, 256 + g * 64:256 + g * 64 + C], bcb[:, g, 0:C],
                                               bcb[:, 2 + g, 0:C], start=True, stop=True, rd=[bcb], wr=[bank[5]])
            for g in range(2):
                kb.op("dve", lambda e: e.tensor_tensor(
                    out=MT[0:C, 8 * g:8 * g + 8, 0:C], in0=Lm[0:C, 8 * g:8 * g + 8, 0:C],
                    in1=bc(bank[5][0:C, None, 256 + g * 64:256 + g * 64 + C], [C, 8, C]), op=ALU.mult),
                    rd=[Lm, bank[5]], wr=[MT])
                kb.op("pool", lambda e: e.tensor_tensor(
                    out=Cdec[:, 8 * g:8 * g + 8, 0:C], in0=Ebc[:, 8 * g:8 * g + 8, 0:C],
                    in1=bc(xbc[:, 10 + g, None, 0:C], [128, 8, C]), op=ALU.mult), rd=[Ebc, xbc], wr=[Cdec])
            yield
            for h in range(16):
                b = 6 + h // 8
                kb.tr(out=bank[b][:, (h % 8) * 64:(h % 8 + 1) * 64], in_=hn[:, h, :],
                                                  identity=ident[0:64, 0:64], rd=[hn, ident], wr=[bank[b]])
            for b in range(2):
                kb.op("act" if b else "dve",
                      (lambda e: e.copy(out=hT[:, 8 * b:8 * b + 8, :], in_=bank[6 + b][:, :].rearrange("p (h q) -> p h q", q=64)))
                      if b else
                      (lambda e: e.tensor_copy(out=hT[:, 8 * b:8 * b + 8, :], in_=bank[6 + b][:, :].rearrange("p (h q) -> p h q", q=64))),
                      rd=[bank[6 + b]], wr=[hT])
            yield
            for h in range(16):
                po = (h % 2) * 64
                o = bank[0][po:po + 64, (h // 2) * 64:(h // 2) * 64 + C]
                kb.mm(o, X[0:C, h, :], MT[0:C, h, 0:C], start=True, stop=False,
                      rd=[X, MT], wr=[bank[0]])
                kb.mm(o, hT[:, h, :], Cdec[:, h, 0:C], start=False, stop=True,
                      rd=[hT, Cdec], wr=[bank[0]])
            yb = yT[ui % 2]
            kb.op("dve", lambda e: e.tensor_tensor(out=yg[:, 0:8, 0:C], in0=xbc[:, 0:8, 0:C], in1=bc(Dsk[:], [128, 8, C]), op=ALU.mult),
                  rd=[xbc, Dsk], wr=[yg])
            kb.op("dve", lambda e: e.tensor_tensor(out=yg[:, 0:8, 0:C], in0=yg[:, 0:8, 0:C],
                                                   in1=bank[0][:, :].rearrange("p (k l) -> p k l", l=64)[:, :, 0:C], op=ALU.add),
                  rd=[yg, bank[0]], wr=[yg])
            kb.op("act", lambda e: e.activation(out=ysq[:, 0:8, 0:C], in_=zb[:, :, 0:C], func=AF.Silu), rd=[zb], wr=[ysq])
            kb.op("dve", lambda e: e.tensor_tensor(out=yg[:, 0:8, 0:C], in0=yg[:, 0:8, 0:C], in1=ysq[:, 0:8, 0:C], op=ALU.mult), rd=[yg, ysq], wr=[yg])
            kb.op("act", lambda e: e.activation(out=ysqb[:, :, 0:C], in_=yg[:, 0:8, 0:C], func=AF.Square), rd=[yg], wr=[ysqb])
            for g in range(2):
                for q in range(4):
                    kb.mm(bank[5][:, g * 64:g * 64 + C], one128[:], ysqb[:, 4 * g + q, 0:C], start=(q == 0), stop=(q == 3),
                          rd=[one128, ysqb], wr=[bank[5]])
            kb.op("dve", lambda e: e.tensor_scalar(out=rsd[:, :, 0:C], in0=bank[5][:, 0:128].rearrange("p (g l) -> p g l", l=64)[:, :, 0:C],
                                                   scalar1=1.0 / 512, scalar2=EPS, op0=ALU.mult, op1=ALU.add), rd=[bank[5]], wr=[rsd])
            kb.op("act", lambda e: e.activation(out=rsd[:, :, 0:C], in_=rsd[:, :, 0:C], func=AF.Sqrt), rd=[rsd], wr=[rsd])
            kb.op("dve", lambda e: e.reciprocal(out=rsd[:, :, 0:C], in_=rsd[:, :, 0:C]), rd=[rsd], wr=[rsd])
            for g in range(2):
                kb.op("dve", lambda e: e.tensor_tensor(out=yg[:, 4 * g:4 * g + 4, 0:C], in0=yg[:, 4 * g:4 * g + 4, 0:C],
                                                       in1=bc(rsd[:, g:g + 1, 0:C], [128, 4, C]), op=ALU.mult), rd=[yg, rsd], wr=[yg])
            kb.op("dve", lambda e: e.tensor_tensor(out=yb[:, :, 0:C], in0=yg[:, 0:8, 0:C], in1=bc(nw[:], [128, 8, C]), op=ALU.mult),
                  rd=[yg, nw], wr=[yb])
            kb.dma("pool", ysc, ysc.t[1024:2048, :].rearrange("(k p) t -> p k t", p=128)[:, :, t0:t0 + C], yb, yb[:, :, 0:C])
            yield
            for h in range(16):
                b = 1 + h // 4
                kb.mm(bank[b][0:64, (h % 4) * 128:(h % 4 + 1) * 128], Xd[0:C, h, :],
                                               Btm[0:C, h // 8, :], start=True, stop=True, rd=[Xd, Btm], wr=[bank[b]])
            kb.op("dve", lambda e: e.tensor_tensor(out=hn[:], in0=hn[:], in1=bc(Ebc[0:64, :, C - 1:C], [64, 16, 128]),
                                                   op=ALU.mult), rd=[hn, Ebc], wr=[hn])
            for b in range(4):
                kb.op("dve", lambda e: e.tensor_tensor(out=hn[:, 4 * b:4 * b + 4, :], in0=hn[:, 4 * b:4 * b + 4, :],
                                                       in1=bank[1 + b][0:64, :].rearrange("p (h n) -> p h n", n=128),
                                                       op=ALU.add), rd=[hn, bank[1 + b]], wr=[hn])
            if s is not None or ch == SEQ // 64 - 1 or DEBUG_UNITS is not None:
                row = 0 if s is None else 1 + s
                kb.dma("pool", io["o_ssd"], io["o_ssd"].t[row].rearrange("h p n -> p h n"), hn, hn[:], is_output=True)
            yield


def hd_param(kb, name, src, n=16):
    t = kb.sb(name, [64, n, 1])
    with kb.nc.allow_non_contiguous_dma(reason="tiny parameter vectors"):
        kb.dma("sp", t, t[:, :, 0], src, src.t.rearrange("(h k) -> k h", k=64))
    return t


def stage_B_rwkv(kb, c, io):
    nc = kb.nc
    PS, bank, ident = c["PS"], c["bank"], c["ident"]
    pT, ymix = io["pT0"], io["ymix"]
    with kb.scope():
        muH = kb.sb("R_muH", [64, 48, 1])
        muW = kb.sb("R_muW", [64, 2, 1])
        muG = kb.sb("R_muG", [128, 1])
        with nc.allow_non_contiguous_dma(reason="tiny parameter vectors"):
            kb.dma("sp", muH, muH[:, :, 0], io["rw_mu"], io["rw_mu"].t[0:3072].rearrange("(a k) -> k a", k=64))
            kb.dma("sp", muW, muW[:, :, 0], io["rw_mu"], io["rw_mu"].t[3072:3200].rearrange("(a k) -> k a", k=64))
            kb.dma("sp", muG, muG[:, :], io["rw_mu"], io["rw_mu"].t[3200:3328].rearrange("(k o) -> k o", o=1))
        w0, a0, kkp, kap, rkp, gnw, gnb = (hd_param(kb, "R_" + n, io[n]) for n in
                                           ("rw_w0", "rw_a0", "rw_k_k", "rw_k_a", "rw_r_k", "rw_gn_w", "rw_gn_b"))
        Wup = kb.sb("R_Wup", [64, 1024], BF16)
        Aup = kb.sb("R_Aup", [64, 1024], BF16)
        Gup = kb.sb("R_Gup", [128, 1024], BF16)
        kb.dma("pool", Wup, Wup[:], io["rw_w_up"], io["rw_w_up"][:, :])
        kb.dma("pool", Aup, Aup[:], io["rw_a_up"], io["rw_a_up"][:, :])
        kb.dma("pool", Gup, Gup[:], io["rw_g_up"], io["rw_g_up"][:, :])
        inclU = kb.sb("R_inclU", [64, 64])
        strU = kb.sb("R_strU", [64, 64])
        strL = kb.sb("R_strL", [64, 64])
        for m, pat, cmul, cmp_ in ((inclU, 1, -1, ALU.is_ge), (strU, 1, -1, ALU.is_gt), (strL, -1, 1, ALU.is_gt)):
            kb.op("pool", lambda e: e.memset(m[:], 1.0), wr=[m])
            kb.op("pool", lambda e: e.affine_select(out=m[:], in_=m[:], pattern=[[pat, 64]], base=0,
                                                    channel_multiplier=cmul, compare_op=cmp_, fill=0.0), rd=[m], wr=[m])
        one64 = kb.sb("R_one64", [64, 64], BF16)
        kb.op("pool", lambda e: e.memset(one64[:], 1.0), wr=[one64])

        rawH = [kb.sb("R_rawH0", [64, 48, 65])] * 2
        rawW = [kb.sb(f"R_rawW{i}", [64, 2, 65]) for i in range(2)]
        rawG = [kb.sb(f"R_rawG{i}", [128, 65]) for i in range(2)]
        PH = kb.sb("R_PH", [64, 48, 64])
        PW = kb.sb("R_PW", [64, 2, 64])
        PG = kb.sb("R_PG", [128, 64])
        T16 = lambda nm: kb.sb("R_" + nm, [64, 16, 64])
        B16 = lambda nm: kb.sb("R_" + nm, [64, 16, 64], BF16)
        th = kb.sb("R_th", [64, 64], BF16)
        adb = kb.sb("R_adb", [64, 64], BF16)
        sgd = kb.sb("R_sgd", [128, 64], BF16)
        t1, lw, asig, gsb, kkr, rn, kk, kfin, bvec = (T16(n) for n in ("t1", "lw", "asig", "gsb", "kkr", "rn", "kk", "kfin", "bvec"))
        cs = [T16("cs0"), T16("cs1")]
        Gm, Gi = (T16(n) for n in ("Gm", "Gi"))
        Bh, Kh = lw, asig
        sq, rt, at, bt, kt = (B16(n) for n in ("sq", "rt", "at", "bt", "kt"))
        MT, Mq, LakT, RBT, RKT = (B16(n) for n in ("MT", "Mq", "LakT", "RBT", "RKT"))
        Pp, Qq, Rr = [B16("P0"), B16("P1")], [B16("Q0"), B16("Q1")], [B16("R0"), B16("R1")]
        Vtm, Bhtm, Khtm, Wsb, Usb, Zb, ysbb = (B16(n) for n in ("Vtm", "Bhtm", "Khtm", "Wsb", "Usb", "Zb", "ysbb"))
        Z = T16("Z")
        Sv = gsb
        Gp = kkr
        rk = sq
        ysb, yc = cs[0], cs[1]
        yn = Gi
        yout = [t1, t1]

        def wide(b0, rows, n, C):
            return PS.t[0:rows, b0 * 512:b0 * 512 + n * C].rearrange("p (j l) -> p j l", l=C)

        def wb(b0, ncols):
            return [bank[b0 + i] for i in range((ncols + 511) // 512)]

        def mm16(b0, rows, C_out, lfn, rfn, rd):
            for h in range(16):
                kb.mm(PS.t[0:rows, b0 * 512 + h * C_out:b0 * 512 + (h + 1) * C_out], lfn(h), rfn(h), rd=rd, wr=wb(b0, 16 * C_out))

        def mmones(b0, src, C):
            for q in range((16 * C + 511) // 512):
                hq = 512 // C if C * 16 > 512 else 16
                kb.mm(PS.t[0:64, b0 * 512 + q * 512:b0 * 512 + q * 512 + hq * C], one64[:], src[:, q * hq:(q + 1) * hq, 0:C],
                      rd=[one64, src], wr=wb(b0, 16 * C))

        units = [(64, 64 * ch, ch, None) for ch in range(SEQ // 64)] + [(LS, SEQ + LS * s, 0, s) for s in range(NS)]
        if DEBUG_UNITS is not None:
            units = [units[i] for i in DEBUG_UNITS]
        for ui, (C, t0, ch, s) in enumerate(units):
            rH, rW, rG = rawH[ui % 2], rawW[ui % 2], rawG[ui % 2]
            srcH = pT.t[0:3072, :].rearrange("(a k) t -> k a t", k=64)
            srcW = pT.t[3072:3200, :].rearrange("(a k) t -> k a t", k=64)
            srcG = pT.t[3200:3328, :]
            lo = 0 if (s is None and ch > 0) else 1
            kb.dma("sp", rH, rH[:, :, lo:1 + C], pT, srcH[:, :, t0 - 1 + lo:t0 + C])
            kb.dma("sp", rW, rW[:, :, lo:1 + C], pT, srcW[:, :, t0 - 1 + lo:t0 + C])
            kb.dma("sp", rG, rG[:, lo:1 + C], pT, srcG[:, t0 - 1 + lo:t0 + C])
            if lo == 1:
                if s is None:
                    kb.op("pool", lambda e: e.memset(rH[:, :, 0:1], 0.0), wr=[rH])
                    kb.op("pool", lambda e: e.memset(rW[:, :, 0:1], 0.0), wr=[rW])
                    kb.op("pool", lambda e: e.memset(rG[:, 0:1], 0.0), wr=[rG])
                else:
                    st = io["state_rwkv_shift"]
                    with nc.allow_non_contiguous_dma(reason="shift state transposing load (small)"):
                        kb.dma("sp", rH, rH[:, :, 0], st, st.t[s, 0:3072].rearrange("(a k) -> k a", k=64))
                        kb.dma("sp", rW, rW[:, :, 0], st, st.t[s, 3072:3200].rearrange("(a k) -> k a", k=64))
                        kb.dma("sp", rG, rG[:, 0:1], st, st.t[s, 3200:3328].rearrange("(k o) -> k o", o=1))
            yield
            if s is None and ch == 0:
                kb.op("pool", lambda e: e.memset(Z[:], 0.0), wr=[Z])
            elif s is not None:
                kb.dma("sp", Sv, Sv[:], io["state_rwkv"], io["state_rwkv"].t[s].rearrange("h v k -> v h k"))
                for h in range(16):
                    kb.tr(out=PS.t[0:64, 3072 + h * 64:3072 + (h + 1) * 64], in_=Sv[:, h, :], identity=ident[0:64, 0:64],
                          rd=[Sv, ident], wr=[bank[6], bank[7]])
                kb.op("act", lambda e: e.copy(out=Z[:], in_=wide(6, 64, 16, 64)), rd=[bank[6], bank[7]], wr=[Z])
            yield
            for raw_, P_, mu_, n in ((rH, PH, muH, 48), (rW, PW, muW, 2)):
                kb.op("dve", lambda e: e.tensor_tensor(out=P_[:, 0:n, 0:C], in0=raw_[:, 0:n, 0:C], in1=raw_[:, 0:n, 1:1 + C], op=ALU.subtract),
                      rd=[raw_], wr=[P_])
                kb.op("dve", lambda e: e.tensor_tensor(out=P_[:, 0:n, 0:C], in0=P_[:, 0:n, 0:C], in1=bc(mu_[:], [64, n, C]), op=ALU.mult),
                      rd=[P_, mu_], wr=[P_])
                kb.op("pool", lambda e: e.tensor_tensor(out=P_[:, 0:n, 0:C], in0=P_[:, 0:n, 0:C], in1=raw_[:, 0:n, 1:1 + C], op=ALU.add),
                      rd=[P_, raw_], wr=[P_])
            kb.op("dve", lambda e: e.tensor_tensor(out=PG[:, 0:C], in0=rG[:, 0:C], in1=rG[:, 1:1 + C], op=ALU.subtract), rd=[rG], wr=[PG])
            kb.op("dve", lambda e: e.tensor_scalar(out=PG[:, 0:C], in0=PG[:, 0:C], scalar1=muG[:, 0:1], scalar2=None, op0=ALU.mult),
                  rd=[PG, muG], wr=[PG])
            kb.op("dve", lambda e: e.tensor_tensor(out=PG[:, 0:C], in0=PG[:, 0:C], in1=rG[:, 1:1 + C], op=ALU.add), rd=[PG, rG], wr=[PG])
            Pr, Pk, Pvv = PH[:, 0:16, 0:C], PH[:, 16:32, 0:C], PH[:, 32:48, 0:C]
            kb.op("act", lambda e: e.copy(out=Zb[:], in_=Z[:]), rd=[Z], wr=[Zb])
            yield
            kb.op("act", lambda e: e.activation(out=th[:, 0:C], in_=PW[:, 0, 0:C], func=AF.Tanh), rd=[PW], wr=[th])
            kb.op("act", lambda e: e.activation(out=sgd[:, 0:C], in_=PG[:, 0:C], func=AF.Sigmoid), rd=[PG], wr=[sgd])
            mm16(0, 64, C, lambda h: Wup[:, h * 64:(h + 1) * 64], lambda h: th[:, 0:C], [Wup, th])
            kb.op("act", lambda e: e.copy(out=adb[:, 0:C], in_=PW[:, 1, 0:C]), rd=[PW], wr=[adb])
            mm16(2, 64, C, lambda h: Aup[:, h * 64:(h + 1) * 64], lambda h: adb[:, 0:C], [Aup, adb])
            mm16(4, 64, C, lambda h: Gup[:, h * 64:(h + 1) * 64], lambda h: sgd[:, 0:C], [Gup, sgd])
            V8 = lambda t: t[:, :, 0:C]
            kb.op("dve", lambda e: e.tensor_tensor(out=V8(t1), in0=wide(0, 64, 16, C), in1=bc(w0[:], [64, 16, C]), op=ALU.add),
                  rd=wb(0, 16 * C) + [w0], wr=[t1])
            kb.op("act", lambda e: e.activation(out=V8(lw), in_=V8(t1), func=AF.Sigmoid), rd=[t1], wr=[lw])
            kb.op("dve", lambda e: e.tensor_scalar(out=V8(lw), in0=V8(lw), scalar1=-0.6065306597126334, scalar2=None, op0=ALU.mult),
                  rd=[lw], wr=[lw])
            kb.op("dve", lambda e: e.tensor_tensor(out=V8(t1), in0=wide(2, 64, 16, C), in1=bc(a0[:], [64, 16, C]), op=ALU.add),
                  rd=wb(2, 16 * C) + [a0], wr=[t1])
            kb.op("act", lambda e: e.activation(out=V8(asig), in_=V8(t1), func=AF.Sigmoid), rd=[t1], wr=[asig])
            kb.op("act", lambda e: e.copy(out=V8(gsb), in_=wide(4, 64, 16, C)), rd=wb(4, 16 * C), wr=[gsb])
            yield
            kb.op("dve", lambda e: e.tensor_tensor(out=V8(kkr), in0=Pk, in1=bc(kkp[:], [64, 16, C]), op=ALU.mult), rd=[PH, kkp], wr=[kkr])
            kb.op("act", lambda e: e.activation(out=V8(sq), in_=V8(kkr), func=AF.Square), rd=[kkr], wr=[sq])
            mmones(6, sq, C)
            kb.op("dve", lambda e: e.tensor_scalar(out=V8(rn), in0=wide(6, 64, 16, C), scalar1=1e-6, scalar2=None, op0=ALU.add),
                  rd=wb(6, 16 * C), wr=[rn])
            kb.op("act", lambda e: e.activation(out=V8(rn), in_=V8(rn), func=AF.Sqrt), rd=[rn], wr=[rn])
            kb.op("dve", lambda e: e.reciprocal(out=V8(rn), in_=V8(rn)), rd=[rn], wr=[rn])
            kb.op("dve", lambda e: e.tensor_tensor(out=V8(kk), in0=V8(kkr), in1=V8(rn), op=ALU.mult), rd=[kkr, rn], wr=[kk])
            yield
            kb.op("dve", lambda e: e.tensor_scalar(out=V8(t1), in0=V8(asig), scalar1=-1.0, scalar2=None, op0=ALU.add), rd=[asig], wr=[t1])
            kb.op("dve", lambda e: e.tensor_tensor(out=V8(t1), in0=V8(t1), in1=bc(kap[:], [64, 16, C]), op=ALU.mult), rd=[t1, kap], wr=[t1])
            kb.op("dve", lambda e: e.tensor_scalar(out=V8(t1), in0=V8(t1), scalar1=1.0, scalar2=None, op0=ALU.add), rd=[t1], wr=[t1])
            kb.op("dve", lambda e: e.tensor_tensor(out=V8(kfin), in0=Pk, in1=V8(t1), op=ALU.mult), rd=[PH, t1], wr=[kfin])
            kb.op("pool", lambda e: e.tensor_tensor(out=V8(bvec), in0=V8(kk), in1=V8(asig), op=ALU.mult), rd=[kk, asig], wr=[bvec])
            yield
            cur = lw
            sft, k2 = 1, 0
            while sft < C:
                nx = cs[k2 % 2]
                kb.op("pool", lambda e: e.tensor_copy(out=nx[:, :, 0:sft], in_=cur[:, :, 0:sft]), rd=[cur], wr=[nx])
                kb.op("dve", lambda e: e.tensor_tensor(out=nx[:, :, sft:C], in0=cur[:, :, sft:C], in1=cur[:, :, 0:C - sft], op=ALU.add),
                      rd=[cur], wr=[nx])
                cur = nx
                sft *= 2
                k2 += 1
            csum = cur
            kb.op("act", lambda e: e.activation(out=V8(Gm), in_=V8(csum), func=AF.Exp), rd=[csum], wr=[Gm])
            kb.op("act", lambda e: e.activation(out=V8(Gi), in_=V8(csum), func=AF.Exp, scale=-1.0), rd=[csum], wr=[Gi])
            kb.op("dve", lambda e: e.tensor_tensor(out=V8(t1), in0=V8(csum), in1=V8(lw), op=ALU.subtract), rd=[csum, lw], wr=[t1])
            kb.op("act", lambda e: e.activation(out=V8(Gp), in_=V8(t1), func=AF.Exp), rd=[t1], wr=[Gp])
            kb.op("dve", lambda e: e.tensor_tensor(out=V8(rt), in0=Pr, in1=V8(Gm), op=ALU.mult), rd=[PH, Gm], wr=[rt])
            kb.op("dve", lambda e: e.scalar_tensor_tensor(out=V8(at), in0=V8(kk), scalar=-1.0, in1=V8(Gp), op0=ALU.mult, op1=ALU.mult),
                  rd=[kk, Gp], wr=[at])
            kb.op("pool", lambda e: e.tensor_tensor(out=V8(bt), in0=V8(bvec), in1=V8(Gi), op=ALU.mult), rd=[bvec, Gi], wr=[bt])
            kb.op("pool", lambda e: e.tensor_tensor(out=V8(kt), in0=V8(kfin), in1=V8(Gi), op=ALU.mult), rd=[kfin, Gi], wr=[kt])
            GCb = lambda n: bc(Gm[:, :, C - 1:C], [64, 16, n])
            kb.op("dve", lambda e: e.tensor_tensor(out=V8(Bh), in0=V8(bt), in1=GCb(C), op=ALU.mult), rd=[bt, Gm], wr=[Bh])
            kb.op("dve", lambda e: e.tensor_tensor(out=V8(Kh), in0=V8(kt), in1=GCb(C), op=ALU.mult), rd=[kt, Gm], wr=[Kh])
            yield
            for dst, sfn, srcb, b0 in ((Vtm, lambda h: PH[:, 32 + h, 0:C], PH, 0), (Bhtm, lambda h: Bh[:, h, 0:C], Bh, 2),
                                       (Khtm, lambda h: Kh[:, h, 0:C], Kh, 4)):
                for h in range(16):
                    kb.tr(out=PS.t[0:C, b0 * 512 + h * 64:b0 * 512 + (h + 1) * 64], in_=sfn(h), identity=ident[0:64, 0:64],
                          rd=[srcb, ident], wr=wb(b0, 1024))
                kb.op("act" if b0 == 2 else "dve",
                      (lambda e: e.copy(out=dst[0:C], in_=wide(b0, C, 16, 64))) if b0 == 2 else
                      (lambda e: e.tensor_copy(out=dst[0:C], in_=wide(b0, C, 16, 64))), rd=wb(b0, 1024), wr=[dst])
            yield
            kinds = ((MT, bt, at, strU), (Mq, at, bt, strL), (LakT, kt, at, strU), (RBT, bt, rt, inclU), (RKT, kt, rt, inclU))
            for ki, (dst, lt, rt_, msk) in enumerate(kinds):
                b0 = (ki % 4) * 2
                mm16(b0, C, C, lambda h: lt[:, h, 0:C], lambda h: rt_[:, h, 0:C], [lt, rt_])
                kb.op("dve", lambda e: e.tensor_tensor(out=dst[0:C, :, 0:C], in0=wide(b0, C, 16, C),
                                                       in1=bc(msk[0:C, None, 0:C], [C, 16, C]), op=ALU.mult),
                      rd=wb(b0, 16 * C) + [msk], wr=[dst])
            yield
            Rc, Pc, Qc = Rr[0], MT, Mq
            kb.op("dve", lambda e: e.tensor_tensor(out=Rc[0:C, :, 0:C], in0=MT[0:C, :, 0:C], in1=bc(ident[0:C, None, 0:C], [C, 16, C]),
                                                   op=ALU.add), rd=[MT, ident], wr=[Rc])
            levels = {64: 5, 4: 1}[C]
            for lvl in range(levels):
                last = lvl == levels - 1
                Qn, Pn, Rn = Qq[lvl % 2], Pp[lvl % 2], Rr[(lvl + 1) % 2]
                mm16(0, C, C, lambda h: Pc[0:C, h, 0:C], lambda h: Qc[0:C, h, 0:C], [Pc, Qc])
                if not last:
                    mm16(2, C, C, lambda h: Qc[0:C, h, 0:C], lambda h: Pc[0:C, h, 0:C], [Pc, Qc])
                kb.op("act", lambda e: e.copy(out=Qn[0:C, :, 0:C], in_=wide(0, C, 16, C)), rd=wb(0, 16 * C), wr=[Qn])
                if not last:
                    kb.op("dve", lambda e: e.tensor_copy(out=Pn[0:C, :, 0:C], in_=wide(2, C, 16, C)), rd=wb(2, 16 * C), wr=[Pn])
                mm16(4, C, C, lambda h: Qn[0:C, h, 0:C], lambda h: Rc[0:C, h, 0:C], [Qn, Rc])
                kb.op("dve", lambda e: e.tensor_tensor(out=Rn[0:C, :, 0:C], in0=Rc[0:C, :, 0:C], in1=wide(4, C, 16, C), op=ALU.add),
                      rd=[Rc] + wb(4, 16 * C), wr=[Rn])
                Rc, Pc, Qc = Rn, Pn, Qn
            yield
            for h in range(16):
                o = PS.t[0:C, 3072 + h * 64:3072 + (h + 1) * 64]
                kb.mm(o, at[:, h, 0:C], Zb[:, h, :], start=True, stop=False, rd=[at, Zb], wr=[bank[6], bank[7]])
                kb.mm(o, LakT[0:C, h, 0:C], Vtm[0:C, h, :], start=False, stop=True, rd=[LakT, Vtm], wr=[bank[6], bank[7]])
            kb.op("act", lambda e: e.copy(out=Wsb[0:C], in_=wide(6, C, 16, 64)), rd=[bank[6], bank[7]], wr=[Wsb])
            mm16(0, C, 64, lambda h: Rc[0:C, h, 0:C], lambda h: Wsb[0:C, h, :], [Rc, Wsb])
            kb.op("dve", lambda e: e.tensor_copy(out=Usb[0:C], in_=wide(0, C, 16, 64)), rd=[bank[0], bank[1]], wr=[Usb])
            yield
            for h in range(16):
                o = PS.t[0:64, 1024 + h * C:1024 + (h + 1) * C]
                kb.mm(o, Zb[:, h, :], rt[:, h, 0:C], start=True, stop=False, rd=[Zb, rt], wr=wb(2, 16 * C))
                kb.mm(o, Usb[0:C, h, :], RBT[0:C, h, 0:C], start=False, stop=False, rd=[Usb, RBT], wr=wb(2, 16 * C))
                kb.mm(o, Vtm[0:C, h, :], RKT[0:C, h, 0:C], start=False, stop=True, rd=[Vtm, RKT], wr=wb(2, 16 * C))
            for h in range(16):
                o = PS.t[0:64, 2048 + h * 64:2048 + (h + 1) * 64]
                kb.mm(o, Bhtm[0:C, h, :], Usb[0:C, h, :], start=True, stop=False, rd=[Bhtm, Usb], wr=[bank[4], bank[5]])
                kb.mm(o, Khtm[0:C, h, :], Vtm[0:C, h, :], start=False, stop=True, rd=[Khtm, Vtm], wr=[bank[4], bank[5]])
            kb.op("act", lambda e: e.copy(out=V8(ysb), in_=wide(2, 64, 16, C)), rd=wb(2, 16 * C), wr=[ysb])
            kb.op("dve", lambda e: e.tensor_tensor(out=Z[:], in0=Z[:], in1=GCb(64), op=ALU.mult), rd=[Z, Gm], wr=[Z])
            kb.op("dve", lambda e: e.tensor_tensor(out=Z[:], in0=Z[:], in1=wide(4, 64, 16, 64), op=ALU.add), rd=[Z, bank[4], bank[5]], wr=[Z])
            yield
            kb.op("dve", lambda e: e.tensor_copy(out=V8(ysbb), in_=V8(ysb)), rd=[ysb], wr=[ysbb])
            mmones(6, ysbb, C)
            kb.op("dve", lambda e: e.scalar_tensor_tensor(out=V8(yc), in0=wide(6, 64, 16, C), scalar=-1.0 / 64, in1=V8(ysb), op0=ALU.mult, op1=ALU.add),
                  rd=wb(6, 16 * C) + [ysb], wr=[yc])
            kb.op("act", lambda e: e.activation(out=V8(sq), in_=V8(yc), func=AF.Square), rd=[yc], wr=[sq])
            mmones(0, sq, C)
            kb.op("dve", lambda e: e.tensor_scalar(out=V8(rn), in0=wide(0, 64, 16, C), scalar1=1.0 / 64, scalar2=6.4e-4, op0=ALU.mult, op1=ALU.add),
                  rd=wb(0, 16 * C), wr=[rn])
            kb.op("act", lambda e: e.activation(out=V8(rn), in_=V8(rn), func=AF.Sqrt), rd=[rn], wr=[rn])
            kb.op("dve", lambda e: e.reciprocal(out=V8(rn), in_=V8(rn)), rd=[rn], wr=[rn])
            kb.op("dve", lambda e: e.tensor_tensor(out=V8(yn), in0=V8(yc), in1=V8(rn), op=ALU.mult), rd=[yc, rn], wr=[yn])
            kb.op("dve", lambda e: e.tensor_tensor(out=V8(yn), in0=V8(yn), in1=bc(gnw[:], [64, 16, C]), op=ALU.mult), rd=[yn, gnw], wr=[yn])
            kb.op("dve", lambda e: e.tensor_tensor(out=V8(yn), in0=V8(yn), in1=bc(gnb[:], [64, 16, C]), op=ALU.add), rd=[yn, gnb], wr=[yn])
            kb.op("pool", lambda e: e.tensor_tensor(out=V8(rk), in0=Pr, in1=V8(kfin), op=ALU.mult), rd=[PH, kfin], wr=[rk])
            kb.op("pool", lambda e: e.tensor_tensor(out=V8(rk), in0=V8(rk), in1=bc(rkp[:], [64, 16, C]), op=ALU.mult), rd=[rk, rkp], wr=[rk])
            mmones(2, rk, C)
            kb.op("dve", lambda e: e.tensor_tensor(out=V8(yc), in0=wide(2, 64, 16, C), in1=Pvv, op=ALU.mult), rd=wb(2, 16 * C) + [PH], wr=[yc])
            kb.op("dve", lambda e: e.tensor_tensor(out=V8(yn), in0=V8(yn), in1=V8(yc), op=ALU.add), rd=[yn, yc], wr=[yn])
            yo = yout[ui % 2]
            kb.op("dve", lambda e: e.tensor_tensor(out=V8(yo), in0=V8(yn), in1=V8(gsb), op=ALU.mult), rd=[yn, gsb], wr=[yo])
            kb.dma("pool", ymix, ymix.t[0:1024, :].rearrange("(h v) t -> v h t", v=64)[:, :, t0:t0 + C], yo, V8(yo))
            yield
            if s is not None or ch == SEQ // 64 - 1 or DEBUG_UNITS is not None:
                for h in range(16):
                    kb.tr(out=PS.t[0:64, 2048 + h * 64:2048 + (h + 1) * 64], in_=Z[:, h, :], identity=ident[0:64, 0:64],
                          rd=[Z, ident], wr=[bank[4], bank[5]])
                kb.op("act", lambda e: e.copy(out=Sv[:], in_=wide(4, 64, 16, 64)), rd=[bank[4], bank[5]], wr=[Sv])
                row = 0 if s is None else 1 + s
                kb.dma("pool", io["o_rw"], io["o_rw"].t[row].rearrange("h v k -> v h k"), Sv, Sv[:], is_output=True)
DFF = 5632
GDN_PROJ = 8224
H2 = TT // 2


def proj_cm(kb, c, W, KC, col0, ncols, rhsT, wt, epi, state, slabw=256, tail=None):
    psb = c["psb"]
    Wv = W.t if isinstance(W, Buf) else W
    for s0 in range(0, ncols, slabw):
        nsl = min(slabw, ncols - s0)
        wb = wt[state["n"] % len(wt)]
        state["n"] += 1
        kb.dma("pool", wb, wb[:, 0:KC, 0:nsl], state["src"],
               Wv[:, col0 + s0:col0 + s0 + nsl].rearrange("(c p) n -> p c n", p=128))
        for j in range((nsl + 127) // 128):
            m = min(128, nsl - j * 128)
            blk = (s0 + j * 128) // 128
            for hh in range(2):
                pb = psb[2 + state["pb"] % 4]
                state["pb"] += 1
                for kc in range(KC):
                    kb.mm(pb[0:m, 0:H2], wb[:, kc, j * 128:j * 128 + m], rhsT[:, kc, hh * H2:(hh + 1) * H2],
                          start=(kc == 0), stop=(kc == KC - 1), rd=[wb, rhsT], wr=[pb])
                epi(blk, m, hh, pb)
        if tail is not None:
            tail(s0, nsl, wb)


def stage_C(kb, c, io, l):
    nc = kb.nc
    PS, bank, ident, psb = c["PS"], c["bank"], c["ident"], c["psb"]
    hT_d = io["hT"]
    w_out = io["w_out_ab"] if l == 0 else io["w_out_c"]
    SC = 1.0 / (128.0 ** 0.5)
    with kb.scope():
        g_xa = load_gain(kb, "C_gxa", io["norm_xa"], l)
        g_ffn = load_gain(kb, "C_gffn", io["norm_ffn"], l)
        if l == 0:
            g_nxt = load_gain(kb, "C_gnxt", io["norm_mix"], 1)
        else:
            g_nxt = kb.sb("C_gfin", [128, 16], F32)
            with nc.allow_non_contiguous_dma(reason="tiny gain vectors"):
                kb.dma("sp", g_nxt, g_nxt[:], io["norm_final"], io["norm_final"].t.rearrange("(c p) -> p c", p=128))
        cw = kb.sb("C_cw", [128, 44, 3])
        cbi = kb.sb("C_cb", [128, 44])
        with nc.allow_non_contiguous_dma(reason="tiny parameter vectors"):
            for i in range(3):
                kb.dma("sp", cw, cw[:, :, i], io["ffn_conv_w"], io["ffn_conv_w"].t[l, i].rearrange("(c p) -> p c", p=128))
            kb.dma("sp", cbi, cbi[:], io["ffn_conv_b"], io["ffn_conv_b"].t[l].rearrange("(c p) -> p c", p=128))
        shalo = kb.sb("C_shalo", [128, 44, 2 * NS], F32)
        with kb.scope():
            stt = kb.sb("C_stt", [2 * NS, DFF], F32)
            kb.dma("sp", stt, stt[:], io["state_ffn_conv"], io["state_ffn_conv"].t[l].rearrange("s r c -> (s r) c"))
            for j in range(44):
                pb = psb[j % 4]
                kb.tr(out=pb[:, 0:2 * NS], in_=stt[:, j * 128:(j + 1) * 128], identity=ident[0:2 * NS, 0:2 * NS], rd=[stt, ident], wr=[pb])
                kb.op("act" if j % 2 else "dve",
                      (lambda e: e.copy(out=shalo[:, j, :], in_=pb[:, 0:2 * NS])) if j % 2 else
                      (lambda e: e.tensor_copy(out=shalo[:, j, :], in_=pb[:, 0:2 * NS])), rd=[pb], wr=[shalo])

        hT = kb.sb("C_hT", [128, 16, TT], F32)
        aT = kb.sb("C_aT", [128, 16, TT], BF16)
        hid = kb.sb("C_hid", [128, 11, TT], BF16)
        wt = [kb.sb(f"C_wt{i}", [128, 16, 256], BF16) for i in range(4)]
        wo = [kb.sb(f"C_wo{i}", [128, 11, 256], BF16) for i in range(3)]
        sq = [kb.sb(f"C_sq{i}", [128, TT], BF16) for i in range(2)]
        rstd = kb.sb("C_rstd", [128, TT], F32)
        qT = kb.sb("C_qT", [128, 4, TT], BF16)
        qS = kb.sb("C_qS", [128, 4, TS], F32)
        oT = kb.sb("C_oT", [128, 4, TT], BF16)
        G4 = kb.sb("C_G4", [128, 2, TT], F32)
        gext = kb.sb("C_gext", [128, 2 + TT], F32)
        gacc = kb.sb("C_gacc", [128, TT], F32)
        halo = kb.sb("C_halo", [128, 44, 2], F32)
        sext = kb.sb("C_sext", [128, NS, 6], F32)
        sacc = kb.sb("C_sacc", [128, NS, 4], F32)
        ptl = [kb.sb(f"C_ptl{i}", [128, 256], F32) for i in range(2)]
        ytok = [kb.sb("C_ytok0", [128, D], F32)] * 2 if l == 1 else None
        E4 = kb.sb("C_E4", [128, 4, NMEM], F32)
        PT4 = kb.sb("C_PT4", [128, 8, 128], BF16)
        st4 = kb.sb("C_st4", [128, 16], F32)
        Kc = kb.sb("C_Kc", [128, 2, XA], F32)
        Vc = kb.sb("C_Vc", [128, 2, XA], F32)
        KT = kb.sb("C_KT", [128, 4, NMEM], F32)
        ms = kb.sb("C_ms", [4, 8], F32)
        PTs = kb.sb("C_PTs", [128, 8, 4], F32)
        oS = kb.sb("C_oS", [128, 4, TS], F32)
        kb.op("pool", lambda e: e.memset(halo[:], 0.0), wr=[halo])
        pst = {"n": 0, "pb": 0, "src": None}

        def add_into_h(blk, m, hh, pb):
            kb.op("dve" if hh else "pool" if False else "dve",
                  lambda e: e.tensor_tensor(out=hT[0:m, blk, hh * H2:(hh + 1) * H2], in0=hT[0:m, blk, hh * H2:(hh + 1) * H2],
                                            in1=pb[0:m, 0:H2], op=ALU.add), rd=[hT, pb], wr=[hT])

        for tt in range(NTT if DEBUG_UNITS is None else 1):
            if DEBUG_UNITS is not None:
                tt = NTT - 1
            c0 = tt * TT
            last = tt == NTT - 1
            kb.dma("sp", hT, hT[:], hT_d, hT_d.t.rearrange("(c p) t -> p c t", p=128)[:, :, c0:c0 + TT])
            kb.dma("pool", aT, aT[:], io["ymix"], io["ymix"].t.rearrange("(c p) t -> p c t", p=128)[:, :, c0:c0 + TT])
            pst["src"] = w_out
            proj_cm(kb, c, w_out, 16, 0, D, aT, wt, add_into_h, pst)
            rmsnorm_cm(kb, c, hT, g_xa, aT, sq, rstd)
            pst["src"] = io["w_xq"]

            def q_epi(blk, m, hh, pb):
                kb.op("act", lambda e: e.copy(out=qT[:, blk, hh * H2:(hh + 1) * H2], in_=pb[:, 0:H2]), rd=[pb], wr=[qT])
                if last and hh == 1:
                    kb.op("dve", lambda e: e.tensor_copy(out=qS[:, blk, :], in_=pb[:, H2 - TS:H2]), rd=[pb], wr=[qS])
            proj_cm(kb, c, io["w_xq"].t[l], 16, 0, XA, aT, wt, q_epi, pst)
            npt = TT - TS if last else TT
            for b0 in range(0, npt, 128):
                nb = min(128, npt - b0)
                for h in range(4):
                    kb.mm(bank[6 + h // 2][0:nb, (h % 2) * NMEM:(h % 2 + 1) * NMEM], qT[:, h, b0:b0 + nb], c["mkT"][:, l, h, :],
                          rd=[qT, c["mkT"]], wr=[bank[6 + h // 2]])
                for hp in range(2):
                    kb.op("dve", lambda e: e.tensor_reduce(out=st4[0:nb, 2 * hp:2 * hp + 2],
                                                           in_=bank[6 + hp][0:nb, :].rearrange("p (h m) -> p h m", m=NMEM), axis=AX.X, op=ALU.max),
                          rd=[bank[6 + hp]], wr=[st4])
                kb.op("dve", lambda e: e.tensor_scalar(out=st4[0:nb, 4:8], in0=st4[0:nb, 0:4], scalar1=-SC, scalar2=None, op0=ALU.mult),
                      rd=[st4], wr=[st4])
                for h in range(4):
                    kb.op("act", lambda e: e.activation(out=E4[0:nb, h, :], in_=bank[6 + h // 2][0:nb, (h % 2) * NMEM:(h % 2 + 1) * NMEM],
                                                        func=AF.Exp, bias=st4[0:nb, 4 + h:5 + h], scale=SC, accum_out=st4[0:nb, 8 + h:9 + h]),
                          rd=[bank[6 + h // 2], st4], wr=[E4, st4])
                kb.op("dve", lambda e: e.reciprocal(out=st4[0:nb, 12:16], in_=st4[0:nb, 8:12]), rd=[st4], wr=[st4])
                kb.op("dve", lambda e: e.tensor_tensor(out=E4[0:nb], in0=E4[0:nb], in1=bc(st4[0:nb, 12:16, None], [nb, 4, NMEM]), op=ALU.mult),
                      rd=[E4, st4], wr=[E4])
                for h in range(4):
                    for mb in range(2):
                        kb.tr(out=bank[2 + h // 2][:, ((h % 2) * 2 + mb) * 128:((h % 2) * 2 + mb) * 128 + nb],
                              in_=E4[0:nb, h, mb * 128:(mb + 1) * 128], identity=ident[0:nb, 0:nb], rd=[E4, ident], wr=[bank[2 + h // 2]])
                for hp in range(2):
                    kb.op("act" if hp else "dve",
                          (lambda e: e.copy(out=PT4[:, 4 * hp:4 * hp + 4, 0:nb], in_=bank[2 + hp][:, :].rearrange("p (a b) -> p a b", b=128)[:, :, 0:nb]))
                          if hp else
                          (lambda e: e.tensor_copy(out=PT4[:, 4 * hp:4 * hp + 4, 0:nb], in_=bank[2 + hp][:, :].rearrange("p (a b) -> p a b", b=128)[:, :, 0:nb])),
                          rd=[bank[2 + hp]], wr=[PT4])
                for h in range(4):
                    for mb in range(2):
                        kb.mm(bank[4][:, h * 128:h * 128 + nb], c["mvB"][:, l, mb, h * 128:(h + 1) * 128], PT4[:, h * 2 + mb, 0:nb],
                              start=(mb == 0), stop=(mb == 1), rd=[c["mvB"], PT4], wr=[bank[4]])
                kb.op("act", lambda e: e.copy(out=oT[:, :, b0:b0 + nb], in_=bank[4][:, :].rearrange("p (h t) -> p h t", t=128)[:, :, 0:nb]),
                      rd=[bank[4]], wr=[oT])
            if last:
                Es = E4.t[0:LS]
                for s in range(NS):
                    kb.dma("sp", Kc, Kc[:], io["cache_mem_k"], io["cache_mem_k"].t[l, s].rearrange("(mb m) h d -> m mb (h d)", m=128))
                    kb.dma("sp", Vc, Vc[:], io["cache_mem_v"], io["cache_mem_v"].t[l, s].rearrange("(mb m) h d -> m mb (h d)", m=128))
                    for h in range(4):
                        pb = bank[h % 2]
                        for mb in range(2):
                            kb.tr(out=pb[:, mb * 128:(mb + 1) * 128], in_=Kc[:, mb, h * 128:(h + 1) * 128], identity=ident[:],
                                  rd=[Kc, ident], wr=[pb])
                        kb.op("act" if h % 2 else "dve",
                              (lambda e: e.copy(out=KT[:, h, :], in_=pb[:, 0:NMEM])) if h % 2 else
                              (lambda e: e.tensor_copy(out=KT[:, h, :], in_=pb[:, 0:NMEM])), rd=[pb], wr=[KT])
                    for h in range(4):
                        kb.mm(PS.t[0:LS, 3072 + h * NMEM:3072 + (h + 1) * NMEM], qS[:, h, s * LS:(s + 1) * LS], KT[:, h, :],
                              rd=[qS, KT], wr=[bank[6], bank[7]])
                    Sv4 = PS.t[0:LS, 3072:4096].rearrange("p (h m) -> p h m", m=NMEM)
                    kb.op("dve", lambda e: e.tensor_reduce(out=ms[:, 0:4], in_=Sv4, axis=AX.X, op=ALU.max), rd=[bank[6], bank[7]], wr=[ms])
                    kb.op("dve", lambda e: e.tensor_tensor(out=Es[:], in0=Sv4, in1=bc(ms[:, 0:4, None], [LS, 4, NMEM]), op=ALU.subtract),
                          rd=[bank[6], bank[7], ms], wr=[E4])
                    kb.op("act", lambda e: e.activation(out=Es[:], in_=Es[:], func=AF.Exp, scale=SC), rd=[E4], wr=[E4])
                    kb.op("dve", lambda e: e.tensor_reduce(out=ms[:, 4:8], in_=Es[:], axis=AX.X, op=ALU.add), rd=[E4], wr=[ms])
                    kb.op("dve", lambda e: e.reciprocal(out=ms[:, 4:8], in_=ms[:, 4:8]), rd=[ms], wr=[ms])
                    kb.op("dve", lambda e: e.tensor_tensor(out=Es[:], in0=Es[:], in1=bc(ms[:, 4:8, None], [LS, 4, NMEM]), op=ALU.mult),
                          rd=[E4, ms], wr=[E4])
                    for h in range(4):
                        for mb in range(2):
                            kb.tr(out=bank[2][:, (h * 2 + mb) * LS:(h * 2 + mb + 1) * LS], in_=Es[:, h, mb * 128:(mb + 1) * 128],
                                  identity=ident[0:LS, 0:LS], rd=[E4, ident], wr=[bank[2]])
                    kb.op("act", lambda e: e.copy(out=PTs[:], in_=bank[2][:, 0:8 * LS].rearrange("p (a b) -> p a b", b=LS)), rd=[bank[2]], wr=[PTs])
                    for h in range(4):
                        for mb in range(2):
                            kb.mm(bank[3][:, h * LS:(h + 1) * LS], Vc[:, mb, h * 128:(h + 1) * 128], PTs[:, h * 2 + mb, :],
                                  start=(mb == 0), stop=(mb == 1), rd=[Vc, PTs], wr=[bank[3]])
                    kb.op("dve", lambda e: e.tensor_copy(out=oT[:, :, npt + s * LS:npt + (s + 1) * LS],
                                                         in_=bank[3][:, 0:4 * LS].rearrange("p (h t) -> p h t", t=LS)), rd=[bank[3]], wr=[oT])
            pst["src"] = io["w_xo"]
            proj_cm(kb, c, io["w_xo"].t[l], 4, 0, D, oT, wt, add_into_h, pst)
            rmsnorm_cm(kb, c, hT, g_ffn, aT, sq, rstd)
            pst["src"] = io["ffn_w_in"]
            Win = io["ffn_w_in"].t[l]
            for half in range(4):
                for jp in range(0, 11, 2):
                    j0 = half * 11 + jp
                    nc2 = 256 if jp + 2 <= 11 else 128

                    def gate_epi(blk, m, hh, pb, j0=j0):
                        j = j0 + blk
                        kb.op("act", lambda e: e.copy(out=gext[:, 2 + hh * H2:2 + (hh + 1) * H2], in_=pb[:, 0:H2]), rd=[pb], wr=[gext])
                        if hh == 1:
                            kb.op("act", lambda e: e.copy(out=gext[:, 0:2], in_=halo[:, j, :]), rd=[halo], wr=[gext])
                            kb.op("dve", lambda e: e.tensor_scalar(out=gacc[:], in0=gext[:, 0:TT], scalar1=cw[:, j, 0:1], scalar2=None, op0=ALU.mult),
                                  rd=[gext, cw], wr=[gacc])
                            for i in (1, 2):
                                kb.op("dve", lambda e: e.scalar_tensor_tensor(out=gacc[:], in0=gext[:, i:i + TT], scalar=cw[:, j, i:i + 1],
                                                                              in1=gacc[:], op0=ALU.mult, op1=ALU.add), rd=[gext, cw, gacc], wr=[gacc])
                            if last:
                                kb.op("act", lambda e: e.copy(out=sext[:, :, 0:2], in_=shalo[:, j, :].rearrange("p (s r) -> p s r", r=2)),
                                      rd=[shalo], wr=[sext])
                                kb.op("act", lambda e: e.copy(out=sext[:, :, 2:6],
                                                                in_=gext[:, 2 + npt:2 + TT].rearrange("p (s t) -> p s t", t=LS)),
                                      rd=[gext], wr=[sext])
                                kb.op("dve", lambda e: e.tensor_scalar(out=sacc[:], in0=sext[:, :, 0:4], scalar1=cw[:, j, 0:1], scalar2=None, op0=ALU.mult),
                                      rd=[sext, cw], wr=[sacc])
                                for i in (1, 2):
                                    kb.op("dve", lambda e: e.scalar_tensor_tensor(out=sacc[:], in0=sext[:, :, i:i + 4], scalar=cw[:, j, i:i + 1],
                                                                                  in1=sacc[:], op0=ALU.mult, op1=ALU.add), rd=[sext, cw, sacc], wr=[sacc])
                                kb.op("act", lambda e: e.copy(out=gacc[:, npt:TT].rearrange("p (s t) -> p s t", t=LS), in_=sacc[:]),
                                      rd=[sacc], wr=[gacc])
                            else:
                                kb.op("act", lambda e: e.copy(out=halo[:, j, :], in_=gext[:, TT:TT + 2]), rd=[gext], wr=[halo])
                            kb.op("act", lambda e: e.activation(out=G4[:, blk, :], in_=gacc[:], func=AF.Silu, bias=cbi[:, j:j + 1]),
                                  rd=[gacc, cbi], wr=[G4])

                    def gate_tail(s0, nsl, wb, j0=j0):
                        if not last:
                            return
                        pb = bank[6 + (j0 // 2) % 2]
                        for kc in range(16):
                            kb.mm(pb[:, 0:nsl], aT[:, kc, TT - 128:TT], wb[:, kc, 0:nsl], start=(kc == 0), stop=(kc == 15), rd=[wb, aT], wr=[pb])
                        pt = ptl[(j0 // 2) % 2]
                        kb.op("act", lambda e: e.copy(out=pt[:, 0:nsl], in_=pb[:, 0:nsl]), rd=[pb], wr=[pt])
                        cl = j0 * 128
                        o = io["o_ffn"]
                        kb.dma("sp", o, o.t[l, 0, :, cl:cl + nsl], pt, pt[62:64, 0:nsl], is_output=True)
                        for r in range(2):
                            kb.dma("sp", o, o.t[l, 1:1 + NS, r, cl:cl + nsl], pt, pt[64 + 2 + r:128:4, 0:nsl], is_output=True)
                    proj_cm(kb, c, Win, 16, j0 * 128, nc2, aT, wt, gate_epi, pst, tail=gate_tail)

                    def up_epi(blk, m, hh, pb, jp=jp):
                        kb.op("dve", lambda e: e.tensor_tensor(out=hid[:, jp + blk, hh * H2:(hh + 1) * H2], in0=G4[:, blk, hh * H2:(hh + 1) * H2],
                                                               in1=pb[:, 0:H2], op=ALU.mult), rd=[G4, pb], wr=[hid])
                    proj_cm(kb, c, Win, 16, DFF + j0 * 128, nc2, aT, wt, up_epi, pst)
                pst["src"] = io["ffn_w_out"]
                proj_cm(kb, c, io["ffn_w_out"].t[l, half * 11 * 128:(half + 1) * 11 * 128, :], 11, 0, D, hid, wo, add_into_h, pst, slabw=256)
                pst["src"] = io["ffn_w_in"]
            if l == 1:
                rmsnorm_cm(kb, c, hT, g_nxt, None, sq, rstd)
                for b0 in range(0, TT, 128):
                    nb = min(128, TT - b0)
                    yt = ytok[(b0 // 128) % 2]
                    for g4 in range(4):
                        pb = psb[2 + g4]
                        for k in range(4):
                            cc = g4 * 4 + k
                            kb.tr(out=pb[0:nb, k * 128:(k + 1) * 128], in_=hT[:, cc, b0:b0 + nb], identity=ident[:], rd=[hT, ident], wr=[pb])
                        kb.op("act" if g4 % 2 else "dve",
                              (lambda e: e.copy(out=yt[0:nb, g4 * 512:(g4 + 1) * 512], in_=pb[0:nb, :])) if g4 % 2 else
                              (lambda e: e.tensor_copy(out=yt[0:nb, g4 * 512:(g4 + 1) * 512], in_=pb[0:nb, :])), rd=[pb], wr=[yt])
                    kb.dma("sp", io["o_y"], io["o_y"][c0 + b0:c0 + b0 + nb, :], yt, yt[0:nb, :], is_output=True)
            else:
                kb.dma("sp", hT_d, hT_d.t.rearrange("(c p) t -> p c t", p=128)[:, :, c0:c0 + TT], hT, hT[:])
            if l == 0:
                rmsnorm_cm(kb, c, hT, g_nxt, aT, sq, rstd)
                pst["src"] = io["w_in_c"]
                ev = [gext, gacc]

                def p1_epi(blk, m, hh, pb):
                    eb = ev[blk % 2]
                    kb.op("act" if hh else "dve",
                          (lambda e: e.copy(out=eb[0:m, hh * H2:(hh + 1) * H2], in_=pb[0:m, 0:H2])) if hh else
                          (lambda e: e.tensor_copy(out=eb[0:m, hh * H2:(hh + 1) * H2], in_=pb[0:m, 0:H2])), rd=[pb], wr=[eb])
                    if hh == 1:
                        kb.dma("sp", io["pT1"], io["pT1"][blk * 128:blk * 128 + m, c0:c0 + TT], eb, eb[0:m, 0:TT])

                def p1_tail(s0, nsl, wb):
                    if not last or s0 >= 6144:
                        return
                    pb = bank[6 + (s0 // 256) % 2]
                    for kc in range(16):
                        kb.mm(pb[:, 0:nsl], aT[:, kc, TT - 128:TT], wb[:, kc, 0:nsl], start=(kc == 0), stop=(kc == 15), rd=[wb, aT], wr=[pb])
                    pt = ptl[(s0 // 256) % 2]
                    kb.op("act", lambda e: e.copy(out=pt[:, 0:nsl], in_=pb[:, 0:nsl]), rd=[pb], wr=[pt])
                    o = io["o_gdnc"]
                    kb.dma("sp", o, o.t[0, :, s0:s0 + nsl], pt, pt[61:64, 0:nsl], is_output=True)
                    for t in range(1, 4):
                        kb.dma("sp", o, o.t[1:1 + NS, t - 1, s0:s0 + nsl], pt, pt[64 + t:128:4, 0:nsl], is_output=True)
                proj_cm(kb, c, io["w_in_c"], 16, 0, GDN_PROJ, aT, wt, p1_epi, pst, tail=p1_tail)
            if DEBUG_UNITS is not None:
                break


def stage_B_gdn(kb, c, io):
    nc = kb.nc
    PS, bank, ident = c["PS"], c["bank"], c["ident"]
    pT, ymix = io["pT1"], io["ymix"]
    with kb.scope():
        cw = kb.sb("G_cw", [128, 48, 4])
        nwv = kb.sb("G_nw", [128, 1])
        dtb = kb.sb("G_dtb", [64, 16])
        Aneg = kb.sb("G_A", [64, 16])
        with nc.allow_non_contiguous_dma(reason="tiny parameter vectors"):
            for i in range(4):
                kb.dma("sp", cw, cw[:, :, i], io["gdn_conv_w"], io["gdn_conv_w"].t[i].rearrange("(c p) -> p c", p=128))
            kb.dma("sp", nwv, nwv[:, :], io["gdn_norm_w"], io["gdn_norm_w"].t.rearrange("(p o) -> p o", o=1))
            kb.dma("sp", dtb, dtb[:], io["gdn_dt_bias"], io["gdn_dt_bias"].t.partition_broadcast(64))
            kb.dma("sp", Aneg, Aneg[:], io["gdn_A_log"], io["gdn_A_log"].t.partition_broadcast(64))
        kb.op("act", lambda e: e.activation(out=Aneg[:], in_=Aneg[:], func=AF.Exp), rd=[Aneg], wr=[Aneg])
        kb.op("dve", lambda e: e.tensor_scalar(out=Aneg[:], in0=Aneg[:], scalar1=-1.0, scalar2=None, op0=ALU.mult), rd=[Aneg], wr=[Aneg])
        inclU = kb.sb("G_inclU", [64, 64])
        strU = kb.sb("G_strU", [64, 64])
        strL = kb.sb("G_strL", [64, 64])
        for m, pat, cmul, cmp_ in ((inclU, 1, -1, ALU.is_ge), (strU, 1, -1, ALU.is_gt), (strL, -1, 1, ALU.is_gt)):
            kb.op("pool", lambda e: e.memset(m[:], 1.0), wr=[m])
            kb.op("pool", lambda e: e.affine_select(out=m[:], in_=m[:], pattern=[[pat, 64]], base=0,
                                                    channel_multiplier=cmul, compare_op=cmp_, fill=0.0), rd=[m], wr=[m])
        one128 = kb.sb("G_one", [128, 128])
        kb.op("pool", lambda e: e.memset(one128[:], 1.0), wr=[one128])

        raw = kb.sb("G_raw", [128, 48, 67])
        t1 = kb.sb("G_t1", [128, 48, 64])
        qkv = kb.sb("G_qkv", [128, 48, 64])
        zt = kb.sb("G_zt", [128, 16, 64])
        bar = kb.sb("G_bar", [16, 2, 64])
        T64 = lambda nm: kb.sb("G_" + nm, [64, 16, 64])
        B64 = lambda nm: kb.sb("G_" + nm, [64, 16, 64], BF16)
        T128 = lambda nm: kb.sb("G_" + nm, [128, 16, 64])
        B128 = lambda nm: kb.sb("G_" + nm, [128, 16, 64], BF16)
        TK = lambda nm: kb.sb("G_" + nm, [64, 16, 128])
        BK = lambda nm: kb.sb("G_" + nm, [64, 16, 128], BF16)
        Gt, dU, dL = (T64(n) for n in ("Gt", "dU", "dL"))
        Mq, MT, atT = (B64(n) for n in ("Mq", "MT", "atT"))
        Pp, Qq, Rr = [B64("P0"), B64("P1")], [B64("Q0"), B64("Q1")], [B64("R0"), B64("R1")]
        Ebc, osb = T128("Ebc"), T128("osb")
        bbc, qgT, nkc = (B128(n) for n in ("bbc", "qgT", "nkc"))
        ktm, vtm, kbt = (TK(n) for n in ("ktm", "vtm", "kbt"))
        kgt, kbtb, vtb, vn = (BK(n) for n in ("kgt", "kbtb", "vtb", "vn"))
        sqb = kb.sb("G_sqb", [128, 32, 64], BF16)
        qkb = kb.sb("G_qkb", [128, 32, 64], BF16)
        S = kb.sb("G_S", [128, 16, 128])
        Sb = kb.sb("G_Sb", [128, 16, 128], BF16)
        one128b = kb.sb("G_oneb", [128, 128], BF16)
        kb.op("pool", lambda e: e.memset(one128b[:], 1.0), wr=[one128b])
        bet = kb.sb("G_bet", [64, 16])
        gg = kb.sb("G_gg", [64, 16])
        gcs = kb.sb("G_gcs", [64, 16])
        egc = kb.sb("G_egc", [64, 16])
        rn = t1

        def wide(b0, rows, n, C):
            return PS.t[0:rows, b0 * 512:b0 * 512 + n * C].rearrange("p (j l) -> p j l", l=C)

        def wb(b0, ncols):
            return [bank[b0 + i] for i in range((ncols + 511) // 512)]

        def mm16(b0, rows, C_out, lfn, rfn, rd):
            for h in range(16):
                kb.mm(PS.t[0:rows, b0 * 512 + h * C_out:b0 * 512 + (h + 1) * C_out], lfn(h), rfn(h), rd=rd, wr=wb(b0, 16 * C_out))

        def mmbc(b0, lhsT, src, nblk, C, rd):
            per = max(1, 512 // C)
            for q in range(0, nblk, per):
                n = min(per, nblk - q)
                kb.mm(PS.t[:, b0 * 512 + q * C:b0 * 512 + (q + n) * C], lhsT, src(q, q + n), rd=rd, wr=wb(b0, nblk * C))

        units = [(64, 64 * ch, ch, None) for ch in range(SEQ // 64)] + [(LS, SEQ + LS * s, 0, s) for s in range(NS)]
        if DEBUG_UNITS is not None:
            units = [units[i] for i in DEBUG_UNITS]
        for ui, (C, t0, ch, s) in enumerate(units):
            src = pT.t[0:6144, :].rearrange("(c p) t -> p c t", p=128)
            lo = 0 if (s is None and ch > 0) else 3
            kb.dma("sp", raw, raw[:, :, lo:3 + C], pT, src[:, :, t0 - 3 + lo:t0 + C])
            if lo:
                if s is None:
                    kb.op("pool", lambda e: e.memset(raw[:, :, 0:3], 0.0), wr=[raw])
                else:
                    with nc.allow_non_contiguous_dma(reason="conv state transposing load (small)"):
                        for r in range(3):
                            kb.dma("sp", raw, raw[:, :, r], io["state_gdn_conv"], io["state_gdn_conv"].t[s, r].rearrange("(c p) -> p c", p=128))
            kb.dma("sp", zt, zt[:, :, 0:C], pT, pT.t[6144:8192, :].rearrange("(c p) t -> p c t", p=128)[:, :, t0:t0 + C])
            kb.dma("sp", bar, bar[:, :, 0:C], pT, pT.t[8192:8224, :].rearrange("(a h) t -> h a t", h=16)[:, :, t0:t0 + C])
            if s is None and ch == 0:
                kb.op("pool", lambda e: e.memset(S[:], 0.0), wr=[S])
            elif s is not None:
                kb.dma("sp", S, S[:], io["state_gdn"], io["state_gdn"].t[s].rearrange("h k v -> k h v"))
            V = lambda t, n=48: t[:, 0:n, 0:C]
            kb.op("dve", lambda e: e.tensor_tensor(out=V(qkv), in0=raw[:, :, 0:C], in1=bc(cw[:, :, 0:1], [128, 48, C]), op=ALU.mult),
                  rd=[raw, cw], wr=[qkv])
            for i in range(1, 4):
                kb.op("pool", lambda e: e.tensor_tensor(out=V(t1), in0=raw[:, :, i:i + C], in1=bc(cw[:, :, i:i + 1], [128, 48, C]), op=ALU.mult),
                      rd=[raw, cw], wr=[t1])
                kb.op("dve", lambda e: e.tensor_tensor(out=V(qkv), in0=V(qkv), in1=V(t1), op=ALU.add), rd=[qkv, t1], wr=[qkv])
            kb.op("act", lambda e: e.activation(out=V(qkv), in_=V(qkv), func=AF.Silu), rd=[qkv], wr=[qkv])
            kb.op("act", lambda e: e.activation(out=V(sqb, 32), in_=V(qkv, 32), func=AF.Square), rd=[qkv], wr=[sqb])
            mmbc(0, one128b[:], lambda a, b: sqb[:, a:b, 0:C], 32, C, [one128b, sqb])
            kb.op("dve", lambda e: e.tensor_scalar(out=V(rn, 32), in0=wide(0, 128, 32, C), scalar1=1e-6, scalar2=None, op0=ALU.add),
                  rd=wb(0, 32 * C), wr=[rn])
            kb.op("act", lambda e: e.activation(out=V(rn, 32), in_=V(rn, 32), func=AF.Sqrt), rd=[rn], wr=[rn])
            kb.op("dve", lambda e: e.reciprocal(out=V(rn, 32), in_=V(rn, 32)), rd=[rn], wr=[rn])
            kb.op("dve", lambda e: e.scalar_tensor_tensor(out=qkv[:, 0:16, 0:C], in0=qkv[:, 0:16, 0:C], scalar=128.0 ** -0.5, in1=rn[:, 0:16, 0:C],
                                                          op0=ALU.mult, op1=ALU.mult), rd=[qkv, rn], wr=[qkv])
            kb.op("dve", lambda e: e.tensor_tensor(out=qkv[:, 16:32, 0:C], in0=qkv[:, 16:32, 0:C], in1=rn[:, 16:32, 0:C], op=ALU.mult),
                  rd=[qkv, rn], wr=[qkv])
            kb.op("dve", lambda e: e.tensor_copy(out=V(qkb, 32), in_=V(qkv, 32)), rd=[qkv], wr=[qkb])
            kb.op("act", lambda e: e.copy(out=Sb[:], in_=S[:]), rd=[S], wr=[Sb])
            qT, kT, vT = (lambda h: qkb[:, h, 0:C]), (lambda h: qkb[:, 16 + h, 0:C]), (lambda h: qkv[:, 32 + h, 0:C])
            kTf = lambda h: qkv[:, 16 + h, 0:C]
            for a in range(2):
                kb.tr(out=bank[4][0:C, a * 16:(a + 1) * 16], in_=bar[:, a, 0:C], identity=ident[0:16, 0:16], rd=[bar, ident], wr=[bank[4]])
            kb.op("act", lambda e: e.activation(out=bet[0:C, :], in_=bank[4][0:C, 0:16], func=AF.Sigmoid), rd=[bank[4]], wr=[bet])
            kb.op("dve", lambda e: e.tensor_tensor(out=gg[0:C, :], in0=bank[4][0:C, 16:32], in1=dtb[0:C, :], op=ALU.add), rd=[bank[4], dtb], wr=[gg])
            kb.op("act", lambda e: e.activation(out=gg[0:C, :], in_=gg[0:C, :], func=AF.Exp), rd=[gg], wr=[gg])
            kb.op("act", lambda e: e.activation(out=gg[0:C, :], in_=gg[0:C, :], func=AF.Ln, bias=1.0), rd=[gg], wr=[gg])
            kb.op("dve", lambda e: e.tensor_tensor(out=gg[0:C, :], in0=gg[0:C, :], in1=Aneg[0:C, :], op=ALU.mult), rd=[gg, Aneg], wr=[gg])
            kb.mm(bank[4][0:C, 32:48], inclU[0:C, 0:C], gg[0:C, :], rd=[inclU, gg], wr=[bank[4]])
            kb.op("dve", lambda e: e.tensor_copy(out=gcs[0:C, :], in_=bank[4][0:C, 32:48]), rd=[bank[4]], wr=[gcs])
            kb.op("act", lambda e: e.activation(out=egc[0:C, :], in_=bank[4][0:C, 32:48], func=AF.Exp), rd=[bank[4]], wr=[egc])
            kb.op("dve", lambda e: e.tensor_copy(out=Gt[0:C, :, 0:C], in_=bc(inclU[0:C, None, 0:C], [C, 16, C])), rd=[inclU], wr=[Gt])
            kb.op("dve", lambda e: e.tensor_tensor(out=Gt[0:C, :, 0:C], in0=Gt[0:C, :, 0:C], in1=bc(gg[0:C, :, None], [C, 16, C]), op=ALU.mult),
                  rd=[Gt, gg], wr=[Gt])
            mmbc(5, one128[0:C, :], lambda a, b: Gt[0:C, a:b, 0:C], 16, C, [one128, Gt])
            kb.op("act", lambda e: e.activation(out=Ebc[:, :, 0:C], in_=wide(5, 128, 16, C), func=AF.Exp), rd=wb(5, 16 * C), wr=[Ebc])
            kb.op("dve", lambda e: e.tensor_tensor(out=dU[0:C, :, 0:C], in0=wide(5, C, 16, C), in1=bc(gcs[0:C, :, None], [C, 16, C]), op=ALU.subtract),
                  rd=wb(5, 16 * C) + [gcs], wr=[dU])
            kb.op("dve", lambda e: e.tensor_scalar(out=dL[0:C, :, 0:C], in0=dU[0:C, :, 0:C], scalar1=-1.0, scalar2=0.0, op0=ALU.mult, op1=ALU.min),
                  rd=[dU], wr=[dL])
            kb.op("dve", lambda e: e.tensor_scalar(out=dU[0:C, :, 0:C], in0=dU[0:C, :, 0:C], scalar1=0.0, scalar2=None, op0=ALU.min), rd=[dU], wr=[dU])
            kb.op("act", lambda e: e.activation(out=dU[0:C, :, 0:C], in_=dU[0:C, :, 0:C], func=AF.Exp), rd=[dU], wr=[dU])
            kb.op("act", lambda e: e.activation(out=dL[0:C, :, 0:C], in_=dL[0:C, :, 0:C], func=AF.Exp), rd=[dL], wr=[dL])
            kb.op("dve", lambda e: e.tensor_tensor(out=dU[0:C, :, 0:C], in0=dU[0:C, :, 0:C], in1=bc(inclU[0:C, None, 0:C], [C, 16, C]), op=ALU.mult),
                  rd=[dU, inclU], wr=[dU])
            kb.op("dve", lambda e: e.tensor_copy(out=Gt[0:C, :, 0:C], in_=bc(ident[0:C, None, 0:C], [C, 16, C])), rd=[ident], wr=[Gt])
            kb.op("dve", lambda e: e.tensor_tensor(out=Gt[0:C, :, 0:C], in0=Gt[0:C, :, 0:C], in1=bc(bet[0:C, :, None], [C, 16, C]), op=ALU.mult),
                  rd=[Gt, bet], wr=[Gt])
            mmbc(0, one128[0:C, :], lambda a, b: Gt[0:C, a:b, 0:C], 16, C, [one128, Gt])
            kb.op("dve", lambda e: e.tensor_tensor(out=bbc[:, :, 0:C], in0=qkv[:, 16:32, 0:C], in1=wide(0, 128, 16, C), op=ALU.mult),
                  rd=[qkv] + wb(0, 16 * C), wr=[bbc])
            kb.op("pool", lambda e: e.tensor_tensor(out=qgT[:, :, 0:C], in0=qkv[:, 0:16, 0:C], in1=Ebc[:, :, 0:C], op=ALU.mult), rd=[qkv, Ebc], wr=[qgT])
            for dst, fn, b0 in ((ktm, kTf, 0), (vtm, vT, 4)):
                for h in range(16):
                    kb.tr(out=PS.t[0:C, b0 * 512 + h * 128:b0 * 512 + (h + 1) * 128], in_=fn(h), identity=ident[:], rd=[qkv, ident], wr=wb(b0, 2048))
                kb.op("act" if b0 else "dve",
                      (lambda e: e.copy(out=dst[0:C], in_=wide(b0, C, 16, 128))) if b0 else
                      (lambda e: e.tensor_copy(out=dst[0:C], in_=wide(b0, C, 16, 128))), rd=wb(b0, 2048), wr=[dst])
            kb.op("dve", lambda e: e.tensor_tensor(out=kgt[0:C], in0=ktm[0:C], in1=bc(dU[0:C, :, C - 1:C], [C, 16, 128]), op=ALU.mult), rd=[ktm, dU], wr=[kgt])
            kb.op("dve", lambda e: e.tensor_tensor(out=kbt[0:C], in0=ktm[0:C], in1=bc(bet[0:C, :, None], [C, 16, 128]), op=ALU.mult), rd=[ktm, bet], wr=[kbt])
            kb.op("dve", lambda e: e.tensor_tensor(out=kbtb[0:C], in0=kbt[0:C], in1=bc(egc[0:C, :, None], [C, 16, 128]), op=ALU.mult), rd=[kbt, egc], wr=[kbtb])
            kb.op("dve", lambda e: e.tensor_tensor(out=vtb[0:C], in0=vtm[0:C], in1=bc(bet[0:C, :, None], [C, 16, 128]), op=ALU.mult), rd=[vtm, bet], wr=[vtb])
            mm16(0, C, C, lambda h: bbc[:, h, 0:C], kT, [bbc, qkb])
            kb.op("dve", lambda e: e.scalar_tensor_tensor(out=Mq[0:C, :, 0:C], in0=wide(0, C, 16, C), scalar=-1.0, in1=dL[0:C, :, 0:C], op0=ALU.mult, op1=ALU.mult),
                  rd=wb(0, 16 * C) + [dL], wr=[Mq])
            kb.op("dve", lambda e: e.tensor_tensor(out=Mq[0:C, :, 0:C], in0=Mq[0:C, :, 0:C], in1=bc(strL[0:C, None, 0:C], [C, 16, C]), op=ALU.mult),
                  rd=[Mq, strL], wr=[Mq])
            mm16(2, C, C, kT, lambda h: bbc[:, h, 0:C], [bbc, qkb])
            kb.op("dve", lambda e: e.scalar_tensor_tensor(out=MT[0:C, :, 0:C], in0=wide(2, C, 16, C), scalar=-1.0, in1=dU[0:C, :, 0:C], op0=ALU.mult, op1=ALU.mult),
                  rd=wb(2, 16 * C) + [dU], wr=[MT])
            kb.op("dve", lambda e: e.tensor_tensor(out=MT[0:C, :, 0:C], in0=MT[0:C, :, 0:C], in1=bc(strU[0:C, None, 0:C], [C, 16, C]), op=ALU.mult),
                  rd=[MT, strU], wr=[MT])
            mm16(4, C, C, kT, qT, [qkb])
            kb.op("dve", lambda e: e.tensor_tensor(out=atT[0:C, :, 0:C], in0=wide(4, C, 16, C), in1=dU[0:C, :, 0:C], op=ALU.mult),
                  rd=wb(4, 16 * C) + [dU], wr=[atT])
            Rc, Pc, Qc = Rr[0], MT, Mq
            kb.op("dve", lambda e: e.tensor_tensor(out=Rc[0:C, :, 0:C], in0=MT[0:C, :, 0:C], in1=bc(ident[0:C, None, 0:C], [C, 16, C]), op=ALU.add),
                  rd=[MT, ident], wr=[Rc])
            levels = {64: 5, 4: 1}[C]
            for lvl in range(levels):
                lastl = lvl == levels - 1
                Qn, Pn, Rn = Qq[lvl % 2], Pp[lvl % 2], Rr[(lvl + 1) % 2]
                mm16(0, C, C, lambda h: Pc[0:C, h, 0:C], lambda h: Qc[0:C, h, 0:C], [Pc, Qc])
                if not lastl:
                    mm16(2, C, C, lambda h: Qc[0:C, h, 0:C], lambda h: Pc[0:C, h, 0:C], [Pc, Qc])
                kb.op("act", lambda e: e.copy(out=Qn[0:C, :, 0:C], in_=wide(0, C, 16, C)), rd=wb(0, 16 * C), wr=[Qn])
                if not lastl:
                    kb.op("dve", lambda e: e.tensor_copy(out=Pn[0:C, :, 0:C], in_=wide(2, C, 16, C)), rd=wb(2, 16 * C), wr=[Pn])
                mm16(4, C, C, lambda h: Qn[0:C, h, 0:C], lambda h: Rc[0:C, h, 0:C], [Qn, Rc])
                kb.op("dve", lambda e: e.tensor_tensor(out=Rn[0:C, :, 0:C], in0=Rc[0:C, :, 0:C], in1=wide(4, C, 16, C), op=ALU.add),
                      rd=[Rc] + wb(4, 16 * C), wr=[Rn])
                Rc, Pc, Qc = Rn, Pn, Qn
            mm16(6, 128, C, lambda h: kbtb[0:C, h, :], lambda h: Rc[0:C, h, 0:C], [kbtb, Rc])
            kb.op("dve", lambda e: e.tensor_scalar(out=nkc[:, :, 0:C], in0=wide(6, 128, 16, C), scalar1=-1.0, scalar2=None, op0=ALU.mult),
                  rd=wb(6, 16 * C), wr=[nkc])
            for h in range(16):
                o = PS.t[0:C, h * 128:(h + 1) * 128]
                kb.mm(o, Rc[0:C, h, 0:C], vtb[0:C, h, :], start=True, stop=False, rd=[Rc, vtb], wr=wb(0, 2048))
                kb.mm(o, nkc[:, h, 0:C], Sb[:, h, :], start=False, stop=True, rd=[nkc, Sb], wr=wb(0, 2048))
            kb.op("act", lambda e: e.copy(out=vn[0:C], in_=wide(0, C, 16, 128)), rd=wb(0, 2048), wr=[vn])
            for h in range(16):
                o = PS.t[:, 2048 + h * C:2048 + (h + 1) * C]
                kb.mm(o, Sb[:, h, :], qgT[:, h, 0:C], start=True, stop=False, rd=[Sb, qgT], wr=wb(4, 16 * C))
                kb.mm(o, vn[0:C, h, :], atT[0:C, h, 0:C], start=False, stop=True, rd=[vn, atT], wr=wb(4, 16 * C))
            kb.op("act", lambda e: e.copy(out=osb[:, :, 0:C], in_=wide(4, 128, 16, C)), rd=wb(4, 16 * C), wr=[osb])
            mm16(0, 128, 128, lambda h: kgt[0:C, h, :], lambda h: vn[0:C, h, :], [kgt, vn])
            kb.op("dve", lambda e: e.tensor_tensor(out=S[:], in0=S[:], in1=bc(Ebc[:, :, C - 1:C], [128, 16, 128]), op=ALU.mult), rd=[S, Ebc], wr=[S])
            kb.op("dve", lambda e: e.tensor_tensor(out=S[:], in0=S[:], in1=wide(0, 128, 16, 128), op=ALU.add), rd=[S] + wb(0, 2048), wr=[S])
            kb.op("act", lambda e: e.activation(out=qgT[:, :, 0:C], in_=osb[:, :, 0:C], func=AF.Square), rd=[osb], wr=[qgT])
            mmbc(6, one128b[:], lambda a, b: qgT[:, a:b, 0:C], 16, C, [one128b, qgT])
            kb.op("dve", lambda e: e.tensor_scalar(out=Ebc[:, :, 0:C], in0=wide(6, 128, 16, C), scalar1=1.0 / 128, scalar2=EPS, op0=ALU.mult, op1=ALU.add),
                  rd=wb(6, 16 * C), wr=[Ebc])
            kb.op("act", lambda e: e.activation(out=Ebc[:, :, 0:C], in_=Ebc[:, :, 0:C], func=AF.Sqrt), rd=[Ebc], wr=[Ebc])
            kb.op("dve", lambda e: e.reciprocal(out=Ebc[:, :, 0:C], in_=Ebc[:, :, 0:C]), rd=[Ebc], wr=[Ebc])
            kb.op("dve", lambda e: e.scalar_tensor_tensor(out=osb[:, :, 0:C], in0=osb[:, :, 0:C], scalar=nwv[:, 0:1], in1=Ebc[:, :, 0:C], op0=ALU.mult, op1=ALU.mult),
                  rd=[osb, nwv, Ebc], wr=[osb])
            kb.op("act", lambda e: e.activation(out=zt[:, :, 0:C], in_=zt[:, :, 0:C], func=AF.Silu), rd=[zt], wr=[zt])
            kb.op("dve", lambda e: e.tensor_tensor(out=osb[:, :, 0:C], in0=osb[:, :, 0:C], in1=zt[:, :, 0:C], op=ALU.mult), rd=[osb, zt], wr=[osb])
            kb.dma("pool", ymix, ymix.t.rearrange("(h p) t -> p h t", p=128)[:, :, t0:t0 + C], osb, osb[:, :, 0:C])
            if s is not None or ch == SEQ // 64 - 1 or DEBUG_UNITS is not None:
                row = 0 if s is None else 1 + s
                kb.dma("pool", io["o_gdn"], io["o_gdn"].t[row].rearrange("h k v -> k h v"), S, S[:], is_output=True)
            yield
SCRATCH_KIND = "Internal"
_NC_CACHE = {}


def make_in_maps(inp):
    f32 = np.float32
    B = 4
    in_maps = []
    ca = lambda a: np.ascontiguousarray(a, dtype=f32)
    shared = {
        "norm_mem": ca(inp["norm_mem"]), "w_xk": ca(inp["w_xk"]), "w_xv": ca(inp["w_xv"]),
        "norm_mix": ca(inp["norm_mix"]), "w_in_ab": ca(inp["w_in_ab"][0]),
        "rw_mu": ca(inp["rw_mu"][0]), "rw_w0": ca(inp["rw_w0"][0]), "rw_a0": ca(inp["rw_a0"][0]),
        "rw_k_k": ca(inp["rw_k_k"][0]), "rw_k_a": ca(inp["rw_k_a"][0]), "rw_r_k": ca(inp["rw_r_k"][0]).reshape(1024),
        "rw_gn_w": ca(inp["rw_gn_w"][0]).reshape(1024), "rw_gn_b": ca(inp["rw_gn_b"][0]).reshape(1024),
        "rw_w_up": ca(inp["rw_w_up"][0]), "rw_a_up": ca(inp["rw_a_up"][0]), "rw_g_up": ca(inp["rw_g_up"][0]),
        "norm_xa": ca(inp["norm_xa"]), "norm_ffn": ca(inp["norm_ffn"]), "w_out_ab": ca(inp["w_out_ab"][0]),
        "w_xq": ca(inp["w_xq"]), "w_xo": ca(inp["w_xo"]), "ffn_w_in": ca(inp["ffn_w_in"]), "ffn_conv_w": ca(inp["ffn_conv_w"]),
        "ffn_conv_b": ca(inp["ffn_conv_b"]), "ffn_w_out": ca(inp["ffn_w_out"]), "w_in_c": ca(inp["w_in_c"][0]),
        "gdn_conv_w": ca(inp["gdn_conv_w"][0]), "gdn_A_log": ca(inp["gdn_A_log"][0]), "gdn_dt_bias": ca(inp["gdn_dt_bias"][0]),
        "gdn_norm_w": ca(inp["gdn_norm_w"][0]), "w_out_c": ca(inp["w_out_c"][0]), "norm_final": ca(inp["norm_final"]),
        "ssd_D": ca(inp["ssd_D"][0]), "ssd_norm_w": ca(inp["ssd_norm_w"][0]),
        "ssd_conv_w": ca(inp["ssd_conv_w"][0]), "ssd_conv_b": ca(inp["ssd_conv_b"][0]),
        "ssd_dt_bias": ca(inp["ssd_dt_bias"][0]), "ssd_A_log": ca(inp["ssd_A_log"][0]),
    }
    for c in range(NCORES):
        b = c % B
        xs = inp["x_sample"][c * NS:(c + 1) * NS].reshape(TS, D)
        m = dict(shared)
        m["x"] = ca(np.concatenate([inp["x_prompt"][b], xs], axis=0))
        m["mem"] = ca(inp["mem_prompt"][b])
        ss = slice(c * NS, (c + 1) * NS)
        m["state_ssd_conv"] = ca(inp["state_ssd_conv"][0, ss])
        m["state_ffn_conv"] = ca(inp["state_ffn_conv"][:, ss])
        m["cache_mem_k"] = ca(inp["cache_mem_k"][:, ss])
        m["cache_mem_v"] = ca(inp["cache_mem_v"][:, ss])
        m["state_gdn_conv"] = ca(inp["state_gdn_conv"][0, ss])
        m["state_gdn"] = ca(inp["state_gdn"][0, ss])
        m["state_rwkv"] = ca(inp["state_rwkv"][0, ss])
        m["state_rwkv_shift"] = ca(inp["state_rwkv_shift"][0, ss])
        m["state_ssd"] = ca(inp["state_ssd"][0, ss]).reshape(NS, 16, 64, 128)
        in_maps.append(m)
    return in_maps


def kernel(**inp):
    f32 = np.float32
    B = 4
    if "nc" not in _NC_CACHE:
        _NC_CACHE["nc"] = build()
    nc = _NC_CACHE["nc"]
    in_maps = make_in_maps(inp)
    res = run_bass_kernel_spmd(nc, in_maps, core_ids=list(range(NCORES)))
    R = res.results
    _NC_CACHE["R"] = R
    E, O, DEPTH, DB = 1, 1, 2, 128
    y_p = np.zeros((B, SEQ, D), f32)
    y_s = np.zeros((DB, LS, D), f32)
    rw_p = np.zeros((E, B, 16, 64, 64), f32)
    rw_s = np.zeros((E, DB, 16, 64, 64), f32)
    sh_p = np.zeros((E, B, RW_PROJ), f32)
    sh_s = np.zeros((E, DB, RW_PROJ), f32)
    ssd_p = np.zeros((E, B, 2, 8, 64, 128), f32)
    ssd_s = np.zeros((E, DB, 2, 8, 64, 128), f32)
    ssdc_p = np.zeros((E, B, 3, 1536), f32)
    ssdc_s = np.zeros((E, DB, 3, 1536), f32)
    gdn_p = np.zeros((O, B, 16, 128, 128), f32)
    gdn_s = np.zeros((O, DB, 16, 128, 128), f32)
    gdnc_p = np.zeros((O, B, 3, 6144), f32)
    gdnc_s = np.zeros((O, DB, 3, 6144), f32)
    ffn_p = np.zeros((DEPTH, B, 2, 5632), f32)
    ffn_s = np.zeros((DEPTH, DB, 2, 5632), f32)
    mk_p = np.zeros((DEPTH, B, NMEM, 4, 128), f32)
    mv_p = np.zeros((DEPTH, B, NMEM, 4, 128), f32)
    for c in range(NCORES):
        r = R[c]
        sl = slice(c * NS, (c + 1) * NS)
        sh_s[0, sl] = r["o_sh"][1:]
        ssdc_s[0, sl] = r["o_ssdc"][1:]
        ssd_s[0, sl] = r["o_ssd"][1:].reshape(NS, 2, 8, 64, 128)
        rw_s[0, sl] = r["o_rw"][1:]
        y_s[sl] = r["o_y"][SEQ:].reshape(NS, LS, D)
        gdn_s[0, sl] = r["o_gdn"][1:]
        ffn_s[:, sl] = r["o_ffn"][:, 1:]
        gdnc_s[0, sl] = r["o_gdnc"][1:]
        if c < B:
            mk_p[:, c] = r["o_memk"].reshape(DEPTH, NMEM, 4, 128)
            mv_p[:, c] = r["o_memv"].reshape(DEPTH, NMEM, 4, 128)
            sh_p[0, c] = r["o_sh"][0]
            ssdc_p[0, c] = r["o_ssdc"][0]
            ssd_p[0, c] = r["o_ssd"][0].reshape(2, 8, 64, 128)
            rw_p[0, c] = r["o_rw"][0]
            y_p[c] = r["o_y"][:SEQ]
            gdn_p[0, c] = r["o_gdn"][0]
            ffn_p[:, c] = r["o_ffn"][:, 0]
            gdnc_p[0, c] = r["o_gdnc"][0]
    return (y_p, y_s, rw_p, rw_s, sh_p, sh_s, ssd_p, ssd_s, ssdc_p, ssdc_s, gdn_p, gdn_s, gdnc_p, gdnc_s,
            ffn_p, ffn_s, mk_p, mv_p)
```
